# Optimizing a Trainium2 kernel written in Bass

```python
import jax, jax.numpy as jnp
from jax import lax
import numpy as np

D_MODEL = 1024
BATCH = 16
SEQ = 4096
DEPTH = 1
DEC_BATCH = 2
DEC_SEQ = 16384
PAST_LEN = 128

GRID_W = 64
N_HEADS = 8
N_KV_HEADS = 2
HEAD_DIM = 64
GROUP = N_HEADS // N_KV_HEADS
ATTN_W = N_HEADS * HEAD_DIM
KV_W = N_KV_HEADS * HEAD_DIM
ROPE_THETA = 10000.0
AXIS_ROT = HEAD_DIM // 2
Q_BLOCK = 128
FOURIER_GROUPS = 4
FOURIER_GROUP_W = 128
FOURIER_W = FOURIER_GROUPS * FOURIER_GROUP_W
P_IN = ATTN_W + 2 * KV_W + FOURIER_W + 2 * D_MODEL
MEM_TOKENS = 256
MEM_HEADS = 4
MEM_HEAD_DIM = D_MODEL // MEM_HEADS
MEM_W = MEM_HEADS * MEM_HEAD_DIM
D_FF = 4 * D_MODEL
ALPHA = (2 * DEPTH) ** 0.25
BETA = (8 * DEPTH) ** -0.25
RMS_EPS = 1e-6
LN_EPS = 1e-5

kernel_name = "gated_parallel_gqa_fourier_encoder"


def _layernorm(x, g, b):
    xf = x.astype(jnp.float32)
    mu = jnp.mean(xf, axis=-1, keepdims=True)
    var = jnp.mean(jnp.square(xf - mu), axis=-1, keepdims=True)
    y = (xf - mu) * lax.rsqrt(var + LN_EPS) * g.astype(jnp.float32) + b.astype(jnp.float32)
    return y.astype(x.dtype)


def _axial_rope_tables(seq_len):
    rows = seq_len // GRID_W
    row = jnp.repeat(jnp.arange(rows, dtype=jnp.float32), GRID_W)
    col = jnp.tile(jnp.arange(GRID_W, dtype=jnp.float32), rows)
    freqs = ROPE_THETA ** (-jnp.arange(0, AXIS_ROT, 2, dtype=jnp.float32) / AXIS_ROT)
    ang = jnp.concatenate([row[:, None] * freqs, col[:, None] * freqs], axis=-1)
    ang = jnp.concatenate([ang, ang], axis=-1)
    return jnp.cos(ang), jnp.sin(ang)


def _qk_prep(x, g, cos, sin):
    xf = x.astype(jnp.float32)
    xf = xf * lax.rsqrt(jnp.mean(jnp.square(xf), axis=-1, keepdims=True) + RMS_EPS) * g.astype(jnp.float32)
    half = HEAD_DIM // 2
    rot = jnp.concatenate([-xf[..., half:], xf[..., :half]], axis=-1)
    y = xf * cos[:, None, :] + rot * sin[:, None, :]
    return y.astype(x.dtype)


def _gqa_blocks(q, k, v):
    bsz, seq_len = q.shape[0], q.shape[1]
    n_blk = seq_len // Q_BLOCK
    qb = q.reshape(bsz, n_blk, Q_BLOCK, N_KV_HEADS, GROUP, HEAD_DIM).transpose(1, 0, 2, 3, 4, 5)
    scale = HEAD_DIM ** -0.5

    def one_block(q_blk):
        s = jnp.einsum('bqkgd,bskd->bkgqs', q_blk, k).astype(jnp.float32) * scale
        p = jax.nn.softmax(s, axis=-1).astype(v.dtype)
        return jnp.einsum('bkgqs,bskd->bqkgd', p, v)

    o = lax.map(one_block, qb)
    return o.transpose(1, 0, 2, 3, 4, 5).reshape(bsz, seq_len, ATTN_W)


def _fourier_mix(u):
    bsz, seq_len = u.shape[0], u.shape[1]
    ug = u.reshape(bsz, seq_len, FOURIER_GROUPS, FOURIER_GROUP_W).astype(jnp.float32)
    f = jnp.fft.fft2(ug, axes=(1, 3), norm='ortho').real
    return f.reshape(bsz, seq_len, FOURIER_W).astype(u.dtype)


def _token_mixer(x, w_in, q_norm, k_norm, w_attn_branch, w_fourier_branch, w_mix_out):
    bsz, seq_len, _ = x.shape
    h = x @ w_in
    o1 = ATTN_W
    o2 = o1 + KV_W
    o3 = o2 + KV_W
    o4 = o3 + FOURIER_W
    o5 = o4 + D_MODEL
    q = h[..., :o1].reshape(bsz, seq_len, N_HEADS, HEAD_DIM)
    k = h[..., o1:o2].reshape(bsz, seq_len, N_KV_HEADS, HEAD_DIM)
    v = h[..., o2:o3].reshape(bsz, seq_len, N_KV_HEADS, HEAD_DIM)
    u = h[..., o3:o4]
    gate_attn = h[..., o4:o5]
    gate_fourier = h[..., o5:]
    cos, sin = _axial_rope_tables(seq_len)
    q = _qk_prep(q, q_norm, cos, sin)
    k = _qk_prep(k, k_norm, cos, sin)
    y_attn = _gqa_blocks(q, k, v) @ w_attn_branch
    y_fourier = _fourier_mix(u) @ w_fourier_branch
    merged = jax.nn.sigmoid(gate_attn) * y_attn + jax.nn.sigmoid(gate_fourier) * y_fourier
    return merged @ w_mix_out


def _memory_xattn(x, mem, w_mem_q, w_mem_k, w_mem_v, w_mem_o):
    bsz, seq_len, _ = x.shape
    q = (x @ w_mem_q).reshape(bsz, seq_len, MEM_HEADS, MEM_HEAD_DIM)
    k = (mem @ w_mem_k).reshape(bsz, MEM_TOKENS, MEM_HEADS, MEM_HEAD_DIM)
    v = (mem @ w_mem_v).reshape(bsz, MEM_TOKENS, MEM_HEADS, MEM_HEAD_DIM)
    s = jnp.einsum('bqhd,bmhd->bhqm', q, k).astype(jnp.float32) * (MEM_HEAD_DIM ** -0.5)
    p = jax.nn.softmax(s, axis=-1).astype(v.dtype)
    o = jnp.einsum('bhqm,bmhd->bqhd', p, v).reshape(bsz, seq_len, MEM_W)
    return o @ w_mem_o


def _sqrelu_mlp(x, w_up, w_down):
    return jnp.square(jax.nn.relu(x @ w_up)) @ w_down


def _layer(x, mem, w_in, q_norm, k_norm, w_attn_branch, w_fourier_branch, w_mix_out,
           ln1_g, ln1_b, w_mem_q, w_mem_k, w_mem_v, w_mem_o, ln2_g, ln2_b,
           w_up, w_down, ln3_g, ln3_b):
    x = _layernorm(ALPHA * x + _token_mixer(x, w_in, q_norm, k_norm, w_attn_branch,
                                            w_fourier_branch, w_mix_out), ln1_g, ln1_b)
    x = _layernorm(ALPHA * x + _memory_xattn(x, mem, w_mem_q, w_mem_k, w_mem_v, w_mem_o), ln2_g, ln2_b)
    x = _layernorm(ALPHA * x + _sqrelu_mlp(x, w_up, w_down), ln3_g, ln3_b)
    return x


def setup_inputs(seed: int = 0) -> dict:
    key = jax.random.key(seed)
    ks = jax.random.split(key, 24)
    f32 = jnp.float32

    def nrm(k, shape, scale):
        return jax.random.normal(k, shape, f32) * scale

    def gain(k):
        return 1.0 + 0.02 * jax.random.normal(k, (DEPTH, D_MODEL), f32)

    def bias(k):
        return 0.02 * jax.random.normal(k, (DEPTH, D_MODEL), f32)

    return {
        "x_prompt": nrm(ks[0], (BATCH, SEQ, D_MODEL), 1.0),
        "x_sample": nrm(ks[1], (DEC_BATCH, DEC_SEQ, D_MODEL), 1.0),
        "mem_prompt": nrm(ks[2], (BATCH, MEM_TOKENS, D_MODEL), 1.0),
        "mem_sample": nrm(ks[3], (DEC_BATCH, MEM_TOKENS, D_MODEL), 1.0),
        "w_in": nrm(ks[4], (DEPTH, D_MODEL, P_IN), D_MODEL ** -0.5),
        "q_norm": 1.0 + 0.02 * jax.random.normal(ks[5], (DEPTH, HEAD_DIM), f32),
        "k_norm": 1.0 + 0.02 * jax.random.normal(ks[6], (DEPTH, HEAD_DIM), f32),
        "w_attn_branch": nrm(ks[7], (DEPTH, ATTN_W, D_MODEL), ATTN_W ** -0.5),
        "w_fourier_branch": nrm(ks[8], (DEPTH, FOURIER_W, D_MODEL), FOURIER_W ** -0.5),
        "w_mix_out": nrm(ks[9], (DEPTH, D_MODEL, D_MODEL), BETA * D_MODEL ** -0.5),
        "ln1_g": gain(ks[10]),
        "ln1_b": bias(ks[11]),
        "w_mem_q": nrm(ks[12], (DEPTH, D_MODEL, MEM_W), D_MODEL ** -0.5),
        "w_mem_k": nrm(ks[13], (DEPTH, D_MODEL, MEM_W), D_MODEL ** -0.5),
        "w_mem_v": nrm(ks[14], (DEPTH, D_MODEL, MEM_W), D_MODEL ** -0.5),
        "w_mem_o": nrm(ks[15], (DEPTH, MEM_W, D_MODEL), BETA * MEM_W ** -0.5),
        "ln2_g": gain(ks[16]),
        "ln2_b": bias(ks[17]),
        "w_up": nrm(ks[18], (DEPTH, D_MODEL, D_FF), D_MODEL ** -0.5),
        "w_down": nrm(ks[19], (DEPTH, D_FF, D_MODEL), BETA * D_FF ** -0.5),
        "ln3_g": gain(ks[20]),
        "ln3_b": bias(ks[21]),
    }


def reference(x_prompt, x_sample, mem_prompt, mem_sample, w_in, q_norm, k_norm,
              w_attn_branch, w_fourier_branch, w_mix_out, ln1_g, ln1_b,
              w_mem_q, w_mem_k, w_mem_v, w_mem_o, ln2_g, ln2_b,
              w_up, w_down, ln3_g, ln3_b):
    y_prompt = x_prompt
    y_sample = x_sample
    for l in range(DEPTH):
        lp = (w_in[l], q_norm[l], k_norm[l], w_attn_branch[l], w_fourier_branch[l], w_mix_out[l],
              ln1_g[l], ln1_b[l], w_mem_q[l], w_mem_k[l], w_mem_v[l], w_mem_o[l],
              ln2_g[l], ln2_b[l], w_up[l], w_down[l], ln3_g[l], ln3_b[l])
        y_prompt = _layer(y_prompt, mem_prompt, *lp)
        y_sample = _layer(y_sample, mem_sample, *lp)
    return (y_prompt, y_sample)
```

```python
import numpy as np
import ml_dtypes
from contextlib import ExitStack
import concourse.bass as bass
import concourse.mybir as mybir
from concourse.bass_utils import run_bass_kernel_spmd

F32 = mybir.dt.float32
BF16 = mybir.dt.bfloat16
AF = mybir.ActivationFunctionType
ALU = mybir.AluOpType
AX = mybir.AxisListType

D = 1024
S = 4096
SF = 16384
NT = 32
ALPHA = 2.0 ** 0.25
RMS_EPS = 1e-6
LN_EPS = 1e-5


class Buf:
    __slots__ = ("name", "lw", "rd", "psum")

    def __init__(self, name, psum=False):
        self.name = name
        self.lw = None
        self.rd = []
        self.psum = psum


class Prog:
    ENG = ("pe", "act", "dve", "pool", "sp")

    def __init__(self, nc, needed=None):
        self.nc = nc
        self.dry = needed is None
        self.needed = set() if needed is None else needed
        self.e = {"pe": nc.tensor, "act": nc.scalar, "dve": nc.vector,
                  "pool": nc.gpsimd, "sp": nc.sync}
        self.nops = {k: 0 for k in self.ENG}
        self.sig = {k: 0 for k in self.ENG}
        self.waited = {k: {} for k in self.ENG}
        self.pending = {k: [] for k in self.ENG}
        self.sem = {}
        self.dmacnt = {}
        self.last_tok = {k: None for k in self.ENG}
        self.last_dma = {}
        self.ninst = 0

    def _sem(self, key):
        if key not in self.sem:
            cm = self.nc.semaphore("s_" + str(len(self.sem)))
            self.sem[key] = cm.__enter__()
        return self.sem[key]

    def _deps(self, eng, reads, writes):
        deps = list(self.pending[eng])
        self.pending[eng] = []
        for b in reads:
            if b.lw is not None:
                deps.append(b.lw)
            if b.psum:
                deps.extend(t for t in b.rd if t[1] != eng)
        for b in writes:
            if b.lw is not None:
                deps.append(b.lw)
            deps.extend(b.rd)
        return deps

    def _emit_waits(self, eng, deps):
        best = {}
        for t in deps:
            if t[0] == "eng":
                src, idx, val = t[1], t[2], t[3]
                if src == eng and eng == "pe":
                    continue
                key = src
            else:
                idx, val = t[2], t[3]
                key = ("dma", t[1])
            if self.waited[eng].get(key, -1) >= idx:
                continue
            if key not in best or best[key][0] < idx:
                best[key] = (idx, val, t)
        for key, (idx, val, t) in best.items():
            self.waited[eng][key] = idx
            if self.dry:
                if t[0] == "eng":
                    self.needed.add((t[1], t[2]))
            else:
                assert val is not None, ("dep not signalled", t)
                self.e[eng].wait_ge(self._sem(key), val)
                self.ninst += 1

    def _update(self, tok, reads, writes):
        for b in reads:
            b.rd.append(tok)
        for b in writes:
            b.lw = tok
            b.rd = []

    def op(self, eng, fn, reads=(), writes=()):
        deps = self._deps(eng, reads, writes)
        self._emit_waits(eng, deps)
        idx = self.nops[eng]
        self.nops[eng] += 1
        val = None
        if not self.dry:
            ins = fn(self.e[eng])
            self.ninst += 1
            if (eng, idx) in self.needed:
                self.sig[eng] += 1
                val = self.sig[eng]
                ins.then_inc(self._sem(eng), 1)
        tok = ("eng", eng, idx, val)
        self.last_tok[eng] = tok
        self._update(tok, reads, writes)
        return tok

    def dma(self, q, out, in_, reads=(), writes=(), key=None, **kw):
        deps = self._deps(q, reads, writes)
        self._emit_waits(q, deps)
        n = self.dmacnt.get(key, 0) + 1
        self.dmacnt[key] = n
        if not self.dry:
            self.e[q].dma_start(out=out, in_=in_, **kw).then_inc(self._sem(("dma", key)), 16)
            self.ninst += 1
        tok = ("dma", key, n, 16 * n)
        self.last_dma[key] = tok
        self._update(tok, reads, writes)
        return tok

    def barrier(self):
        toks = [t for t in self.last_tok.values() if t is not None]
        toks.extend(self.last_dma.values())
        for k in self.ENG:
            self.pending[k] = list(toks)

    def finish(self):
        self.barrier()
        for k in self.ENG:
            deps = self.pending[k]
            self.pending[k] = []
            self._emit_waits(k, deps)


def _bf(a):
    return np.ascontiguousarray(a.astype(np.float32)).astype(ml_dtypes.bfloat16)


def _rope_table(pos):
    pos = np.asarray(pos)
    row = (pos // 64).astype(np.float32)
    col = (pos % 64).astype(np.float32)
    freqs = (np.float32(10000.0) ** (-np.arange(0, 32, 2, dtype=np.float32) / np.float32(32))).astype(np.float32)
    ang = np.concatenate([row[:, None] * freqs, col[:, None] * freqs], axis=-1).astype(np.float32)
    ang = np.concatenate([ang, ang], axis=-1)
    c = np.cos(ang).astype(np.float32)
    s = np.sin(ang).astype(np.float32)
    s2 = np.concatenate([-s[:, :32], s[:, 32:]], axis=-1)
    return np.ascontiguousarray(np.concatenate([c, s2], axis=-1).astype(np.float32))


def _e1_table(s_true, complex_in):
    tau = np.arange(32)[:, None, None]
    p = np.arange(128)[None, :, None]
    ka = np.arange(128)[None, None, :]
    n = 32 * p + tau
    ang = 2.0 * np.pi * ((ka * n) % 4096) / 4096.0
    nrm = 1.0 / np.sqrt(float(s_true) * 128.0)
    ec = np.cos(ang) * nrm
    es = np.sin(ang) * nrm
    parts = [ec, es] + ([-ec] if complex_in else [])
    return _bf(np.concatenate(parts, axis=-1))


def _bd_table():
    l = np.arange(4)[:, None, None, None]
    tau = np.arange(32)[None, :, None, None]
    l2 = np.arange(4)[None, None, :, None]
    kb = np.arange(32)[None, None, None, :]
    ang = 2.0 * np.pi * ((tau * kb) % 32) / 32.0
    dl = (l == l2).astype(np.float64)
    bc = (dl * np.cos(ang)).reshape(128, 128)
    bs = (dl * np.sin(ang)).reshape(128, 128)
    return _bf(np.concatenate([-bs, bc, bs], axis=-1))


def _cd_table():
    j = np.arange(128)[:, None]
    j2 = np.arange(128)[None, :]
    ang = 2.0 * np.pi * ((j * j2) % 128) / 128.0
    return _bf(np.concatenate([np.cos(ang), -np.sin(ang)], axis=-1))


def _coef_table(cq):
    p = np.arange(128)[:, None, None]
    tau = np.arange(32)[None, :, None]
    n1 = np.arange(4)[None, None, :]
    n = 4096 * n1 + 32 * p + tau
    ang = 2.0 * np.pi * ((cq * n) % 16384) / 16384.0
    return np.ascontiguousarray(np.stack([np.cos(ang), -np.sin(ang)], axis=-1).astype(np.float32))


WNAMES = ["w_in", "w_ab", "w_fb", "w_mix", "w_mq", "w_mk", "w_mv", "w_mo", "w_up", "w_dn"]
WSHAPES = {"w_in": [1024, 3328], "w_ab": [512, 1024], "w_fb": [512, 1024], "w_mix": [1024, 1024],
           "w_mq": [1024, 1024], "w_mk": [1024, 1024], "w_mv": [1024, 1024], "w_mo": [1024, 1024],
           "w_up": [1024, 4096], "w_dn": [4096, 1024]}
VNAMES = ["ln1_g", "ln1_b", "ln2_g", "ln2_b", "ln3_g", "ln3_b"]

U_GA, U_GF, U_BR, U_MIX, U_MQ, U_MO, U_UP, U_DN, U_MK, U_MV = 0, 2, 4, 6, 8, 10, 12, 20, 28, 30
NUNITS = 32


def make_nc(cfg):
    nc = bass.Bass("TRN2", target_bir_lowering=False)
    c = {}

    def inp(name, shape, dt=F32):
        c[name] = nc.dram_tensor(name, shape, dt, kind="ExternalInput").ap()

    inp("xseq", [3, S, D])
    inp("xfull", [SF, D])
    inp("mem", [3, 256, D])
    for w in WNAMES:
        inp(w, WSHAPES[w])
    inp("qn", [1, 64])
    inp("kn", [1, 64])
    for v in VNAMES:
        inp(v, [1, D])
    inp("rope_p", [S, 128])
    inp("rope_s", [S, 128])
    inp("rope_f", [SF, 128])
    inp("e1p", [32, 128, 256], BF16)
    inp("e1s", [32, 128, 384], BF16)
    inp("bd", [128, 384], BF16)
    inp("cd", [128, 256], BF16)
    inp("coef", [128, 32 * 4 * 2])
    c["y"] = nc.dram_tensor("y", [3, S, D], F32, kind="ExternalOutput").ap()
    c["ws"] = nc.dram_tensor("ws", [NUNITS, 128, 4096], BF16).ap()
    c["dsc"] = nc.dram_tensor("dsc", [128, 32, 2, 512], BF16).ap()
    for name, shape, dt in cfg.get("dbg", []):
        c[name] = nc.dram_tensor(name, shape, dt, kind="ExternalOutput").ap()
    return nc, c


def build(P, c, cfg):
    nc = P.nc
    glob = ExitStack()

    uniq = {"n": 0}

    def sbuf(es, name, shape, dt):
        uniq["n"] += 1
        return es.enter_context(nc.sbuf_tensor("sb%d_%s" % (uniq["n"], name), shape, dt))

    def psum(es, name, shape, dt):
        uniq["n"] += 1
        return es.enter_context(nc.psum_tensor("ps%d_%s" % (uniq["n"], name), shape, dt))

    dbg = {n for n, _, _ in cfg.get("dbg", [])}

    ident = sbuf(glob, "ident", [128, 128], BF16)
    EB = sbuf(glob, "EB", [64, 128], BF16)
    onesf = sbuf(glob, "onesf", [128, 64], F32)
    onesb = sbuf(glob, "onesb", [128, 128], BF16)
    Bc = Buf("consts")
    tmp_es = ExitStack()
    identf = sbuf(tmp_es, "identf", [128, 128], F32)
    ebf = sbuf(tmp_es, "ebf", [64, 128], F32)

    P.op("pool", lambda e: e.memset(identf[:], 0.0), writes=[Bc])
    P.op("pool", lambda e: e.affine_select(out=identf[:], in_=identf[:], pattern=[[-1, 128]],
                                            compare_op=ALU.not_equal, fill=1.0, base=0,
                                            channel_multiplier=1), reads=[Bc], writes=[Bc])
    P.op("pool", lambda e: e.memset(ebf[:], 0.0), writes=[Bc])
    P.op("pool", lambda e: e.affine_select(out=ebf[:], in_=ebf[:], pattern=[[-1, 128]],
                                            compare_op=ALU.not_equal, fill=1.0, base=64,
                                            channel_multiplier=1), reads=[Bc], writes=[Bc])
    P.op("dve", lambda e: e.tensor_copy(out=ident[:], in_=identf[:]), reads=[Bc], writes=[Bc])
    P.op("dve", lambda e: e.tensor_copy(out=EB[:], in_=ebf[:]), reads=[Bc], writes=[Bc])
    P.op("dve", lambda e: e.memset(onesf[:], 1.0), writes=[Bc])
    P.op("dve", lambda e: e.memset(onesb[:], 1.0), writes=[Bc])
    P.barrier()
    tmp_es.close()
    P.barrier()

    Bws = [Buf("ws%d" % u) for u in range(NUNITS)]
    ws = c["ws"]

    def ws_cast(u, src):
        a, b = src.shape[1], src.shape[2]
        P.dma("pool", ws[u].rearrange("p (a b) -> p a b", a=a), src, writes=[Bws[u]], key="wsc%d" % u)

    def kview(w, c0, c1):
        return w.rearrange("(k p) n -> p k n", p=128)[:, :, c0:c1]

    ws_jobs = []
    if cfg.get("tail", True):
        w_in = c["w_in"]

        def J(fn, *a):
            ws_jobs.append(lambda: fn(*a))

        def br_job(hb):
            dst = ws[U_BR + hb].rearrange("p (a b) -> p a b", a=8)
            wab = c["w_ab"].rearrange("(g i d) n -> g d i n", g=2, i=4)
            for g in range(2):
                P.dma("pool", dst[64 * g:64 * g + 64, 0:4, :], wab[g][:, :, 512 * hb:512 * hb + 512],
                      writes=[Bws[U_BR + hb]], key="wsc%da%d" % (U_BR + hb, g))
            P.dma("pool", dst[:, 4:8, :], kview(c["w_fb"], 512 * hb, 512 * hb + 512),
                  writes=[Bws[U_BR + hb]], key="wsc%db" % (U_BR + hb))

        for hb in range(2):
            J(ws_cast, U_GA + hb, kview(w_in, 1280 + 512 * hb, 1280 + 512 * hb + 512))
            J(ws_cast, U_GF + hb, kview(w_in, 2304 + 512 * hb, 2304 + 512 * hb + 512))
            J(br_job, hb)
            J(ws_cast, U_MIX + hb, kview(c["w_mix"], 512 * hb, 512 * hb + 512))
            J(ws_cast, U_MQ + hb, kview(c["w_mq"], 512 * hb, 512 * hb + 512))
            J(ws_cast, U_MO + hb, kview(c["w_mo"], 512 * hb, 512 * hb + 512))
            J(ws_cast, U_MK + hb, kview(c["w_mk"], 512 * hb, 512 * hb + 512))
            J(ws_cast, U_MV + hb, kview(c["w_mv"], 512 * hb, 512 * hb + 512))
        for j in range(8):
            J(ws_cast, U_UP + j, kview(c["w_up"], 512 * j, 512 * j + 512))
        for hb in range(2):
            for j in range(4):
                src = c["w_dn"].rearrange("(k p) n -> p k n", p=128)[:, 8 * j:8 * j + 8, 512 * hb:512 * hb + 512]
                J(ws_cast, U_DN + 4 * hb + j, src)


    xv = c["xseq"].rearrange("s (p t) d -> s t p d", t=32)
    yv = c["y"].rearrange("s (p t) d -> s t p d", t=32)
    ropev = {0: c["rope_p"].rearrange("(p t) c -> t p c", t=32),
             1: c["rope_s"].rearrange("(p t) c -> t p c", t=32)}
    xfv = c["xfull"].rearrange("(n p t) d -> n t p d", n=4, t=32)
    ropefv = c["rope_f"].rearrange("(n p t) c -> n t p c", n=4, t=32)
    Bdsc = [Buf("dsc%d" % t) for t in range(32)]

    seqs = cfg.get("seqs", [0, 1, 2])
    for s in seqs:
      with ExitStack() as seq_es:
        sample = (s == 2)
        NKT = 128 if sample else 32
        P.barrier()
        OT = sbuf(seq_es, "OT", [128, 4, S], BF16)
        BYT = [Buf("YT%d" % i) for i in range(8)]
        BOT = [Buf("OT%d" % i) for i in range(8)]
        with ExitStack() as qes:
            QT = sbuf(qes, "QT", [128, 4, S], BF16)
            KT = sbuf(qes, "KT", [128, NKT * 128], BF16)
            VP = sbuf(qes, "VP", [128, NKT, 2, 65], BF16)
            BQT = [Buf("QT%d" % t) for t in range(32)]
            BKT = [Buf("KT%d" % t) for t in range(NKT)]
            BVP = [Buf("VP%d" % t) for t in range(NKT)]
            BVPo = Buf("VPones")
            Wf = sbuf(qes, "Wf", [128, 8, 1280], BF16)
            BWf = Buf("Wf")
            for k in range(8):
                P.dma("pool", Wf[:, k, :], c["w_in"][k * 128:(k + 1) * 128, 0:1280], writes=[BWf], key="wf")
            P.op("pool", lambda e: e.memset(VP[:, :, :, 64:65], 1.0), writes=[BVPo])

            with ExitStack() as es:
                NXB = 2 if sample else 3
                HM = 8 if sample else 20
                gqk = sbuf(es, "gqk", [128, 10, 64], F32)
                coef = sbuf(es, "coef", [128, 32, 4, 2], F32)
                for h in range(10):
                    src = c["qn"] if h < 8 else c["kn"]
                    P.dma("sp", gqk[:, h, :], src.partition_broadcast(128), writes=[Bc], key="c_gqk")
                if sample:
                    P.dma("sp", coef[:].rearrange("p a b c -> p (a b c)"), c["coef"], writes=[Bc], key="c_coef")
                xb = [sbuf(es, "xb%d" % i, [128, D], BF16) for i in range(NXB)]
                xT = [sbuf(es, "xT%d" % i, [128, 8, 128], BF16) for i in range(2)]
                sq = sbuf(es, "sq", [128, HM, 64], F32)
                ssum = sbuf(es, "ssum", [128, HM], F32)
                rstd = sbuf(es, "rstd", [128, HM], F32)
                qn_ = sbuf(es, "qn_", [128, HM, 64], F32)
                t1 = sbuf(es, "t1", [128, HM, 64], F32)
                t2 = sbuf(es, "t2", [128, HM, 64], F32)
                qr = sbuf(es, "qr", [128, max(HM * 64, 1024)], BF16)
                if sample:
                    qraw = [sbuf(es, "qraw%d" % i, [128, 1, 8, 64], F32) for i in range(2)]
                    kraw = [sbuf(es, "kraw%d" % i, [128, 4, 2, 64], F32) for i in range(2)]
                    rpq = [sbuf(es, "rpq%d" % i, [128, 1, 128], F32) for i in range(2)]
                    rpk = [sbuf(es, "rpk%d" % i, [128, 4, 128], F32) for i in range(2)]
                    uacc = [sbuf(es, "uacc%d" % i, [128, 2, 512], F32) for i in range(2)]
                else:
                    qraw = [sbuf(es, "qkraw%d" % i, [128, 2, 10, 64], F32) for i in range(2)]
                    rpq = [sbuf(es, "rpT%d" % i, [128, 2, 128], F32) for i in range(2)]
                Ub = [sbuf(es, "Ub%d" % i, [128, 2, 512], BF16) for i in range(2 if sample else 4)]
                e1 = [sbuf(es, "e1_%d" % i, [128, 384], BF16) for i in range(2)]
                tb = [sbuf(es, "tb%d" % i, [128, 2, 512], BF16) for i in range(2)]
                pT = [psum(es, "pT%d" % i, [128, 8, 128], BF16) for i in range(2)]
                pq = psum(es, "pq", [128, 512], F32)
                pkv = psum(es, "pkv", [128, 512], F32)
                pu = psum(es, "pu", [128, 512], F32)
                pqT = psum(es, "pqT", [128, 8, 128], BF16)
                ps1 = [psum(es, "ps1_%d" % i, [128, 512], F32) for i in range(2)]
                Bn = lambda n: Buf(n)
                Bxb = [Bn("xb") for i in range(NXB)]
                BxT = [Bn("xT0"), Bn("xT1")]
                Bsq, Bss, Brs, Bqn, Bt1, Bt2, Bqr = (Bn(x) for x in "sq ss rs qn t1 t2 qr".split())
                Bqraw = [Bn("qraw0"), Bn("qraw1")]
                Bkraw = [Bn("kraw0"), Bn("kraw1")]
                Brpq = [Bn("rpq0"), Bn("rpq1")]
                Brpk = [Bn("rpk0"), Bn("rpk1")]
                Bua2 = [Bn("uacc0"), Bn("uacc1")]
                BUb = [Bn("Ub%d" % i) for i in range(4)]
                Be1 = [Bn("e1_0"), Bn("e1_1")]
                Btb = [Bn("tb0"), Bn("tb1")]
                BpT = [Buf("pT0", True), Buf("pT1", True)]
                Bpq, Bpkv, Bpu, BpqT = Buf("pq", True), Buf("pkv", True), Buf("pu", True), Buf("pqT", True)
                Bps1 = [Buf("ps1_0", True), Buf("ps1_1", True)]
                e1src = c["e1s"] if sample else c["e1p"]
                e1w = 384 if sample else 256
                cnt = {"t": 0}

                def load_tile(xsrc):
                    n = cnt["t"]
                    cnt["t"] += 1
                    ib, i = n % NXB, n % 2
                    P.dma("pool", xb[ib][:], xsrc, writes=[Bxb[ib]], key="xb%d" % ib)
                    for k in range(8):
                        P.op("pe", lambda e, k=k: e.transpose(out=pT[i][:, k, :], in_=xb[ib][:, k * 128:(k + 1) * 128],
                                                              identity=ident[:]),
                             reads=[Bxb[ib], Bc], writes=[BpT[i]])
                    P.op("act", lambda e: e.activation(out=xT[i][:], in_=pT[i][:], func=AF.Copy),
                         reads=[BpT[i]], writes=[BxT[i]])
                    return i

                def project(i, ps_t, Bps, c0, c1):
                    n = c1 - c0
                    for k in range(8):
                        P.op("pe", lambda e, k=k: e.matmul(ps_t[:, 0:n], lhsT=xT[i][:, k, :], rhs=Wf[:, k, c0:c1],
                                                           start=(k == 0), stop=(k == 7)),
                             reads=[BxT[i], BWf], writes=[Bps])

                def qk_chain(raw, Braw, T, H, gain, rope, Brope):
                    N = T * H
                    v4 = lambda ap: ap[:, 0:N, :].rearrange("p (t h) d -> p t h d", t=T)
                    L = []
                    L.append(lambda: P.op("act", lambda e: e.activation(out=v4(sq), in_=raw, func=AF.Square), reads=[Braw], writes=[Bsq]))
                    L.append(lambda: P.op("dve", lambda e: e.tensor_reduce(out=ssum[:, 0:N], in_=sq[:, 0:N, :], axis=AX.X, op=ALU.add),
                                          reads=[Bsq], writes=[Bss]))
                    L.append(lambda: P.op("dve", lambda e: e.tensor_scalar(out=ssum[:, 0:N], in0=ssum[:, 0:N], scalar1=1.0 / 64.0,
                                                                           scalar2=RMS_EPS, op0=ALU.mult, op1=ALU.add),
                                          reads=[Bss], writes=[Bss]))
                    L.append(lambda: P.op("act", lambda e: e.activation(out=rstd[:, 0:N], in_=ssum[:, 0:N], func=AF.Sqrt),
                                          reads=[Bss], writes=[Brs]))
                    L.append(lambda: P.op("dve", lambda e: e.reciprocal(out=rstd[:, 0:N], in_=rstd[:, 0:N]), reads=[Brs], writes=[Brs]))
                    L.append(lambda: P.op("dve", lambda e: e.tensor_tensor(
                        out=qn_[:, 0:N, :], in0=raw.rearrange("p t h d -> p (t h) d"),
                        in1=rstd[:, 0:N].unsqueeze(2).to_broadcast([128, N, 64]), op=ALU.mult),
                        reads=[Braw, Brs], writes=[Bqn]))
                    L.append(lambda: P.op("dve", lambda e: e.tensor_tensor(out=v4(qn_), in0=v4(qn_),
                                                                           in1=gain.unsqueeze(1).to_broadcast([128, T, H, 64]), op=ALU.mult),
                                          reads=[Bqn, Bc], writes=[Bqn]))
                    L.append(lambda: P.op("dve", lambda e: e.tensor_tensor(out=v4(t1), in0=v4(qn_),
                                                                           in1=rope[:, :, 0:64].unsqueeze(2).to_broadcast([128, T, H, 64]), op=ALU.mult),
                                          reads=[Bqn, Brope], writes=[Bt1]))
                    L.append(lambda: P.op("dve", lambda e: e.tensor_tensor(out=v4(t2)[:, :, :, 0:32], in0=v4(qn_)[:, :, :, 32:64],
                                                                           in1=rope[:, :, 64:96].unsqueeze(2).to_broadcast([128, T, H, 32]), op=ALU.mult),
                                          reads=[Bqn, Brope], writes=[Bt2]))
                    L.append(lambda: P.op("dve", lambda e: e.tensor_tensor(out=v4(t2)[:, :, :, 32:64], in0=v4(qn_)[:, :, :, 0:32],
                                                                           in1=rope[:, :, 96:128].unsqueeze(2).to_broadcast([128, T, H, 32]), op=ALU.mult),
                                          reads=[Bqn, Brope], writes=[Bt2]))
                    return L, v4(t1), v4(t2)

                def stage1(ti, srcs):
                    j = ti % 2
                    P.dma("sp", e1[j][:, 0:e1w], e1src[ti], writes=[Be1[j]], key="e1_%d" % j)
                    if len(srcs) == 1:
                        (u0, B0), = srcs
                        P.op("pe", lambda e: e.matmul(ps1[0][:], lhsT=e1[j][:, 0:128], rhs=u0, start=True, stop=True),
                             reads=[Be1[j], B0], writes=[Bps1[0]])
                        P.op("pe", lambda e: e.matmul(ps1[1][:], lhsT=e1[j][:, 128:256], rhs=u0, start=True, stop=True),
                             reads=[Be1[j], B0], writes=[Bps1[1]])
                    else:
                        (u0, B0), (u1, B1) = srcs
                        P.op("pe", lambda e: e.matmul(ps1[0][:], lhsT=e1[j][:, 0:128], rhs=u0, start=True, stop=False),
                             reads=[Be1[j], B0], writes=[Bps1[0]])
                        P.op("pe", lambda e: e.matmul(ps1[0][:], lhsT=e1[j][:, 128:256], rhs=u1, start=False, stop=True),
                             reads=[Be1[j], B1], writes=[Bps1[0]])
                        P.op("pe", lambda e: e.matmul(ps1[1][:], lhsT=e1[j][:, 128:256], rhs=u0, start=True, stop=False),
                             reads=[Be1[j], B0], writes=[Bps1[1]])
                        P.op("pe", lambda e: e.matmul(ps1[1][:], lhsT=e1[j][:, 256:384], rhs=u1, start=False, stop=True),
                             reads=[Be1[j], B1], writes=[Bps1[1]])
                    P.op("act", lambda e: e.activation(out=tb[j][:, 0, :], in_=ps1[0][:], func=AF.Copy),
                         reads=[Bps1[0]], writes=[Btb[j]])
                    P.op("dve", lambda e: e.tensor_copy(out=tb[j][:, 1, :], in_=ps1[1][:]),
                         reads=[Bps1[1]], writes=[Btb[j]])
                    P.dma("sp", c["dsc"][:, ti], tb[j][:], reads=[Btb[j]], writes=[Bdsc[ti]], key="tbw%d" % j)

                def emit_q_add(t1v, t2v, tt, qro, with_k):
                    L = []
                    qv = qr[:, qro:qro + 512]
                    L.append(lambda: P.op("dve", lambda e: e.tensor_tensor(
                        out=qv.rearrange("p (i g d) -> p g i d", i=4, g=2),
                        in0=t1v[:, tt, 0:8, :].rearrange("p (g i) d -> p g i d", g=2),
                        in1=t2v[:, tt, 0:8, :].rearrange("p (g i) d -> p g i d", g=2), op=ALU.add),
                        reads=[Bt1, Bt2], writes=[Bqr]))
                    if with_k:
                        L.append(lambda: P.op("dve", lambda e: e.tensor_tensor(
                            out=qr[:, qro + 512:qro + 640].rearrange("p (h d) -> p h d", d=64),
                            in0=t1v[:, tt, 8:10, :], in1=t2v[:, tt, 8:10, :], op=ALU.add),
                            reads=[Bt1, Bt2], writes=[Bqr]))
                    return L

                def emit_q_pe(qro, tau, with_k):
                    nT = 5 if with_k else 4
                    for a in range(nT):
                        P.op("pe", lambda e, a=a: e.transpose(out=pqT[:, a, :],
                                                              in_=qr[:, qro + a * 128:qro + (a + 1) * 128],
                                                              identity=ident[:]),
                             reads=[Bqr, Bc], writes=[BpqT])
                    P.op("act", lambda e: e.activation(out=QT[:, :, tau * 128:(tau + 1) * 128], in_=pqT[:, 0:4, :],
                                                       func=AF.Copy), reads=[BpqT], writes=[BQT[tau]])
                    if with_k:
                        P.op("act", lambda e: e.activation(out=KT[:, tau * 128:(tau + 1) * 128], in_=pqT[:, 4, :],
                                                           func=AF.Copy), reads=[BpqT], writes=[BKT[tau]])

                def run_sched(iters):
                    prevB1, prevB2 = [], None
                    for (As, mkB1, B2) in iters + [([], None, None)]:
                        nA = max(len(As), 1)
                        per = (len(prevB1) + nA - 1) // nA
                        for ai in range(nA):
                            if ai < len(As):
                                As[ai]()
                            for _ in range(per):
                                if prevB1:
                                    prevB1.pop(0)()
                        while prevB1:
                            prevB1.pop(0)()
                        if prevB2 is not None:
                            prevB2()
                        prevB1 = mkB1() if mkB1 is not None else []
                        prevB2 = B2

                iters = []
                if not sample:
                    for pr in range(NT // 2):
                        par = pr % 2

                        def Atile(tt, pr=pr, par=par):
                            tau = 2 * pr + tt
                            if tt == 0:
                                P.dma("sp", rpq[par][:], ropev[0][2 * pr:2 * pr + 2].rearrange("t p c -> p t c"),
                                      writes=[Brpq[par]], key="rpq%d" % par)
                            i = load_tile(xv[s, tau])
                            project(i, pq, Bpq, 0, 512)
                            P.op("act", lambda e: e.activation(out=qraw[par][:, tt, 0:8, :],
                                                               in_=pq[:].rearrange("p (h d) -> p h d", d=64), func=AF.Copy),
                                 reads=[Bpq], writes=[Bqraw[par]])
                            project(i, pkv, Bpkv, 512, 768)
                            P.op("act", lambda e: e.activation(out=qraw[par][:, tt, 8:10, :],
                                                               in_=pkv[:, 0:128].rearrange("p (h d) -> p h d", d=64), func=AF.Copy),
                                 reads=[Bpkv], writes=[Bqraw[par]])
                            P.op("act", lambda e: e.activation(out=VP[:, tau, :, 0:64],
                                                               in_=pkv[:, 128:256].rearrange("p (g d) -> p g d", g=2),
                                                               func=AF.Copy), reads=[Bpkv], writes=[BVP[tau]])
                            project(i, pu, Bpu, 768, 1280)
                            j = tau % 4
                            P.op("act", lambda e: e.activation(out=Ub[j][:, 0, :], in_=pu[:], func=AF.Copy),
                                 reads=[Bpu], writes=[BUb[j]])

                        def mkB1(pr=pr, par=par):
                            L, t1v, t2v = qk_chain(qraw[par][:], Bqraw[par], 2, 10, gqk[:, 0:10, :], rpq[par][:], Brpq[par])
                            for tt in range(2):
                                L += emit_q_add(t1v, t2v, tt, tt * 640, True)
                            return L

                        def B2(pr=pr):
                            for tt in range(2):
                                tau = 2 * pr + tt
                                stage1(tau, [(Ub[tau % 4][:, 0, :], BUb[tau % 4])])
                            for tt in range(2):
                                emit_q_pe(tt * 640, 2 * pr + tt, True)
                        iters.append(([lambda a=Atile: a(0), lambda a=Atile: a(1)], mkB1, B2))
                else:
                    for tau in range(NT):
                        par = tau % 2

                        def Aown(tau=tau, par=par):
                            P.dma("sp", rpq[par][:, 0, :], ropev[1][tau], writes=[Brpq[par]], key="rpq%d" % par)
                            P.dma("sp", rpk[par][:], ropefv[:, tau].rearrange("n p c -> p n c"),
                                  writes=[Brpk[par]], key="rpk%d" % par)
                            i = load_tile(xv[s, tau])
                            project(i, pq, Bpq, 0, 512)
                            P.op("act", lambda e: e.activation(out=qraw[par][:, 0, :, :],
                                                               in_=pq[:].rearrange("p (h d) -> p h d", d=64), func=AF.Copy),
                                 reads=[Bpq], writes=[Bqraw[par]])

                        def Afull(n1, tau=tau, par=par):
                            ua, Bu = uacc[par], Bua2[par]
                            kt = tau * 4 + n1
                            i2 = load_tile(xfv[n1, tau])
                            project(i2, pkv, Bpkv, 512, 768)
                            P.op("act", lambda e: e.activation(out=kraw[par][:, n1, :, :],
                                                               in_=pkv[:, 0:128].rearrange("p (h d) -> p h d", d=64), func=AF.Copy),
                                 reads=[Bpkv], writes=[Bkraw[par]])
                            P.op("act", lambda e: e.activation(out=VP[:, kt, :, 0:64],
                                                               in_=pkv[:, 128:256].rearrange("p (g d) -> p g d", g=2),
                                                               func=AF.Copy), reads=[Bpkv], writes=[BVP[kt]])
                            project(i2, pu, Bpu, 768, 1280)
                            for ri in range(2):
                                sc = coef[:, tau, n1, ri:ri + 1]
                                if n1 == 0:
                                    P.op("dve", lambda e: e.tensor_scalar_mul(out=ua[:, ri, :], in0=pu[:], scalar1=sc),
                                         reads=[Bpu, Bc], writes=[Bu])
                                else:
                                    P.op("dve", lambda e: e.scalar_tensor_tensor(
                                        out=ua[:, ri, :], in0=pu[:], scalar=sc, in1=ua[:, ri, :],
                                        op0=ALU.mult, op1=ALU.add), reads=[Bpu, Bu, Bc], writes=[Bu])

                        def mkB1(tau=tau, par=par):
                            L, t1v, t2v = qk_chain(qraw[par][:], Bqraw[par], 1, 8, gqk[:, 0:8, :], rpq[par][:], Brpq[par])
                            L += emit_q_add(t1v, t2v, 0, 0, False)
                            L2, k1v, k2v = qk_chain(kraw[par][:], Bkraw[par], 4, 2, gqk[:, 8:10, :], rpk[par][:], Brpk[par])
                            L2.append(lambda: P.op("dve", lambda e: e.tensor_tensor(
                                out=qr[:, 512:1024].rearrange("p (t h d) -> p t h d", t=4, h=2), in0=k1v, in1=k2v, op=ALU.add),
                                reads=[Bt1, Bt2], writes=[Bqr]))
                            return L + L2

                        def B2(tau=tau, par=par):
                            P.op("act", lambda e: e.activation(out=Ub[par][:], in_=uacc[par][:], func=AF.Copy),
                                 reads=[Bua2[par]], writes=[BUb[par]])
                            stage1(tau, [(Ub[par][:, 0, :], BUb[par]), (Ub[par][:, 1, :], BUb[par])])
                            emit_q_pe(0, tau, False)
                            for a in range(4):
                                P.op("pe", lambda e, a=a: e.transpose(out=pqT[:, 4 + a, :], in_=qr[:, 512 + a * 128:512 + (a + 1) * 128],
                                                                      identity=ident[:]),
                                     reads=[Bqr, Bc], writes=[BpqT])
                            P.op("act", lambda e: e.activation(
                                out=KT[:, tau * 512:(tau + 1) * 512].rearrange("p (a q) -> p a q", a=4),
                                in_=pqT[:, 4:8, :], func=AF.Copy), reads=[BpqT], writes=[BKT[4 * tau + a] for a in range(4)])
                        iters.append(([Aown] + [(lambda n1=n1, f=Afull: f(n1)) for n1 in range(4)], mkB1, B2))
                run_sched(iters)

            P.barrier()
            if "d_QT" in dbg:
                P.dma("sp", c["d_QT"], QT[:], reads=BQT, key="d_QT")
                P.dma("sp", c["d_KT"], KT[:, 0:4096], reads=BKT, key="d_KT")
                P.dma("sp", c["d_VP"], VP[:, 0:32], reads=BVP + [BVPo], key="d_VP")

            if cfg.get("attn", True):
                with ExitStack() as es:
                    NPS = 2
                    pS = [psum(es, "pS%d" % i, [128, 2, 512], F32) for i in range(NPS)]
                    pacc = [psum(es, "pacc%d" % g, [128, 512], F32) for g in range(2)]
                    pbc = psum(es, "pbc", [128, 512], F32)
                    ppk = psum(es, "ppk", [128, 512], F32)
                    NPT = 4
                    PTs = [sbuf(es, "PT%d" % i, [128, 2, 512], BF16) for i in range(NPT)]
                    accs = [sbuf(es, "accs%d" % g, [128, 512], F32) for g in range(2)]
                    rs = sbuf(es, "rs", [128, 2, 512], F32)
                    bcs = sbuf(es, "bcs", [64, 512], F32)
                    stg = [sbuf(es, "stg%d" % g, [64, 512], BF16) for g in range(2)]
                    BpS = [Buf("pS", True) for i in range(NPS)]
                    Bpacc = [Buf("pacc", True) for g in range(2)]
                    Bpbc, Bppk = Buf("pbc", True), Buf("ppk", True)
                    Brs_, Bbcs = Buf("rs"), Buf("bcs")
                    BPTs = [Buf("PT") for i in range(NPT)]
                    Baccs = [Buf("accs") for g in range(2)]
                    Bstg = [Buf("stg") for g in range(2)]
                    epi = []
                    for tau in range(NT):
                        qcols = slice(tau * 128, (tau + 1) * 128)
                        if ws_jobs:
                            ws_jobs.pop(0)()

                        def score(kt):
                            i = kt % NPS
                            for g in range(2):
                                P.op("pe", lambda e, g=g: e.matmul(
                                    pS[i][:, g, :].rearrange("p (i q) -> p i q", i=4), lhsT=KT[64 * g:64 * g + 64, kt * 128:(kt + 1) * 128],
                                    rhs=QT[64 * g:64 * g + 64, :, qcols], start=True, stop=True),
                                    reads=[BKT[kt], BQT[tau]], writes=[BpS[i]])

                        def expo(kt):
                            i = kt % NPS
                            j = kt % NPT
                            P.op("act", lambda e: e.activation(out=PTs[j][:], in_=pS[i][:], func=AF.Exp, scale=0.125),
                                 reads=[BpS[i]], writes=[BPTs[j]])

                        def pv(kt):
                            j = kt % NPT
                            for g in range(2):
                                P.op("pe", lambda e, g=g: e.matmul(
                                    pacc[g][0:65, :], lhsT=VP[:, kt, g, :], rhs=PTs[j][:, g, :],
                                    start=(kt == 0), stop=(kt == NKT - 1)),
                                    reads=[BPTs[j], BVP[kt], BVPo], writes=[Bpacc[g]])

                        score(0)
                        score(1)
                        for kt in range(NKT):
                            expo(kt)
                            if kt + 2 < NKT:
                                score(kt + 2)
                            pv(kt)
                            if epi and kt % 2 == 1:
                                epi.pop(0)()
                        while epi:
                            epi.pop(0)()
                        for g in range(2):
                            P.op("dve", lambda e, g=g: e.tensor_copy(out=accs[g][0:65, :], in_=pacc[g][0:65, :]),
                                 reads=[Bpacc[g]], writes=[Baccs[g]])

                        def mk_epi(tau=tau, qcols=qcols):
                            L = []
                            for g in range(2):
                                L.append(lambda g=g: P.op("dve", lambda e: e.reciprocal(out=rs[64:65, g, :], in_=accs[g][64:65, :]),
                                                          reads=[Baccs[g]], writes=[Brs_]))
                                L.append(lambda g=g: P.op("pe", lambda e: e.matmul(pbc[0:64, :], lhsT=onesf[64:65, 0:64], rhs=rs[64:65, g, :],
                                                                                  start=True, stop=True),
                                                          reads=[Brs_, Bc], writes=[Bpbc]))
                                L.append(lambda g=g: P.op("dve", lambda e: e.tensor_copy(out=bcs[:], in_=pbc[0:64, :]),
                                                          reads=[Bpbc], writes=[Bbcs]))
                                L.append(lambda g=g: P.op("dve", lambda e: e.tensor_tensor(out=stg[g][:], in0=accs[g][0:64, :], in1=bcs[:], op=ALU.mult),
                                                          reads=[Baccs[g], Bbcs], writes=[Bstg[g]]))

                            def pack():
                                P.op("pe", lambda e: e.matmul(ppk[:], lhsT=ident[0:64, :], rhs=stg[0][:], start=True, stop=False),
                                     reads=[Bstg[0], Bc], writes=[Bppk])
                                P.op("pe", lambda e: e.matmul(ppk[:], lhsT=EB[:], rhs=stg[1][:], start=False, stop=True),
                                     reads=[Bstg[1], Bc], writes=[Bppk])
                            L.append(pack)
                            L.append(lambda: P.op("dve", lambda e: e.tensor_copy(out=OT[:, :, qcols], in_=ppk[:].rearrange("p (i q) -> p i q", i=4)),
                                                  reads=[Bppk], writes=[BOT[tau // 4]]))
                            return L
                        epi = mk_epi()
                    while epi:
                        epi.pop(0)()
        while ws_jobs:
            ws_jobs.pop(0)()
        P.barrier()

        YT = sbuf(seq_es, "YT", [128, 4, S], BF16)
        if cfg.get("fft2", True):
            with ExitStack() as es:
                cdt = sbuf(es, "cdt", [128, 256], BF16)
                bdt = sbuf(es, "bdt", [128, 384], BF16)
                P.dma("sp", cdt[:], c["cd"], writes=[Bc], key="c_cd")
                P.dma("sp", bdt[:], c["bd"], writes=[Bc], key="c_bd")
                NT2 = 8
                T2 = [sbuf(es, "T2_%d" % i, [128, 2, 512], BF16) for i in range(NT2)]
                Zb = [sbuf(es, "Zb%d" % i, [128, 4, 2, 512], BF16) for i in range(2)]
                pz = [psum(es, "pz%d" % i, [128, 2, 256], F32) for i in range(4)]
                pyc = [psum(es, "pyc%d" % i, [128, 512], F32) for i in range(2)]
                BT2 = [Buf("T2") for i in range(NT2)]
                BZb = [Buf("Zb") for i in range(2)]
                Bpz = [Buf("pz", True) for i in range(4)]
                Bpyc = [Buf("pyc", True) for i in range(2)]
                dscv = c["dsc"].rearrange("(m l) t c h -> m (l t) c h", l=4)
                pzc = 0
                for M in range(8):
                    zb = M % 2
                    for mm in range(4):
                        m = 4 * M + mm
                        i3 = m % NT2
                        P.dma("sp", T2[i3][:], dscv[m], reads=Bdsc, writes=[BT2[i3]], key="t2_%d" % i3)
                        for gp in range(2):
                            pzi = pzc % 4
                            pzc += 1
                            for gg in range(2):
                                g = 2 * gp + gg
                                P.op("pe", lambda e, g=g, gg=gg, pzi=pzi: e.matmul(
                                    pz[pzi][:, gg, :], lhsT=T2[i3][:, 0, g * 128:(g + 1) * 128], rhs=bdt[:, 128:384],
                                    start=True, stop=False), reads=[BT2[i3], Bc], writes=[Bpz[pzi]])
                                P.op("pe", lambda e, g=g, gg=gg, pzi=pzi: e.matmul(
                                    pz[pzi][:, gg, :], lhsT=T2[i3][:, 1, g * 128:(g + 1) * 128], rhs=bdt[:, 0:256],
                                    start=False, stop=True), reads=[BT2[i3], Bc], writes=[Bpz[pzi]])
                            eng = "act" if gp == 0 else "dve"
                            outap = Zb[zb][:, 2 * gp:2 * gp + 2, :, mm * 128:(mm + 1) * 128]
                            inap = pz[pzi][:].rearrange("p g (c t) -> p g c t", c=2)
                            if eng == "act":
                                P.op("act", lambda e, outap=outap, inap=inap: e.activation(out=outap, in_=inap, func=AF.Copy),
                                     reads=[Bpz[pzi]], writes=[BZb[zb]])
                            else:
                                P.op("dve", lambda e, outap=outap, inap=inap: e.tensor_copy(out=outap, in_=inap),
                                     reads=[Bpz[pzi]], writes=[BZb[zb]])
                    for g in range(4):
                        pi = g % 2
                        P.op("pe", lambda e, g=g, pi=pi: e.matmul(pyc[pi][:], lhsT=cdt[:, 0:128], rhs=Zb[zb][:, g, 0, :],
                                                                 start=True, stop=False), reads=[BZb[zb], Bc], writes=[Bpyc[pi]])
                        P.op("pe", lambda e, g=g, pi=pi: e.matmul(pyc[pi][:], lhsT=cdt[:, 128:256], rhs=Zb[zb][:, g, 1, :],
                                                                 start=False, stop=True), reads=[BZb[zb], Bc], writes=[Bpyc[pi]])
                        ytv = YT[:, g, :].rearrange("q (t k r) -> q t k r", t=32, k=32, r=4)
                        t0 = 16 * (M % 2)
                        outap = ytv[:, t0:t0 + 16, :, M // 2]
                        inap = pyc[pi][:].rearrange("p (a k) -> p a k", a=16)
                        if g % 2 == 0:
                            P.op("act", lambda e, outap=outap, inap=inap: e.activation(out=outap, in_=inap, func=AF.Copy),
                                 reads=[Bpyc[pi]], writes=BYT)
                        else:
                            P.op("dve", lambda e, outap=outap, inap=inap: e.tensor_copy(out=outap, in_=inap),
                                 reads=[Bpyc[pi]], writes=BYT)
            P.barrier()

        if "d_YT" in dbg:
            P.dma("sp", c["d_YT"], YT[:], reads=BYT, key="d_YT")
        if "d_OT" in dbg:
            P.dma("sp", c["d_OT"], OT[:], reads=BOT, key="d_OT")

        if cfg.get("tail", True):
            tail_phase(P, c, cfg, s, glob, sbuf, psum, YT, OT, BYT, BOT, Bws, ident, onesb, Bc, xv, yv)
            P.barrier()


def tail_phase(P, c, cfg, s, glob, sbuf, psum, YT, OT, BYT, BOT, Bws, ident, onesb, Bc, xv, yv):
    nc = P.nc
    ws = c["ws"]
    NW = 4
    with ExitStack() as es:
        X = sbuf(es, "X", [128, 4, D], F32)
        xpre = sbuf(es, "xpre", [128, 4, D], BF16)
        xbt = [xpre[:, i, :] for i in range(4)]
        aT = [sbuf(es, "aT%d" % i, [128, 8, 512], BF16) for i in range(2)]
        big = sbuf(es, "big", [128, 32, 512], BF16)
        sg = sbuf(es, "sg", [128, 2, 512], F32)
        LNt = sbuf(es, "LNt", [128, 2, D], F32)
        LNt1 = sbuf(es, "LNt1", [128, 2, D], F32)
        wr = [sbuf(es, "wr%d" % i, [128, 4096], BF16) for i in range(NW)]
        KmT = sbuf(es, "KmT", [128, 8, 256], BF16)
        Vm = sbuf(es, "Vm", [128, 2, D], BF16)
        st = sbuf(es, "st", [128, 4, 2, 6], F32)
        mv = sbuf(es, "mv", [128, 4, 2], F32)
        rstd = sbuf(es, "rstd_t", [128, 4], F32)
        nmr = sbuf(es, "nmr", [128, 4], F32)
        ptr = [psum(es, "ptr%d" % i, [128, 8, 128], BF16) for i in range(2)]
        NB = 6
        pb = [psum(es, "pb%d" % i, [128, 512], F32) for i in range(NB)]
        BX = [Buf("X%d" % t) for t in range(4)]
        Bxpre = [Buf("xpre") for i in range(4)]
        Bxbt = Bxpre
        BaT = [Buf("aT") for i in range(2)]
        Bbig = [Buf("big%d" % i) for i in range(32)]
        Bsg = [Buf("sg") for i in range(2)]
        BLN = Buf("LNt")
        BLN1 = Buf("LNt1")
        Bwr = [Buf("wr") for i in range(NW)]
        BKm, BVm = Buf("KmT"), Buf("Vm")
        Bst = [Buf("st") for t in range(4)]
        Bmv = [Buf("mv") for t in range(4)]
        Brstd = [Buf("rstd") for t in range(4)]
        Bnmr = [Buf("nmr") for t in range(4)]
        Bptr = [Buf("ptr", True) for i in range(2)]
        Bpb = [Buf("pb", True) for i in range(NB)]
        state = {"w": 0, "b": 0, "x": 0, "p": 0}

        def fetch(u):
            sl = state["w"] % NW
            state["w"] += 1
            P.dma("sp", wr[sl][:], ws[u], reads=[Bws[u]], writes=[Bwr[sl]], key="wr%d" % sl)
            return wr[sl][:].rearrange("p (a b) -> p a b", a=8), Bwr[sl]

        def bank():
            i = state["b"] % NB
            state["b"] += 1
            return pb[i], Bpb[i]

        def transp(src, Bsrc, dst, Bdst, c0, n=128):
            pi = state["p"] % 2
            state["p"] += 1
            for k in range(8):
                P.op("pe", lambda e, k=k: e.transpose(out=ptr[pi][:, k, :], in_=src[:, k * 128:(k + 1) * 128], identity=ident[:]),
                     reads=[Bsrc, Bc], writes=[Bptr[pi]])
            P.op("act", lambda e: e.activation(out=dst[:, :, c0:c0 + n], in_=ptr[pi][:, :, 0:n], func=AF.Copy),
                 reads=[Bptr[pi]], writes=[Bdst])

        def load_ln(gname, bname):
            P.dma("pool", LNt[:, 0, :], c[gname].partition_broadcast(128), writes=[BLN], key="lng")
            P.dma("pool", LNt[:, 1, :], c[bname].partition_broadcast(128), writes=[BLN], key="lnb")

        for mc in range(2):
            i = state["x"] % 2
            state["x"] += 1
            P.dma("pool", xbt[i][:], c["mem"][s, mc * 128:(mc + 1) * 128, :], writes=[Bxbt[i]], key="memb%d" % i)
            transp(xbt[i], Bxbt[i], aT[0], BaT[0], mc * 128)
        for hb in range(2):
            wk, Bwk = fetch(U_MK + hb)
            for fl in range(4):
                p_, Bp_ = bank()
                for k in range(8):
                    P.op("pe", lambda e, k=k: e.matmul(p_[:, 0:256], lhsT=wk[:, k, fl * 128:(fl + 1) * 128], rhs=aT[0][:, k, 0:256],
                                                       start=(k == 0), stop=(k == 7)), reads=[Bwk, BaT[0]], writes=[Bp_])
                P.op("act", lambda e: e.activation(out=KmT[:, 4 * hb + fl, :], in_=p_[:, 0:256], func=AF.Copy),
                     reads=[Bp_], writes=[BKm])
        for hb in range(2):
            wv, Bwv = fetch(U_MV + hb)
            for mc in range(2):
                p_, Bp_ = bank()
                for k in range(8):
                    P.op("pe", lambda e, k=k: e.matmul(p_[:], lhsT=aT[0][:, k, mc * 128:(mc + 1) * 128], rhs=wv[:, k, :],
                                                       start=(k == 0), stop=(k == 7)), reads=[Bwv, BaT[0]], writes=[Bp_])
                P.op("dve", lambda e: e.tensor_copy(out=Vm[:, mc, hb * 512:(hb + 1) * 512], in_=p_[:]),
                     reads=[Bp_], writes=[BVm])

        def layer_norm(t, dstT, BdstT, G, last, tab=None, Btab=None):
            tab = LNt if tab is None else tab
            Btab = BLN if Btab is None else Btab
            xt = X[:, t, :]
            for a in range(2):
                P.op("dve", lambda e, a=a: e.bn_stats(out=st[:, t, a, :], in_=X[:, t, a * 512:(a + 1) * 512]),
                     reads=[BX[t]], writes=[Bst[t]])
            P.op("dve", lambda e: e.bn_aggr(out=mv[:, t, :], in_=st[:, t, :, :].rearrange("p a b -> p (a b)")),
                 reads=[Bst[t]], writes=[Bmv[t]])
            P.op("dve", lambda e: e.tensor_scalar_add(out=rstd[:, t:t + 1], in0=mv[:, t, 1:2], scalar1=LN_EPS),
                 reads=[Bmv[t]], writes=[Brstd[t]])
            P.op("act", lambda e: e.activation(out=rstd[:, t:t + 1], in_=rstd[:, t:t + 1], func=AF.Sqrt),
                 reads=[Brstd[t]], writes=[Brstd[t]])
            P.op("dve", lambda e: e.reciprocal(out=rstd[:, t:t + 1], in_=rstd[:, t:t + 1]), reads=[Brstd[t]], writes=[Brstd[t]])
            P.op("dve", lambda e: e.tensor_scalar(out=nmr[:, t:t + 1], in0=mv[:, t, 0:1], scalar1=rstd[:, t:t + 1], scalar2=-1.0,
                                                  op0=ALU.mult, op1=ALU.mult), reads=[Bmv[t], Brstd[t]], writes=[Bnmr[t]])
            P.op("act", lambda e: e.activation(out=xt, in_=xt, func=AF.Identity, bias=nmr[:, t:t + 1], scale=rstd[:, t:t + 1]),
                 reads=[BX[t], Bnmr[t], Brstd[t]], writes=[BX[t]])
            P.op("dve", lambda e: e.tensor_tensor(out=xt, in0=xt, in1=tab[:, 0, :], op=ALU.mult),
                 reads=[BX[t], Btab], writes=[BX[t]])
            P.op("dve", lambda e: e.tensor_tensor(out=xt, in0=xt, in1=tab[:, 1, :], op=ALU.add),
                 reads=[BX[t], Btab], writes=[BX[t]])
            if not last:
                i = state["x"] % 4
                state["x"] += 1
                P.op("act", lambda e: e.activation(out=xbt[i][:], in_=xt, func=AF.Copy), reads=[BX[t]], writes=[Bxbt[i]])
                return lambda: transp(xbt[i], Bxbt[i], dstT, BdstT, t * 128)
            else:
                P.dma("pool", yv[s, 4 * G + t], xt, reads=[BX[t]], key="yst%d" % t)
                return None

        def residual(t, hb, p_, Bp_):
            xs = X[:, t, hb * 512:(hb + 1) * 512]
            P.op("dve", lambda e: e.scalar_tensor_tensor(out=xs, in0=xs, scalar=ALPHA, in1=p_[:], op0=ALU.mult, op1=ALU.add),
                 reads=[BX[t], Bp_], writes=[BX[t]])

        def t0_loads(G):
            for t in range(4):
                P.dma("pool", xpre[:, t, :], xv[s, 4 * G + t], writes=[Bxpre[t]], key="xpre%d" % t)

        def t0_transposes(G):
            a = G % 2
            for t in range(4):
                transp(xpre[:, t, :], Bxpre[t], aT[a], BaT[a], t * 128)

        mg = sbuf(es, "mg", [128, 8, 512], BF16)
        Bmg = [Buf("mg%d" % i) for i in range(8)]
        SA, BSA = aT[0], BaT[0]
        SB, BSB = aT[1], BaT[1]

        def prefetch_T(G):
            for t in range(4):
                transp(xpre[:, t, :], Bxpre[t], SA, BSA, t * 128)

        def T1_half(G, hb, fls=(0, 1, 2, 3)):
            gc = slice(512 * G, 512 * G + 512)
            wga, Bga = fetch(U_GA + hb)
            wgf, Bgf = fetch(U_GF + hb)
            wbr, Bbr = fetch(U_BR + hb)
            for fl in fls:
                fc = 4 * hb + fl
                fs = slice(fl * 128, (fl + 1) * 128)
                pga, Bpga = bank()
                for k in range(8):
                    P.op("pe", lambda e, k=k: e.matmul(pga[:], lhsT=wga[:, k, fs], rhs=SA[:, k, :], start=(k == 0), stop=(k == 7)),
                         reads=[Bga, BSA], writes=[Bpga])
                pgf, Bpgf = bank()
                for k in range(8):
                    P.op("pe", lambda e, k=k: e.matmul(pgf[:], lhsT=wgf[:, k, fs], rhs=SA[:, k, :], start=(k == 0), stop=(k == 7)),
                         reads=[Bgf, BSA], writes=[Bpgf])
                pya, Bpya = bank()
                for a in range(4):
                    P.op("pe", lambda e, a=a: e.matmul(pya[:], lhsT=wbr[:, a, fs], rhs=OT[:, a, gc], start=(a == 0), stop=(a == 3)),
                         reads=[Bbr, BOT[G]], writes=[Bpya])
                pyf, Bpyf = bank()
                for a in range(4):
                    P.op("pe", lambda e, a=a: e.matmul(pyf[:], lhsT=wbr[:, 4 + a, fs], rhs=YT[:, a, gc], start=(a == 0), stop=(a == 3)),
                         reads=[Bbr, BYT[G]], writes=[Bpyf])
                P.op("act", lambda e: e.activation(out=sg[:, 0, :], in_=pga[:], func=AF.Sigmoid), reads=[Bpga], writes=[Bsg[0]])
                P.op("act", lambda e: e.activation(out=sg[:, 1, :], in_=pgf[:], func=AF.Sigmoid), reads=[Bpgf], writes=[Bsg[1]])
                P.op("dve", lambda e: e.tensor_tensor(out=sg[:, 0, :], in0=pya[:], in1=sg[:, 0, :], op=ALU.mult),
                     reads=[Bpya, Bsg[0]], writes=[Bsg[0]])
                P.op("dve", lambda e: e.tensor_tensor(out=sg[:, 1, :], in0=pyf[:], in1=sg[:, 1, :], op=ALU.mult),
                     reads=[Bpyf, Bsg[1]], writes=[Bsg[1]])
                P.op("pool", lambda e: e.tensor_tensor(out=mg[:, fc, :], in0=sg[:, 0, :], in1=sg[:, 1, :], op=ALU.add),
                     reads=[Bsg[0], Bsg[1]], writes=[Bmg[fc]])

        groups = cfg.get("groups", list(range(8)))
        P.dma("pool", LNt1[:, 0, :], c["ln1_g"].partition_broadcast(128), writes=[BLN1], key="ln1g")
        P.dma("pool", LNt1[:, 1, :], c["ln1_b"].partition_broadcast(128), writes=[BLN1], key="ln1b")
        load_ln("ln2_g", "ln2_b")
        t0_loads(groups[0])
        prefetch_T(groups[0])
        T1_half(groups[0], 0)
        T1_half(groups[0], 1)
        if len(groups) > 1:
            t0_loads(groups[1])
            prefetch_T(groups[1])
        ln3_pend = []
        for gi, G in enumerate(groups):
            Gn = groups[gi + 1] if gi + 1 < len(groups) else None
            Gnn = groups[gi + 2] if gi + 2 < len(groups) else None
            if gi == 0:
                for t in range(4):
                    P.dma("pool", X[:, t, :], xv[s, 4 * G + t], writes=[BX[t]], key="xld%d" % t)
            wm = [fetch(U_MIX + hb) for hb in range(2)]
            pend = []
            for t in range(4):
                for _ in range(3 if t == 0 else 1):
                    if ln3_pend:
                        ln3_pend.pop(0)()
                for hb in range(2):
                    p_, Bp_ = bank()
                    for k in range(8):
                        P.op("pe", lambda e, k=k: e.matmul(p_[:], lhsT=mg[:, k, t * 128:(t + 1) * 128], rhs=wm[hb][0][:, k, :],
                                                           start=(k == 0), stop=(k == 7)), reads=[Bmg[k], wm[hb][1]], writes=[Bp_])
                    residual(t, hb, p_, Bp_)
                if len(pend) >= 2:
                    pend.pop(0)()
                pend.append(layer_norm(t, SB, BSB, G, False, LNt1, BLN1))
            if Gn is not None:
                T1_half(Gn, 0)
            while pend:
                pend.pop(0)()
            for hb in range(2):
                wq, Bwq = fetch(U_MQ + hb)
                for fl in range(4):
                    qc = 4 * hb + fl
                    p_, Bp_ = bank()
                    for k in range(8):
                        P.op("pe", lambda e, k=k: e.matmul(p_[:], lhsT=wq[:, k, fl * 128:(fl + 1) * 128], rhs=SB[:, k, :],
                                                           start=(k == 0), stop=(k == 7)), reads=[Bwq, BSB], writes=[Bp_])
                    P.op("act", lambda e, qc=qc: e.activation(out=big[:, qc, :], in_=p_[:], func=AF.Copy),
                         reads=[Bp_], writes=[Bbig[qc]])
            def xa_scores(h):
                for mc in range(2):
                    p_, Bp_ = bank()
                    for dc in range(2):
                        P.op("pe", lambda e, dc=dc: e.matmul(p_[:], lhsT=KmT[:, 2 * h + dc, mc * 128:(mc + 1) * 128], rhs=big[:, 2 * h + dc, :],
                                                             start=(dc == 0), stop=(dc == 1)), reads=[BKm, Bbig[2 * h + dc]], writes=[Bp_])
                    P.op("act", lambda e: e.activation(out=big[:, 8 + 2 * h + mc, :], in_=p_[:], func=AF.Exp, scale=1.0 / 16.0),
                         reads=[Bp_], writes=[Bbig[8 + 2 * h + mc]])

            xa_scores(0)
            for h in range(4):
                if h + 1 < 4:
                    xa_scores(h + 1)
                psm, Bpsm = bank()
                for mc in range(2):
                    P.op("pe", lambda e, mc=mc: e.matmul(psm[:], lhsT=onesb[:], rhs=big[:, 8 + 2 * h + mc, :], start=(mc == 0), stop=(mc == 1)),
                         reads=[Bc, Bbig[8 + 2 * h + mc]], writes=[Bpsm])
                r = h % 2
                P.op("dve", lambda e: e.reciprocal(out=sg[:, r, :], in_=psm[:]), reads=[Bpsm], writes=[Bsg[r]])
                for dc in range(2):
                    p_, Bp_ = bank()
                    for mc in range(2):
                        P.op("pe", lambda e, mc=mc: e.matmul(p_[:], lhsT=Vm[:, mc, (2 * h + dc) * 128:(2 * h + dc + 1) * 128],
                                                             rhs=big[:, 8 + 2 * h + mc, :], start=(mc == 0), stop=(mc == 1)),
                             reads=[BVm, Bbig[8 + 2 * h + mc]], writes=[Bp_])
                    P.op("dve", lambda e: e.tensor_tensor(out=big[:, 16 + 2 * h + dc, :], in0=p_[:], in1=sg[:, r, :], op=ALU.mult),
                         reads=[Bp_, Bsg[r]], writes=[Bbig[16 + 2 * h + dc]])
            wo = [fetch(U_MO + hb) for hb in range(2)]
            pend = []
            for t in range(4):
                for hb in range(2):
                    p_, Bp_ = bank()
                    for k in range(8):
                        P.op("pe", lambda e, k=k: e.matmul(p_[:], lhsT=big[:, 16 + k, t * 128:(t + 1) * 128], rhs=wo[hb][0][:, k, :],
                                                           start=(k == 0), stop=(k == 7)), reads=[Bbig[16 + k], wo[hb][1]], writes=[Bp_])
                    residual(t, hb, p_, Bp_)
                if len(pend) >= 2:
                    pend.pop(0)()
                pend.append(layer_norm(t, SB, BSB, G, False))
            if Gn is not None:
                T1_half(Gn, 1, (0, 1))
            while pend:
                pend.pop(0)()
            load_ln("ln3_g", "ln3_b")
            if Gnn is not None:
                t0_loads(Gnn)
            for j in range(8):
                wu, Bwu = fetch(U_UP + j)
                for fl in range(4):
                    fc = 4 * j + fl
                    p_, Bp_ = bank()
                    for k in range(8):
                        P.op("pe", lambda e, k=k: e.matmul(p_[:], lhsT=wu[:, k, fl * 128:(fl + 1) * 128], rhs=SB[:, k, :],
                                                           start=(k == 0), stop=(k == 7)), reads=[Bwu, BSB], writes=[Bp_])
                    m = fl % 2
                    P.op("act", lambda e: e.activation(out=sg[:, m, :], in_=p_[:], func=AF.Relu), reads=[Bp_], writes=[Bsg[m]])
                    P.op("pool", lambda e, fc=fc: e.tensor_tensor(out=big[:, fc, :], in0=sg[:, m, :], in1=sg[:, m, :], op=ALU.mult),
                         reads=[Bsg[m]], writes=[Bbig[fc]])
            for hb in range(2):
                accs = [bank() for t in range(4)]
                for j in range(4):
                    wd, Bwd = fetch(U_DN + 4 * hb + j)
                    for fcl in range(8):
                        fc = 8 * j + fcl
                        for t in range(4):
                            p_, Bp_ = accs[t]
                            P.op("pe", lambda e, t=t: e.matmul(p_[:], lhsT=big[:, fc, t * 128:(t + 1) * 128], rhs=wd[:, fcl, :],
                                                               start=(fc == 0), stop=(fc == 31)), reads=[Bbig[fc], Bwd], writes=[Bp_])
                for t in range(4):
                    residual(t, hb, accs[t][0], accs[t][1])
            if Gn is not None:
                T1_half(Gn, 1, (2, 3))
            if Gnn is not None:
                prefetch_T(Gnn)
            def ln3_thunk(t, G=G, Gn=Gn):
                layer_norm(t, None, None, G, True)
                if Gn is not None:
                    P.dma("pool", X[:, t, :], xv[s, 4 * Gn + t], writes=[BX[t]], key="xld%d" % t)
                if t == 3:
                    load_ln("ln2_g", "ln2_b")
            if Gn is None:
                for t in range(4):
                    ln3_thunk(t)
            else:
                ln3_pend = [(lambda t=t, f=ln3_thunk: f(t)) for t in range(4)]


_CACHE = {}


def _build_program(cfg_key="full", cfg=None):
    if cfg_key in _CACHE:
        return _CACHE[cfg_key]
    cfg = cfg or {}
    nc1, c1 = make_nc(cfg)
    p1 = Prog(nc1, None)
    build(p1, c1, cfg)
    p1.finish()
    nc2, c2 = make_nc(cfg)
    p2 = Prog(nc2, p1.needed)
    build(p2, c2, cfg)
    p2.finish()
    _CACHE[cfg_key] = (nc2, p2)
    return nc2, p2


def _const_tables():
    return {
        "rope_p": _rope_table(np.arange(S)),
        "rope_f": _rope_table(np.arange(SF)),
        "e1p": _e1_table(S, False),
        "e1s": _e1_table(SF, True),
        "bd": _bd_table(),
        "cd": _cd_table(),
    }


def make_in_maps(inputs, cores=range(8)):
    f = lambda a: np.ascontiguousarray(np.asarray(a, dtype=np.float32))
    xp = f(inputs["x_prompt"])
    xs = f(inputs["x_sample"])
    mp = f(inputs["mem_prompt"])
    ms = f(inputs["mem_sample"])
    tabs = _const_tables()
    shared = {
        "w_in": f(inputs["w_in"][0]), "w_ab": f(inputs["w_attn_branch"][0]), "w_fb": f(inputs["w_fourier_branch"][0]),
        "w_mix": f(inputs["w_mix_out"][0]), "w_mq": f(inputs["w_mem_q"][0]), "w_mk": f(inputs["w_mem_k"][0]),
        "w_mv": f(inputs["w_mem_v"][0]), "w_mo": f(inputs["w_mem_o"][0]), "w_up": f(inputs["w_up"][0]),
        "w_dn": f(inputs["w_down"][0]), "qn": f(inputs["q_norm"]).reshape(1, 64), "kn": f(inputs["k_norm"]).reshape(1, 64),
    }
    for v in VNAMES:
        shared[v] = f(inputs[v]).reshape(1, D)
    shared.update(tabs)
    maps = []
    for cid in cores:
        b, cq = cid // 4, cid % 4
        m = dict(shared)
        m["xseq"] = np.ascontiguousarray(np.stack([xp[2 * cid], xp[2 * cid + 1], xs[b, cq::4]], axis=0))
        m["xfull"] = xs[b]
        m["mem"] = np.ascontiguousarray(np.stack([mp[2 * cid], mp[2 * cid + 1], ms[b]], axis=0))
        m["rope_s"] = _rope_table(cq + 4 * np.arange(S))
        m["coef"] = _coef_table(cq).reshape(128, 256)
        maps.append(m)
    return maps


def kernel(**inputs):
    nc, _ = _build_program("full", {})
    maps = make_in_maps(inputs)
    res = run_bass_kernel_spmd(nc, maps, core_ids=list(range(8)))
    yp = np.empty((16, S, D), np.float32)
    ys = np.empty((2, SF, D), np.float32)
    for cid in range(8):
        y = np.asarray(res.results[cid]["y"], dtype=np.float32)
        b, cq = cid // 4, cid % 4
        yp[2 * cid] = y[0]
        yp[2 * cid + 1] = y[1]
        ys[b, cq::4] = y[2]
    return yp, ys
```

```python
import numpy as np
import ml_dtypes
from contextlib import ExitStack
import concourse.bass as bass
import concourse.mybir as mybir
from concourse.bass_utils import run_bass_kernel_spmd

F32 = mybir.dt.float32
BF16 = mybir.dt.bfloat16
AF = mybir.ActivationFunctionType
ALU = mybir.AluOpType
AX = mybir.AxisListType

D = 1024
S = 4096
SF = 16384
NT = 32
ALPHA = 2.0 ** 0.25
RMS_EPS = 1e-6
LN_EPS = 1e-5


class Buf:
    __slots__ = ("name", "lw", "rd", "psum")

    def __init__(self, name, psum=False):
        self.name = name
        self.lw = None
        self.rd = []
        self.psum = psum


class Prog:
    ENG = ("pe", "act", "dve", "pool", "sp")

    def __init__(self, nc, needed=None):
        self.nc = nc
        self.dry = needed is None
        self.needed = set() if needed is None else needed
        self.e = {"pe": nc.tensor, "act": nc.scalar, "dve": nc.vector,
                  "pool": nc.gpsimd, "sp": nc.sync}
        self.nops = {k: 0 for k in self.ENG}
        self.sig = {k: 0 for k in self.ENG}
        self.waited = {k: {} for k in self.ENG}
        self.pending = {k: [] for k in self.ENG}
        self.sem = {}
        self.dmacnt = {}
        self.last_tok = {k: None for k in self.ENG}
        self.last_dma = {}
        self.ninst = 0

    def _sem(self, key):
        if key not in self.sem:
            cm = self.nc.semaphore("s_" + str(len(self.sem)))
            self.sem[key] = cm.__enter__()
        return self.sem[key]

    def _deps(self, eng, reads, writes):
        deps = list(self.pending[eng])
        self.pending[eng] = []
        for b in reads:
            if b.lw is not None:
                deps.append(b.lw)
            if b.psum:
                deps.extend(t for t in b.rd if t[1] != eng)
        for b in writes:
            if b.lw is not None:
                deps.append(b.lw)
            deps.extend(b.rd)
        return deps

    def _emit_waits(self, eng, deps):
        best = {}
        for t in deps:
            if t[0] == "eng":
                src, idx, val = t[1], t[2], t[3]
                if src == eng and eng == "pe":
                    continue
                key = src
            else:
                idx, val = t[2], t[3]
                key = ("dma", t[1])
            if self.waited[eng].get(key, -1) >= idx:
                continue
            if key not in best or best[key][0] < idx:
                best[key] = (idx, val, t)
        for key, (idx, val, t) in best.items():
            self.waited[eng][key] = idx
            if self.dry:
                if t[0] == "eng":
                    self.needed.add((t[1], t[2]))
            else:
                assert val is not None, ("dep not signalled", t)
                self.e[eng].wait_ge(self._sem(key), val)
                self.ninst += 1

    def _update(self, tok, reads, writes):
        for b in reads:
            b.rd.append(tok)
        for b in writes:
            b.lw = tok
            b.rd = []

    def op(self, eng, fn, reads=(), writes=()):
        deps = self._deps(eng, reads, writes)
        self._emit_waits(eng, deps)
        idx = self.nops[eng]
        self.nops[eng] += 1
        val = None
        if not self.dry:
            ins = fn(self.e[eng])
            self.ninst += 1
            if (eng, idx) in self.needed:
                self.sig[eng] += 1
                val = self.sig[eng]
                ins.then_inc(self._sem(eng), 1)
        tok = ("eng", eng, idx, val)
        self.last_tok[eng] = tok
        self._update(tok, reads, writes)
        return tok

    def dma(self, q, out, in_, reads=(), writes=(), key=None, **kw):
        deps = self._deps(q, reads, writes)
        self._emit_waits(q, deps)
        n = self.dmacnt.get(key, 0) + 1
        self.dmacnt[key] = n
        if not self.dry:
            self.e[q].dma_start(out=out, in_=in_, **kw).then_inc(self._sem(("dma", key)), 16)
            self.ninst += 1
        tok = ("dma", key, n, 16 * n)
        self.last_dma[key] = tok
        self._update(tok, reads, writes)
        return tok

    def barrier(self):
        toks = [t for t in self.last_tok.values() if t is not None]
        toks.extend(self.last_dma.values())
        for k in self.ENG:
            self.pending[k] = list(toks)

    def finish(self):
        self.barrier()
        for k in self.ENG:
            deps = self.pending[k]
            self.pending[k] = []
            self._emit_waits(k, deps)


def _bf(a):
    return np.ascontiguousarray(a.astype(np.float32)).astype(ml_dtypes.bfloat16)


def _rope_table(pos):
    pos = np.asarray(pos)
    row = (pos // 64).astype(np.float32)
    col = (pos % 64).astype(np.float32)
    freqs = (np.float32(10000.0) ** (-np.arange(0, 32, 2, dtype=np.float32) / np.float32(32))).astype(np.float32)
    ang = np.concatenate([row[:, None] * freqs, col[:, None] * freqs], axis=-1).astype(np.float32)
    ang = np.concatenate([ang, ang], axis=-1)
    c = np.cos(ang).astype(np.float32)
    s = np.sin(ang).astype(np.float32)
    s2 = np.concatenate([-s[:, :32], s[:, 32:]], axis=-1)
    return np.ascontiguousarray(np.concatenate([c, s2], axis=-1).astype(np.float32))


def _e1_table(s_true, complex_in):
    tau = np.arange(32)[:, None, None]
    p = np.arange(128)[None, :, None]
    ka = np.arange(128)[None, None, :]
    n = 32 * p + tau
    ang = 2.0 * np.pi * ((ka * n) % 4096) / 4096.0
    nrm = 1.0 / np.sqrt(float(s_true) * 128.0)
    ec = np.cos(ang) * nrm
    es = np.sin(ang) * nrm
    parts = [ec, es] + ([-ec] if complex_in else [])
    return _bf(np.concatenate(parts, axis=-1))


def _bd_table():
    l = np.arange(4)[:, None, None, None]
    tau = np.arange(32)[None, :, None, None]
    l2 = np.arange(4)[None, None, :, None]
    kb = np.arange(32)[None, None, None, :]
    ang = 2.0 * np.pi * ((tau * kb) % 32) / 32.0
    dl = (l == l2).astype(np.float64)
    bc = (dl * np.cos(ang)).reshape(128, 128)
    bs = (dl * np.sin(ang)).reshape(128, 128)
    return _bf(np.concatenate([-bs, bc, bs], axis=-1))


def _cd_table():
    j = np.arange(128)[:, None]
    j2 = np.arange(128)[None, :]
    ang = 2.0 * np.pi * ((j * j2) % 128) / 128.0
    return _bf(np.concatenate([np.cos(ang), -np.sin(ang)], axis=-1))


def _coef_table(cq):
    p = np.arange(128)[:, None, None]
    tau = np.arange(32)[None, :, None]
    n1 = np.arange(4)[None, None, :]
    n = 4096 * n1 + 32 * p + tau
    ang = 2.0 * np.pi * ((cq * n) % 16384) / 16384.0
    return np.ascontiguousarray(np.stack([np.cos(ang), -np.sin(ang)], axis=-1).astype(np.float32))


WNAMES = ["w_in", "w_ab", "w_fb", "w_mix", "w_mq", "w_mk", "w_mv", "w_mo", "w_up", "w_dn"]
WSHAPES = {"w_in": [1024, 3328], "w_ab": [512, 1024], "w_fb": [512, 1024], "w_mix": [1024, 1024],
           "w_mq": [1024, 1024], "w_mk": [1024, 1024], "w_mv": [1024, 1024], "w_mo": [1024, 1024],
           "w_up": [1024, 4096], "w_dn": [4096, 1024]}
VNAMES = ["ln1_g", "ln1_b", "ln2_g", "ln2_b", "ln3_g", "ln3_b"]

U_GA, U_GF, U_BR, U_MIX, U_MQ, U_MO, U_UP, U_DN, U_MK, U_MV = 0, 2, 4, 6, 8, 10, 12, 20, 28, 30
NUNITS = 32


def make_nc(cfg):
    nc = bass.Bass("TRN2", target_bir_lowering=False)
    c = {}

    def inp(name, shape, dt=F32):
        c[name] = nc.dram_tensor(name, shape, dt, kind="ExternalInput").ap()

    inp("xseq", [3, S, D])
    inp("xfull", [SF, D])
    inp("mem", [3, 256, D])
    for w in WNAMES:
        inp(w, WSHAPES[w])
    inp("qn", [1, 64])
    inp("kn", [1, 64])
    for v in VNAMES:
        inp(v, [1, D])
    inp("rope_p", [S, 128])
    inp("rope_s", [S, 128])
    inp("rope_f", [SF, 128])
    inp("e1p", [32, 128, 256], BF16)
    inp("e1s", [32, 128, 384], BF16)
    inp("bd", [128, 384], BF16)
    inp("cd", [128, 256], BF16)
    inp("coef", [128, 32 * 4 * 2])
    c["y"] = nc.dram_tensor("y", [3, S, D], F32, kind="ExternalOutput").ap()
    c["ws"] = nc.dram_tensor("ws", [NUNITS, 128, 4096], BF16).ap()
    c["dsc"] = nc.dram_tensor("dsc", [128, 32, 2, 512], BF16).ap()
    for name, shape, dt in cfg.get("dbg", []):
        c[name] = nc.dram_tensor(name, shape, dt, kind="ExternalOutput").ap()
    return nc, c


def build(P, c, cfg):
    nc = P.nc
    glob = ExitStack()

    uniq = {"n": 0}

    def sbuf(es, name, shape, dt):
        uniq["n"] += 1
        return es.enter_context(nc.sbuf_tensor("sb%d_%s" % (uniq["n"], name), shape, dt))

    def psum(es, name, shape, dt):
        uniq["n"] += 1
        return es.enter_context(nc.psum_tensor("ps%d_%s" % (uniq["n"], name), shape, dt))

    dbg = {n for n, _, _ in cfg.get("dbg", [])}

    ident = sbuf(glob, "ident", [128, 128], BF16)
    EB = sbuf(glob, "EB", [64, 128], BF16)
    onesf = sbuf(glob, "onesf", [128, 64], F32)
    onesb = sbuf(glob, "onesb", [128, 128], BF16)
    Bc = Buf("consts")
    tmp_es = ExitStack()
    identf = sbuf(tmp_es, "identf", [128, 128], F32)
    ebf = sbuf(tmp_es, "ebf", [64, 128], F32)

    P.op("pool", lambda e: e.memset(identf[:], 0.0), writes=[Bc])
    P.op("pool", lambda e: e.affine_select(out=identf[:], in_=identf[:], pattern=[[-1, 128]],
                                            compare_op=ALU.not_equal, fill=1.0, base=0,
                                            channel_multiplier=1), reads=[Bc], writes=[Bc])
    P.op("pool", lambda e: e.memset(ebf[:], 0.0), writes=[Bc])
    P.op("pool", lambda e: e.affine_select(out=ebf[:], in_=ebf[:], pattern=[[-1, 128]],
                                            compare_op=ALU.not_equal, fill=1.0, base=64,
                                            channel_multiplier=1), reads=[Bc], writes=[Bc])
    P.op("dve", lambda e: e.tensor_copy(out=ident[:], in_=identf[:]), reads=[Bc], writes=[Bc])
    P.op("dve", lambda e: e.tensor_copy(out=EB[:], in_=ebf[:]), reads=[Bc], writes=[Bc])
    P.op("dve", lambda e: e.memset(onesf[:], 1.0), writes=[Bc])
    P.op("dve", lambda e: e.memset(onesb[:], 1.0), writes=[Bc])
    P.barrier()
    tmp_es.close()
    P.barrier()

    Bws = [Buf("ws%d" % u) for u in range(NUNITS)]
    ws = c["ws"]

    def ws_cast(u, src):
        a, b = src.shape[1], src.shape[2]
        P.dma("pool", ws[u].rearrange("p (a b) -> p a b", a=a), src, writes=[Bws[u]], key="wsc%d" % u)

    def kview(w, c0, c1):
        return w.rearrange("(k p) n -> p k n", p=128)[:, :, c0:c1]

    ws_jobs = []
    if cfg.get("tail", True):
        w_in = c["w_in"]

        def J(fn, *a):
            ws_jobs.append(lambda: fn(*a))

        def br_job(hb):
            dst = ws[U_BR + hb].rearrange("p (a b) -> p a b", a=8)
            wab = c["w_ab"].rearrange("(g i d) n -> g d i n", g=2, i=4)
            for g in range(2):
                P.dma("pool", dst[64 * g:64 * g + 64, 0:4, :], wab[g][:, :, 512 * hb:512 * hb + 512],
                      writes=[Bws[U_BR + hb]], key="wsc%da%d" % (U_BR + hb, g))
            P.dma("pool", dst[:, 4:8, :], kview(c["w_fb"], 512 * hb, 512 * hb + 512),
                  writes=[Bws[U_BR + hb]], key="wsc%db" % (U_BR + hb))

        for hb in range(2):
            J(ws_cast, U_GA + hb, kview(w_in, 1280 + 512 * hb, 1280 + 512 * hb + 512))
            J(ws_cast, U_GF + hb, kview(w_in, 2304 + 512 * hb, 2304 + 512 * hb + 512))
            J(br_job, hb)
            J(ws_cast, U_MIX + hb, kview(c["w_mix"], 512 * hb, 512 * hb + 512))
            J(ws_cast, U_MQ + hb, kview(c["w_mq"], 512 * hb, 512 * hb + 512))
            J(ws_cast, U_MO + hb, kview(c["w_mo"], 512 * hb, 512 * hb + 512))
            J(ws_cast, U_MK + hb, kview(c["w_mk"], 512 * hb, 512 * hb + 512))
            J(ws_cast, U_MV + hb, kview(c["w_mv"], 512 * hb, 512 * hb + 512))
        for j in range(8):
            J(ws_cast, U_UP + j, kview(c["w_up"], 512 * j, 512 * j + 512))
        for hb in range(2):
            for j in range(4):
                src = c["w_dn"].rearrange("(k p) n -> p k n", p=128)[:, 8 * j:8 * j + 8, 512 * hb:512 * hb + 512]
                J(ws_cast, U_DN + 4 * hb + j, src)


    xv = c["xseq"].rearrange("s (p t) d -> s t p d", t=32)
    yv = c["y"].rearrange("s (p t) d -> s t p d", t=32)
    ropev = {0: c["rope_p"].rearrange("(p t) c -> t p c", t=32),
             1: c["rope_s"].rearrange("(p t) c -> t p c", t=32)}
    xfv = c["xfull"].rearrange("(n p t) d -> n t p d", n=4, t=32)
    ropefv = c["rope_f"].rearrange("(n p t) c -> n t p c", n=4, t=32)
    Bdsc = [Buf("dsc%d" % t) for t in range(32)]

    seqs = cfg.get("seqs", [0, 1, 2])
    for s in seqs:
      with ExitStack() as seq_es:
        sample = (s == 2)
        NKT = 128 if sample else 32
        P.barrier()
        OT = sbuf(seq_es, "OT", [128, 4, S], BF16)
        BYT = [Buf("YT%d" % i) for i in range(8)]
        BOT = [Buf("OT%d" % i) for i in range(8)]
        with ExitStack() as qes:
            QT = sbuf(qes, "QT", [128, 4, S], BF16)
            KT = sbuf(qes, "KT", [128, NKT * 128], BF16)
            VP = sbuf(qes, "VP", [128, NKT, 2, 65], BF16)
            BQT = [Buf("QT%d" % t) for t in range(32)]
            BKT = [Buf("KT%d" % t) for t in range(NKT)]
            BVP = [Buf("VP%d" % t) for t in range(NKT)]
            BVPo = Buf("VPones")
            Wf = sbuf(qes, "Wf", [128, 8, 1280], BF16)
            BWf = Buf("Wf")
            for k in range(8):
                P.dma("pool", Wf[:, k, :], c["w_in"][k * 128:(k + 1) * 128, 0:1280], writes=[BWf], key="wf")
            P.op("pool", lambda e: e.memset(VP[:, :, :, 64:65], 1.0), writes=[BVPo])

            with ExitStack() as es:
                NXB = 2 if sample else 4
                HM = 8 if sample else 20
                gqk = sbuf(es, "gqk", [128, 10, 64], F32)
                coef = sbuf(es, "coef", [128, 32, 4, 2], F32)
                for h in range(10):
                    src = c["qn"] if h < 8 else c["kn"]
                    P.dma("sp", gqk[:, h, :], src.partition_broadcast(128), writes=[Bc], key="c_gqk")
                if sample:
                    P.dma("sp", coef[:].rearrange("p a b c -> p (a b c)"), c["coef"], writes=[Bc], key="c_coef")
                xb = [sbuf(es, "xb%d" % i, [128, D], BF16) for i in range(NXB)]
                xT = [sbuf(es, "xT%d" % i, [128, 8, 128], BF16) for i in range(2)]
                sq = sbuf(es, "sq", [128, HM, 64], F32)
                ssum = sbuf(es, "ssum", [128, HM], F32)
                rstd = sbuf(es, "rstd", [128, HM], F32)
                qn_ = sbuf(es, "qn_", [128, HM, 64], F32)
                t1 = sbuf(es, "t1", [128, HM, 64], F32)
                t2 = sbuf(es, "t2", [128, HM, 64], F32)
                qr = sbuf(es, "qr", [128, max(HM * 64, 1024)], BF16)
                if sample:
                    qraw = [sbuf(es, "qraw%d" % i, [128, 1, 8, 64], F32) for i in range(2)]
                    kraw = [sbuf(es, "kraw%d" % i, [128, 4, 2, 64], F32) for i in range(2)]
                    rpq = [sbuf(es, "rpq%d" % i, [128, 1, 128], F32) for i in range(2)]
                    rpk = [sbuf(es, "rpk%d" % i, [128, 4, 128], F32) for i in range(2)]
                    uacc = [sbuf(es, "uacc%d" % i, [128, 2, 512], F32) for i in range(2)]
                else:
                    qraw = [sbuf(es, "qkraw%d" % i, [128, 2, 10, 64], F32) for i in range(2)]
                    rpq = [sbuf(es, "rpT%d" % i, [128, 2, 128], F32) for i in range(2)]
                Ub = [sbuf(es, "Ub%d" % i, [128, 2, 512], BF16) for i in range(2 if sample else 4)]
                e1 = [sbuf(es, "e1_%d" % i, [128, 384], BF16) for i in range(2)]
                tb = [sbuf(es, "tb%d" % i, [128, 2, 512], BF16) for i in range(2)]
                pT = [psum(es, "pT%d" % i, [128, 8, 128], BF16) for i in range(2)]
                pq = psum(es, "pq", [128, 512], F32)
                pkv = psum(es, "pkv", [128, 512], F32)
                pu = psum(es, "pu", [128, 512], F32)
                pqT = psum(es, "pqT", [128, 8, 128], BF16)
                ps1 = [psum(es, "ps1_%d" % i, [128, 512], F32) for i in range(2)]
                Bn = lambda n: Buf(n)
                Bxb = [Bn("xb") for i in range(NXB)]
                BxT = [Bn("xT0"), Bn("xT1")]
                Bsq, Bss, Brs, Bqn, Bt1, Bt2, Bqr = (Bn(x) for x in "sq ss rs qn t1 t2 qr".split())
                Bqraw = [Bn("qraw0"), Bn("qraw1")]
                Bkraw = [Bn("kraw0"), Bn("kraw1")]
                Brpq = [Bn("rpq0"), Bn("rpq1")]
                Brpk = [Bn("rpk0"), Bn("rpk1")]
                Bua2 = [Bn("uacc0"), Bn("uacc1")]
                BUb = [Bn("Ub%d" % i) for i in range(4)]
                Be1 = [Bn("e1_0"), Bn("e1_1")]
                Btb = [Bn("tb0"), Bn("tb1")]
                BpT = [Buf("pT0", True), Buf("pT1", True)]
                Bpq, Bpkv, Bpu, BpqT = Buf("pq", True), Buf("pkv", True), Buf("pu", True), Buf("pqT", True)
                Bps1 = [Buf("ps1_0", True), Buf("ps1_1", True)]
                e1src = c["e1s"] if sample else c["e1p"]
                e1w = 384 if sample else 256
                cnt = {"t": 0}

                def load_tile(xsrc):
                    n = cnt["t"]
                    cnt["t"] += 1
                    ib, i = n % NXB, n % 2
                    P.dma("pool", xb[ib][:], xsrc, writes=[Bxb[ib]], key="xb%d" % ib)
                    for k in range(8):
                        P.op("pe", lambda e, k=k: e.transpose(out=pT[i][:, k, :], in_=xb[ib][:, k * 128:(k + 1) * 128],
                                                              identity=ident[:]),
                             reads=[Bxb[ib], Bc], writes=[BpT[i]])
                    P.op("act", lambda e: e.activation(out=xT[i][:], in_=pT[i][:], func=AF.Copy),
                         reads=[BpT[i]], writes=[BxT[i]])
                    return i

                def project(i, ps_t, Bps, c0, c1):
                    n = c1 - c0
                    for k in range(8):
                        P.op("pe", lambda e, k=k: e.matmul(ps_t[:, 0:n], lhsT=xT[i][:, k, :], rhs=Wf[:, k, c0:c1],
                                                           start=(k == 0), stop=(k == 7)),
                             reads=[BxT[i], BWf], writes=[Bps])

                def qk_chain(raw, Braw, T, H, gain, rope, Brope):
                    N = T * H
                    v4 = lambda ap: ap[:, 0:N, :].rearrange("p (t h) d -> p t h d", t=T)
                    L = []
                    L.append(lambda: P.op("act", lambda e: e.activation(out=v4(sq), in_=raw, func=AF.Square), reads=[Braw], writes=[Bsq]))
                    L.append(lambda: P.op("dve", lambda e: e.tensor_reduce(out=ssum[:, 0:N], in_=sq[:, 0:N, :], axis=AX.X, op=ALU.add),
                                          reads=[Bsq], writes=[Bss]))
                    L.append(lambda: P.op("dve", lambda e: e.tensor_scalar(out=ssum[:, 0:N], in0=ssum[:, 0:N], scalar1=1.0 / 64.0,
                                                                           scalar2=RMS_EPS, op0=ALU.mult, op1=ALU.add),
                                          reads=[Bss], writes=[Bss]))
                    L.append(lambda: P.op("act", lambda e: e.activation(out=rstd[:, 0:N], in_=ssum[:, 0:N], func=AF.Sqrt),
                                          reads=[Bss], writes=[Brs]))
                    L.append(lambda: P.op("dve", lambda e: e.reciprocal(out=rstd[:, 0:N], in_=rstd[:, 0:N]), reads=[Brs], writes=[Brs]))
                    L.append(lambda: P.op("dve", lambda e: e.tensor_tensor(
                        out=qn_[:, 0:N, :], in0=raw.rearrange("p t h d -> p (t h) d"),
                        in1=rstd[:, 0:N].unsqueeze(2).to_broadcast([128, N, 64]), op=ALU.mult),
                        reads=[Braw, Brs], writes=[Bqn]))
                    L.append(lambda: P.op("dve", lambda e: e.tensor_tensor(out=v4(qn_), in0=v4(qn_),
                                                                           in1=gain.unsqueeze(1).to_broadcast([128, T, H, 64]), op=ALU.mult),
                                          reads=[Bqn, Bc], writes=[Bqn]))
                    L.append(lambda: P.op("dve", lambda e: e.tensor_tensor(out=v4(t1), in0=v4(qn_),
                                                                           in1=rope[:, :, 0:64].unsqueeze(2).to_broadcast([128, T, H, 64]), op=ALU.mult),
                                          reads=[Bqn, Brope], writes=[Bt1]))
                    L.append(lambda: P.op("dve", lambda e: e.tensor_tensor(out=v4(t2)[:, :, :, 0:32], in0=v4(qn_)[:, :, :, 32:64],
                                                                           in1=rope[:, :, 64:96].unsqueeze(2).to_broadcast([128, T, H, 32]), op=ALU.mult),
                                          reads=[Bqn, Brope], writes=[Bt2]))
                    L.append(lambda: P.op("dve", lambda e: e.tensor_tensor(out=v4(t2)[:, :, :, 32:64], in0=v4(qn_)[:, :, :, 0:32],
                                                                           in1=rope[:, :, 96:128].unsqueeze(2).to_broadcast([128, T, H, 32]), op=ALU.mult),
                                          reads=[Bqn, Brope], writes=[Bt2]))
                    return L, v4(t1), v4(t2)

                def stage1(ti, srcs):
                    j = ti % 2
                    P.dma("sp", e1[j][:, 0:e1w], e1src[ti], writes=[Be1[j]], key="e1_%d" % j)
                    if len(srcs) == 1:
                        (u0, B0), = srcs
                        P.op("pe", lambda e: e.matmul(ps1[0][:], lhsT=e1[j][:, 0:128], rhs=u0, start=True, stop=True),
                             reads=[Be1[j], B0], writes=[Bps1[0]])
                        P.op("pe", lambda e: e.matmul(ps1[1][:], lhsT=e1[j][:, 128:256], rhs=u0, start=True, stop=True),
                             reads=[Be1[j], B0], writes=[Bps1[1]])
                    else:
                        (u0, B0), (u1, B1) = srcs
                        P.op("pe", lambda e: e.matmul(ps1[0][:], lhsT=e1[j][:, 0:128], rhs=u0, start=True, stop=False),
                             reads=[Be1[j], B0], writes=[Bps1[0]])
                        P.op("pe", lambda e: e.matmul(ps1[0][:], lhsT=e1[j][:, 128:256], rhs=u1, start=False, stop=True),
                             reads=[Be1[j], B1], writes=[Bps1[0]])
                        P.op("pe", lambda e: e.matmul(ps1[1][:], lhsT=e1[j][:, 128:256], rhs=u0, start=True, stop=False),
                             reads=[Be1[j], B0], writes=[Bps1[1]])
                        P.op("pe", lambda e: e.matmul(ps1[1][:], lhsT=e1[j][:, 256:384], rhs=u1, start=False, stop=True),
                             reads=[Be1[j], B1], writes=[Bps1[1]])
                    P.op("act", lambda e: e.activation(out=tb[j][:, 0, :], in_=ps1[0][:], func=AF.Copy),
                         reads=[Bps1[0]], writes=[Btb[j]])
                    P.op("dve", lambda e: e.tensor_copy(out=tb[j][:, 1, :], in_=ps1[1][:]),
                         reads=[Bps1[1]], writes=[Btb[j]])
                    P.dma("sp", c["dsc"][:, ti], tb[j][:], reads=[Btb[j]], writes=[Bdsc[ti]], key="tbw%d" % j)

                def emit_q_add(t1v, t2v, tt, qro, with_k):
                    L = []
                    qv = qr[:, qro:qro + 512]
                    L.append(lambda: P.op("dve", lambda e: e.tensor_tensor(
                        out=qv.rearrange("p (i g d) -> p g i d", i=4, g=2),
                        in0=t1v[:, tt, 0:8, :].rearrange("p (g i) d -> p g i d", g=2),
                        in1=t2v[:, tt, 0:8, :].rearrange("p (g i) d -> p g i d", g=2), op=ALU.add),
                        reads=[Bt1, Bt2], writes=[Bqr]))
                    if with_k:
                        L.append(lambda: P.op("dve", lambda e: e.tensor_tensor(
                            out=qr[:, qro + 512:qro + 640].rearrange("p (h d) -> p h d", d=64),
                            in0=t1v[:, tt, 8:10, :], in1=t2v[:, tt, 8:10, :], op=ALU.add),
                            reads=[Bt1, Bt2], writes=[Bqr]))
                    return L

                def emit_q_pe(qro, tau, with_k):
                    nT = 5 if with_k else 4
                    for a in range(nT):
                        P.op("pe", lambda e, a=a: e.transpose(out=pqT[:, a, :],
                                                              in_=qr[:, qro + a * 128:qro + (a + 1) * 128],
                                                              identity=ident[:]),
                             reads=[Bqr, Bc], writes=[BpqT])
                    P.op("act", lambda e: e.activation(out=QT[:, :, tau * 128:(tau + 1) * 128], in_=pqT[:, 0:4, :],
                                                       func=AF.Copy), reads=[BpqT], writes=[BQT[tau]])
                    if with_k:
                        P.op("act", lambda e: e.activation(out=KT[:, tau * 128:(tau + 1) * 128], in_=pqT[:, 4, :],
                                                           func=AF.Copy), reads=[BpqT], writes=[BKT[tau]])

                def run_sched(iters):
                    prevB1, prevB2 = [], None
                    for (As, mkB1, B2) in iters + [([], None, None)]:
                        nA = max(len(As), 1)
                        per = (len(prevB1) + nA - 1) // nA
                        for ai in range(nA):
                            if ai < len(As):
                                As[ai]()
                            for _ in range(per):
                                if prevB1:
                                    prevB1.pop(0)()
                        while prevB1:
                            prevB1.pop(0)()
                        if prevB2 is not None:
                            prevB2()
                        prevB1 = mkB1() if mkB1 is not None else []
                        prevB2 = B2

                iters = []
                if not sample:
                    for pr in range(NT // 2):
                        par = pr % 2

                        def Atile(tt, pr=pr, par=par):
                            tau = 2 * pr + tt
                            if tt == 0:
                                P.dma("sp", rpq[par][:], ropev[0][2 * pr:2 * pr + 2].rearrange("t p c -> p t c"),
                                      writes=[Brpq[par]], key="rpq%d" % par)
                            i = load_tile(xv[s, tau])
                            project(i, pq, Bpq, 0, 512)
                            P.op("act", lambda e: e.activation(out=qraw[par][:, tt, 0:8, :],
                                                               in_=pq[:].rearrange("p (h d) -> p h d", d=64), func=AF.Copy),
                                 reads=[Bpq], writes=[Bqraw[par]])
                            project(i, pkv, Bpkv, 512, 768)
                            P.op("act", lambda e: e.activation(out=qraw[par][:, tt, 8:10, :],
                                                               in_=pkv[:, 0:128].rearrange("p (h d) -> p h d", d=64), func=AF.Copy),
                                 reads=[Bpkv], writes=[Bqraw[par]])
                            P.op("act", lambda e: e.activation(out=VP[:, tau, :, 0:64],
                                                               in_=pkv[:, 128:256].rearrange("p (g d) -> p g d", g=2),
                                                               func=AF.Copy), reads=[Bpkv], writes=[BVP[tau]])
                            project(i, pu, Bpu, 768, 1280)
                            j = tau % 4
                            P.op("act", lambda e: e.activation(out=Ub[j][:, 0, :], in_=pu[:], func=AF.Copy),
                                 reads=[Bpu], writes=[BUb[j]])

                        def mkB1(pr=pr, par=par):
                            L, t1v, t2v = qk_chain(qraw[par][:], Bqraw[par], 2, 10, gqk[:, 0:10, :], rpq[par][:], Brpq[par])
                            for tt in range(2):
                                L += emit_q_add(t1v, t2v, tt, tt * 640, True)
                            return L

                        def B2(pr=pr):
                            for tt in range(2):
                                tau = 2 * pr + tt
                                stage1(tau, [(Ub[tau % 4][:, 0, :], BUb[tau % 4])])
                            for tt in range(2):
                                emit_q_pe(tt * 640, 2 * pr + tt, True)
                        iters.append(([lambda a=Atile: a(0), lambda a=Atile: a(1)], mkB1, B2))
                else:
                    for tau in range(NT):
                        par = tau % 2

                        def Aown(tau=tau, par=par):
                            P.dma("sp", rpq[par][:, 0, :], ropev[1][tau], writes=[Brpq[par]], key="rpq%d" % par)
                            P.dma("sp", rpk[par][:], ropefv[:, tau].rearrange("n p c -> p n c"),
                                  writes=[Brpk[par]], key="rpk%d" % par)
                            i = load_tile(xv[s, tau])
                            project(i, pq, Bpq, 0, 512)
                            P.op("act", lambda e: e.activation(out=qraw[par][:, 0, :, :],
                                                               in_=pq[:].rearrange("p (h d) -> p h d", d=64), func=AF.Copy),
                                 reads=[Bpq], writes=[Bqraw[par]])

                        def Afull(n1, tau=tau, par=par):
                            ua, Bu = uacc[par], Bua2[par]
                            kt = tau * 4 + n1
                            i2 = load_tile(xfv[n1, tau])
                            project(i2, pkv, Bpkv, 512, 768)
                            P.op("act", lambda e: e.activation(out=kraw[par][:, n1, :, :],
                                                               in_=pkv[:, 0:128].rearrange("p (h d) -> p h d", d=64), func=AF.Copy),
                                 reads=[Bpkv], writes=[Bkraw[par]])
                            P.op("act", lambda e: e.activation(out=VP[:, kt, :, 0:64],
                                                               in_=pkv[:, 128:256].rearrange("p (g d) -> p g d", g=2),
                                                               func=AF.Copy), reads=[Bpkv], writes=[BVP[kt]])
                            project(i2, pu, Bpu, 768, 1280)
                            for ri in range(2):
                                sc = coef[:, tau, n1, ri:ri + 1]
                                if n1 == 0:
                                    P.op("dve", lambda e: e.tensor_scalar_mul(out=ua[:, ri, :], in0=pu[:], scalar1=sc),
                                         reads=[Bpu, Bc], writes=[Bu])
                                else:
                                    P.op("dve", lambda e: e.scalar_tensor_tensor(
                                        out=ua[:, ri, :], in0=pu[:], scalar=sc, in1=ua[:, ri, :],
                                        op0=ALU.mult, op1=ALU.add), reads=[Bpu, Bu, Bc], writes=[Bu])

                        def mkB1(tau=tau, par=par):
                            L, t1v, t2v = qk_chain(qraw[par][:], Bqraw[par], 1, 8, gqk[:, 0:8, :], rpq[par][:], Brpq[par])
                            L += emit_q_add(t1v, t2v, 0, 0, False)
                            L2, k1v, k2v = qk_chain(kraw[par][:], Bkraw[par], 4, 2, gqk[:, 8:10, :], rpk[par][:], Brpk[par])
                            L2.append(lambda: P.op("dve", lambda e: e.tensor_tensor(
                                out=qr[:, 512:1024].rearrange("p (t h d) -> p t h d", t=4, h=2), in0=k1v, in1=k2v, op=ALU.add),
                                reads=[Bt1, Bt2], writes=[Bqr]))
                            return L + L2

                        def B2(tau=tau, par=par):
                            P.op("act", lambda e: e.activation(out=Ub[par][:], in_=uacc[par][:], func=AF.Copy),
                                 reads=[Bua2[par]], writes=[BUb[par]])
                            stage1(tau, [(Ub[par][:, 0, :], BUb[par]), (Ub[par][:, 1, :], BUb[par])])
                            emit_q_pe(0, tau, False)
                            for a in range(4):
                                P.op("pe", lambda e, a=a: e.transpose(out=pqT[:, 4 + a, :], in_=qr[:, 512 + a * 128:512 + (a + 1) * 128],
                                                                      identity=ident[:]),
                                     reads=[Bqr, Bc], writes=[BpqT])
                            P.op("act", lambda e: e.activation(
                                out=KT[:, tau * 512:(tau + 1) * 512].rearrange("p (a q) -> p a q", a=4),
                                in_=pqT[:, 4:8, :], func=AF.Copy), reads=[BpqT], writes=[BKT[4 * tau + a] for a in range(4)])
                        iters.append(([Aown] + [(lambda n1=n1, f=Afull: f(n1)) for n1 in range(4)], mkB1, B2))
                run_sched(iters)

            P.barrier()
            if "d_QT" in dbg:
                P.dma("sp", c["d_QT"], QT[:], reads=BQT, key="d_QT")
                P.dma("sp", c["d_KT"], KT[:, 0:4096], reads=BKT, key="d_KT")
                P.dma("sp", c["d_VP"], VP[:, 0:32], reads=BVP + [BVPo], key="d_VP")

            if cfg.get("attn", True):
                with ExitStack() as es:
                    NPS = 2
                    pS = [psum(es, "pS%d" % i, [128, 2, 512], F32) for i in range(NPS)]
                    pacc = [psum(es, "pacc%d" % g, [128, 512], F32) for g in range(2)]
                    pbc = psum(es, "pbc", [128, 512], F32)
                    ppk = psum(es, "ppk", [128, 512], F32)
                    NPT = 4
                    PTs = [sbuf(es, "PT%d" % i, [128, 2, 512], BF16) for i in range(NPT)]
                    accs = [sbuf(es, "accs%d" % g, [128, 512], F32) for g in range(2)]
                    rs = sbuf(es, "rs", [128, 2, 512], F32)
                    bcs = sbuf(es, "bcs", [64, 512], F32)
                    stg = [sbuf(es, "stg%d" % g, [64, 512], BF16) for g in range(2)]
                    BpS = [Buf("pS", True) for i in range(NPS)]
                    Bpacc = [Buf("pacc", True) for g in range(2)]
                    Bpbc, Bppk = Buf("pbc", True), Buf("ppk", True)
                    Brs_, Bbcs = Buf("rs"), Buf("bcs")
                    BPTs = [Buf("PT") for i in range(NPT)]
                    Baccs = [Buf("accs") for g in range(2)]
                    Bstg = [Buf("stg") for g in range(2)]
                    epi = []
                    for tau in range(NT):
                        qcols = slice(tau * 128, (tau + 1) * 128)
                        if ws_jobs:
                            ws_jobs.pop(0)()

                        def score(kt):
                            i = kt % NPS
                            for g in range(2):
                                P.op("pe", lambda e, g=g: e.matmul(
                                    pS[i][:, g, :].rearrange("p (i q) -> p i q", i=4), lhsT=KT[64 * g:64 * g + 64, kt * 128:(kt + 1) * 128],
                                    rhs=QT[64 * g:64 * g + 64, :, qcols], start=True, stop=True),
                                    reads=[BKT[kt], BQT[tau]], writes=[BpS[i]])

                        def expo(kt):
                            i = kt % NPS
                            j = kt % NPT
                            P.op("act", lambda e: e.activation(out=PTs[j][:], in_=pS[i][:], func=AF.Exp, scale=0.125),
                                 reads=[BpS[i]], writes=[BPTs[j]])

                        def pv(kt):
                            j = kt % NPT
                            for g in range(2):
                                P.op("pe", lambda e, g=g: e.matmul(
                                    pacc[g][0:65, :], lhsT=VP[:, kt, g, :], rhs=PTs[j][:, g, :],
                                    start=(kt == 0), stop=(kt == NKT - 1)),
                                    reads=[BPTs[j], BVP[kt], BVPo], writes=[Bpacc[g]])

                        score(0)
                        score(1)
                        for kt in range(NKT):
                            expo(kt)
                            if kt + 2 < NKT:
                                score(kt + 2)
                            pv(kt)
                            if epi and kt % 2 == 1:
                                epi.pop(0)()
                        while epi:
                            epi.pop(0)()
                        for g in range(2):
                            P.op("dve", lambda e, g=g: e.tensor_copy(out=accs[g][0:65, :], in_=pacc[g][0:65, :]),
                                 reads=[Bpacc[g]], writes=[Baccs[g]])

                        def mk_epi(tau=tau, qcols=qcols):
                            L = []
                            for g in range(2):
                                L.append(lambda g=g: P.op("dve", lambda e: e.reciprocal(out=rs[64:65, g, :], in_=accs[g][64:65, :]),
                                                          reads=[Baccs[g]], writes=[Brs_]))
                                L.append(lambda g=g: P.op("pe", lambda e: e.matmul(pbc[0:64, :], lhsT=onesf[64:65, 0:64], rhs=rs[64:65, g, :],
                                                                                  start=True, stop=True),
                                                          reads=[Brs_, Bc], writes=[Bpbc]))
                                L.append(lambda g=g: P.op("dve", lambda e: e.tensor_copy(out=bcs[:], in_=pbc[0:64, :]),
                                                          reads=[Bpbc], writes=[Bbcs]))
                                L.append(lambda g=g: P.op("dve", lambda e: e.tensor_tensor(out=stg[g][:], in0=accs[g][0:64, :], in1=bcs[:], op=ALU.mult),
                                                          reads=[Baccs[g], Bbcs], writes=[Bstg[g]]))

                            def pack():
                                P.op("pe", lambda e: e.matmul(ppk[:], lhsT=ident[0:64, :], rhs=stg[0][:], start=True, stop=False),
                                     reads=[Bstg[0], Bc], writes=[Bppk])
                                P.op("pe", lambda e: e.matmul(ppk[:], lhsT=EB[:], rhs=stg[1][:], start=False, stop=True),
                                     reads=[Bstg[1], Bc], writes=[Bppk])
                            L.append(pack)
                            L.append(lambda: P.op("dve", lambda e: e.tensor_copy(out=OT[:, :, qcols], in_=ppk[:].rearrange("p (i q) -> p i q", i=4)),
                                                  reads=[Bppk], writes=[BOT[tau // 4]]))
                            return L
                        epi = mk_epi()
                    while epi:
                        epi.pop(0)()
        while ws_jobs:
            ws_jobs.pop(0)()
        P.barrier()

        YT = sbuf(seq_es, "YT", [128, 4, S], BF16)
        if cfg.get("fft2", True):
            with ExitStack() as es:
                cdt = sbuf(es, "cdt", [128, 256], BF16)
                bdt = sbuf(es, "bdt", [128, 384], BF16)
                P.dma("sp", cdt[:], c["cd"], writes=[Bc], key="c_cd")
                P.dma("sp", bdt[:], c["bd"], writes=[Bc], key="c_bd")
                NT2 = 8
                T2 = [sbuf(es, "T2_%d" % i, [128, 2, 512], BF16) for i in range(NT2)]
                Zb = [sbuf(es, "Zb%d" % i, [128, 4, 2, 512], BF16) for i in range(2)]
                pz = [psum(es, "pz%d" % i, [128, 2, 256], F32) for i in range(4)]
                pyc = [psum(es, "pyc%d" % i, [128, 512], F32) for i in range(2)]
                BT2 = [Buf("T2") for i in range(NT2)]
                BZb = [Buf("Zb") for i in range(2)]
                Bpz = [Buf("pz", True) for i in range(4)]
                Bpyc = [Buf("pyc", True) for i in range(2)]
                dscv = c["dsc"].rearrange("(m l) t c h -> m (l t) c h", l=4)
                pzc = 0
                for M in range(8):
                    zb = M % 2
                    for mm in range(4):
                        m = 4 * M + mm
                        i3 = m % NT2
                        P.dma("sp", T2[i3][:], dscv[m], reads=Bdsc, writes=[BT2[i3]], key="t2_%d" % i3)
                        for gp in range(2):
                            pzi = pzc % 4
                            pzc += 1
                            for gg in range(2):
                                g = 2 * gp + gg
                                P.op("pe", lambda e, g=g, gg=gg, pzi=pzi: e.matmul(
                                    pz[pzi][:, gg, :], lhsT=T2[i3][:, 0, g * 128:(g + 1) * 128], rhs=bdt[:, 128:384],
                                    start=True, stop=False), reads=[BT2[i3], Bc], writes=[Bpz[pzi]])
                                P.op("pe", lambda e, g=g, gg=gg, pzi=pzi: e.matmul(
                                    pz[pzi][:, gg, :], lhsT=T2[i3][:, 1, g * 128:(g + 1) * 128], rhs=bdt[:, 0:256],
                                    start=False, stop=True), reads=[BT2[i3], Bc], writes=[Bpz[pzi]])
                            eng = "act" if gp == 0 else "dve"
                            outap = Zb[zb][:, 2 * gp:2 * gp + 2, :, mm * 128:(mm + 1) * 128]
                            inap = pz[pzi][:].rearrange("p g (c t) -> p g c t", c=2)
                            if eng == "act":
                                P.op("act", lambda e, outap=outap, inap=inap: e.activation(out=outap, in_=inap, func=AF.Copy),
                                     reads=[Bpz[pzi]], writes=[BZb[zb]])
                            else:
                                P.op("dve", lambda e, outap=outap, inap=inap: e.tensor_copy(out=outap, in_=inap),
                                     reads=[Bpz[pzi]], writes=[BZb[zb]])
                    for g in range(4):
                        pi = g % 2
                        P.op("pe", lambda e, g=g, pi=pi: e.matmul(pyc[pi][:], lhsT=cdt[:, 0:128], rhs=Zb[zb][:, g, 0, :],
                                                                 start=True, stop=False), reads=[BZb[zb], Bc], writes=[Bpyc[pi]])
                        P.op("pe", lambda e, g=g, pi=pi: e.matmul(pyc[pi][:], lhsT=cdt[:, 128:256], rhs=Zb[zb][:, g, 1, :],
                                                                 start=False, stop=True), reads=[BZb[zb], Bc], writes=[Bpyc[pi]])
                        ytv = YT[:, g, :].rearrange("q (t k r) -> q t k r", t=32, k=32, r=4)
                        t0 = 16 * (M % 2)
                        outap = ytv[:, t0:t0 + 16, :, M // 2]
                        inap = pyc[pi][:].rearrange("p (a k) -> p a k", a=16)
                        if g % 2 == 0:
                            P.op("act", lambda e, outap=outap, inap=inap: e.activation(out=outap, in_=inap, func=AF.Copy),
                                 reads=[Bpyc[pi]], writes=BYT)
                        else:
                            P.op("dve", lambda e, outap=outap, inap=inap: e.tensor_copy(out=outap, in_=inap),
                                 reads=[Bpyc[pi]], writes=BYT)
            P.barrier()

        if "d_YT" in dbg:
            P.dma("sp", c["d_YT"], YT[:], reads=BYT, key="d_YT")
        if "d_OT" in dbg:
            P.dma("sp", c["d_OT"], OT[:], reads=BOT, key="d_OT")

        if cfg.get("tail", True):
            tail_phase(P, c, cfg, s, glob, sbuf, psum, YT, OT, BYT, BOT, Bws, ident, onesb, Bc, xv, yv)
            P.barrier()


def tail_phase(P, c, cfg, s, glob, sbuf, psum, YT, OT, BYT, BOT, Bws, ident, onesb, Bc, xv, yv):
    nc = P.nc
    ws = c["ws"]
    NW = 4
    with ExitStack() as es:
        X = sbuf(es, "X", [128, 4, D], F32)
        xpre = sbuf(es, "xpre", [128, 4, D], BF16)
        xbt = [xpre[:, i, :] for i in range(4)]
        aT = [sbuf(es, "aT%d" % i, [128, 8, 512], BF16) for i in range(2)]
        big = sbuf(es, "big", [128, 32, 512], BF16)
        sg = sbuf(es, "sg", [128, 2, 512], F32)
        LNt = sbuf(es, "LNt", [128, 2, D], F32)
        LNt1 = sbuf(es, "LNt1", [128, 2, D], F32)
        wr = [sbuf(es, "wr%d" % i, [128, 4096], BF16) for i in range(NW)]
        KmT = sbuf(es, "KmT", [128, 8, 256], BF16)
        Vm = sbuf(es, "Vm", [128, 2, D], BF16)
        st = sbuf(es, "st", [128, 4, 2, 6], F32)
        mv = sbuf(es, "mv", [128, 4, 2], F32)
        rstd = sbuf(es, "rstd_t", [128, 4], F32)
        nmr = sbuf(es, "nmr", [128, 4], F32)
        ptr = [psum(es, "ptr%d" % i, [128, 8, 128], BF16) for i in range(2)]
        NB = 6
        pb = [psum(es, "pb%d" % i, [128, 512], F32) for i in range(NB)]
        BX = [Buf("X%d" % t) for t in range(4)]
        Bxpre = [Buf("xpre") for i in range(4)]
        Bxbt = Bxpre
        BaT = [Buf("aT") for i in range(2)]
        Bbig = [Buf("big%d" % i) for i in range(32)]
        Bsg = [Buf("sg") for i in range(2)]
        BLN = Buf("LNt")
        BLN1 = Buf("LNt1")
        Bwr = [Buf("wr") for i in range(NW)]
        BKm, BVm = Buf("KmT"), Buf("Vm")
        Bst = [Buf("st") for t in range(4)]
        Bmv = [Buf("mv") for t in range(4)]
        Brstd = [Buf("rstd") for t in range(4)]
        Bnmr = [Buf("nmr") for t in range(4)]
        Bptr = [Buf("ptr", True) for i in range(2)]
        Bpb = [Buf("pb", True) for i in range(NB)]
        state = {"w": 0, "b": 0, "x": 0, "p": 0}

        def fetch(u):
            sl = state["w"] % NW
            state["w"] += 1
            P.dma("sp", wr[sl][:], ws[u], reads=[Bws[u]], writes=[Bwr[sl]], key="wr%d" % sl)
            return wr[sl][:].rearrange("p (a b) -> p a b", a=8), Bwr[sl]

        def bank():
            i = state["b"] % NB
            state["b"] += 1
            return pb[i], Bpb[i]

        def transp(src, Bsrc, dst, Bdst, c0, n=128):
            pi = state["p"] % 2
            state["p"] += 1
            for k in range(8):
                P.op("pe", lambda e, k=k: e.transpose(out=ptr[pi][:, k, :], in_=src[:, k * 128:(k + 1) * 128], identity=ident[:]),
                     reads=[Bsrc, Bc], writes=[Bptr[pi]])
            P.op("act", lambda e: e.activation(out=dst[:, :, c0:c0 + n], in_=ptr[pi][:, :, 0:n], func=AF.Copy),
                 reads=[Bptr[pi]], writes=[Bdst])

        def load_ln(gname, bname):
            P.dma("pool", LNt[:, 0, :], c[gname].partition_broadcast(128), writes=[BLN], key="lng")
            P.dma("pool", LNt[:, 1, :], c[bname].partition_broadcast(128), writes=[BLN], key="lnb")

        for mc in range(2):
            i = state["x"] % 2
            state["x"] += 1
            P.dma("pool", xbt[i][:], c["mem"][s, mc * 128:(mc + 1) * 128, :], writes=[Bxbt[i]], key="memb%d" % i)
            transp(xbt[i], Bxbt[i], aT[0], BaT[0], mc * 128)
        for hb in range(2):
            wk, Bwk = fetch(U_MK + hb)
            for fl in range(4):
                p_, Bp_ = bank()
                for k in range(8):
                    P.op("pe", lambda e, k=k: e.matmul(p_[:, 0:256], lhsT=wk[:, k, fl * 128:(fl + 1) * 128], rhs=aT[0][:, k, 0:256],
                                                       start=(k == 0), stop=(k == 7)), reads=[Bwk, BaT[0]], writes=[Bp_])
                P.op("act", lambda e: e.activation(out=KmT[:, 4 * hb + fl, :], in_=p_[:, 0:256], func=AF.Copy),
                     reads=[Bp_], writes=[BKm])
        for hb in range(2):
            wv, Bwv = fetch(U_MV + hb)
            for mc in range(2):
                p_, Bp_ = bank()
                for k in range(8):
                    P.op("pe", lambda e, k=k: e.matmul(p_[:], lhsT=aT[0][:, k, mc * 128:(mc + 1) * 128], rhs=wv[:, k, :],
                                                       start=(k == 0), stop=(k == 7)), reads=[Bwv, BaT[0]], writes=[Bp_])
                P.op("dve", lambda e: e.tensor_copy(out=Vm[:, mc, hb * 512:(hb + 1) * 512], in_=p_[:]),
                     reads=[Bp_], writes=[BVm])

        def layer_norm(t, dstT, BdstT, G, last, tab=None, Btab=None):
            tab = LNt if tab is None else tab
            Btab = BLN if Btab is None else Btab
            xt = X[:, t, :]
            for a in range(2):
                P.op("dve", lambda e, a=a: e.bn_stats(out=st[:, t, a, :], in_=X[:, t, a * 512:(a + 1) * 512]),
                     reads=[BX[t]], writes=[Bst[t]])
            P.op("dve", lambda e: e.bn_aggr(out=mv[:, t, :], in_=st[:, t, :, :].rearrange("p a b -> p (a b)")),
                 reads=[Bst[t]], writes=[Bmv[t]])
            P.op("dve", lambda e: e.tensor_scalar_add(out=rstd[:, t:t + 1], in0=mv[:, t, 1:2], scalar1=LN_EPS),
                 reads=[Bmv[t]], writes=[Brstd[t]])
            P.op("act", lambda e: e.activation(out=rstd[:, t:t + 1], in_=rstd[:, t:t + 1], func=AF.Sqrt),
                 reads=[Brstd[t]], writes=[Brstd[t]])
            P.op("dve", lambda e: e.scalar_tensor_tensor(out=xt, in0=xt, scalar=mv[:, t, 0:1], in1=tab[:, 0, :],
                                                         op0=ALU.subtract, op1=ALU.mult),
                 reads=[BX[t], Bmv[t], Btab], writes=[BX[t]])
            P.op("dve", lambda e: e.reciprocal(out=rstd[:, t:t + 1], in_=rstd[:, t:t + 1]), reads=[Brstd[t]], writes=[Brstd[t]])
            P.op("dve", lambda e: e.scalar_tensor_tensor(out=xt, in0=xt, scalar=rstd[:, t:t + 1], in1=tab[:, 1, :],
                                                         op0=ALU.mult, op1=ALU.add),
                 reads=[BX[t], Brstd[t], Btab], writes=[BX[t]])
            if not last:
                i = state["x"] % 4
                state["x"] += 1
                P.op("act", lambda e: e.activation(out=xbt[i][:], in_=xt, func=AF.Copy), reads=[BX[t]], writes=[Bxbt[i]])
                return lambda: transp(xbt[i], Bxbt[i], dstT, BdstT, t * 128)
            else:
                P.dma("pool", yv[s, 4 * G + t], xt, reads=[BX[t]], key="yst%d" % t)
                return None

        def residual(t, hb, p_, Bp_):
            xs = X[:, t, hb * 512:(hb + 1) * 512]
            P.op("dve", lambda e: e.scalar_tensor_tensor(out=xs, in0=xs, scalar=ALPHA, in1=p_[:], op0=ALU.mult, op1=ALU.add),
                 reads=[BX[t], Bp_], writes=[BX[t]])

        def t0_loads(G):
            for t in range(4):
                P.dma("pool", xpre[:, t, :], xv[s, 4 * G + t], writes=[Bxpre[t]], key="xpre%d" % t)

        def t0_transposes(G):
            a = G % 2
            for t in range(4):
                transp(xpre[:, t, :], Bxpre[t], aT[a], BaT[a], t * 128)

        mg = sbuf(es, "mg", [128, 8, 512], BF16)
        Bmg = [Buf("mg%d" % i) for i in range(8)]
        SA, BSA = aT[0], BaT[0]
        SB, BSB = aT[1], BaT[1]

        def prefetch_T(G):
            for t in range(4):
                transp(xpre[:, t, :], Bxpre[t], SA, BSA, t * 128)

        def T1_half(G, hb, fls=(0, 1, 2, 3)):
            gc = slice(512 * G, 512 * G + 512)
            wga, Bga = fetch(U_GA + hb)
            wgf, Bgf = fetch(U_GF + hb)
            wbr, Bbr = fetch(U_BR + hb)
            for fl in fls:
                fc = 4 * hb + fl
                fs = slice(fl * 128, (fl + 1) * 128)
                pga, Bpga = bank()
                for k in range(8):
                    P.op("pe", lambda e, k=k: e.matmul(pga[:], lhsT=wga[:, k, fs], rhs=SA[:, k, :], start=(k == 0), stop=(k == 7)),
                         reads=[Bga, BSA], writes=[Bpga])
                pgf, Bpgf = bank()
                for k in range(8):
                    P.op("pe", lambda e, k=k: e.matmul(pgf[:], lhsT=wgf[:, k, fs], rhs=SA[:, k, :], start=(k == 0), stop=(k == 7)),
                         reads=[Bgf, BSA], writes=[Bpgf])
                pya, Bpya = bank()
                for a in range(4):
                    P.op("pe", lambda e, a=a: e.matmul(pya[:], lhsT=wbr[:, a, fs], rhs=OT[:, a, gc], start=(a == 0), stop=(a == 3)),
                         reads=[Bbr, BOT[G]], writes=[Bpya])
                pyf, Bpyf = bank()
                for a in range(4):
                    P.op("pe", lambda e, a=a: e.matmul(pyf[:], lhsT=wbr[:, 4 + a, fs], rhs=YT[:, a, gc], start=(a == 0), stop=(a == 3)),
                         reads=[Bbr, BYT[G]], writes=[Bpyf])
                P.op("act", lambda e: e.activation(out=sg[:, 0, :], in_=pga[:], func=AF.Sigmoid), reads=[Bpga], writes=[Bsg[0]])
                P.op("act", lambda e: e.activation(out=sg[:, 1, :], in_=pgf[:], func=AF.Sigmoid), reads=[Bpgf], writes=[Bsg[1]])
                P.op("dve", lambda e: e.tensor_tensor(out=sg[:, 0, :], in0=pya[:], in1=sg[:, 0, :], op=ALU.mult),
                     reads=[Bpya, Bsg[0]], writes=[Bsg[0]])
                P.op("dve", lambda e: e.tensor_tensor(out=sg[:, 1, :], in0=pyf[:], in1=sg[:, 1, :], op=ALU.mult),
                     reads=[Bpyf, Bsg[1]], writes=[Bsg[1]])
                P.op("pool", lambda e: e.tensor_tensor(out=mg[:, fc, :], in0=sg[:, 0, :], in1=sg[:, 1, :], op=ALU.add),
                     reads=[Bsg[0], Bsg[1]], writes=[Bmg[fc]])

        groups = cfg.get("groups", list(range(8)))
        P.dma("pool", LNt1[:, 0, :], c["ln1_g"].partition_broadcast(128), writes=[BLN1], key="ln1g")
        P.dma("pool", LNt1[:, 1, :], c["ln1_b"].partition_broadcast(128), writes=[BLN1], key="ln1b")
        load_ln("ln2_g", "ln2_b")
        t0_loads(groups[0])
        prefetch_T(groups[0])
        T1_half(groups[0], 0)
        T1_half(groups[0], 1)
        if len(groups) > 1:
            t0_loads(groups[1])
            prefetch_T(groups[1])
        ln3_pend = []
        for gi, G in enumerate(groups):
            Gn = groups[gi + 1] if gi + 1 < len(groups) else None
            Gnn = groups[gi + 2] if gi + 2 < len(groups) else None
            if gi == 0:
                for t in range(4):
                    P.dma("pool", X[:, t, :], xv[s, 4 * G + t], writes=[BX[t]], key="xld%d" % t)
            wm = [fetch(U_MIX + hb) for hb in range(2)]
            pend = []
            for t in range(4):
                for _ in range(3 if t == 0 else 1):
                    if ln3_pend:
                        ln3_pend.pop(0)()
                for hb in range(2):
                    p_, Bp_ = bank()
                    for k in range(8):
                        P.op("pe", lambda e, k=k: e.matmul(p_[:], lhsT=mg[:, k, t * 128:(t + 1) * 128], rhs=wm[hb][0][:, k, :],
                                                           start=(k == 0), stop=(k == 7)), reads=[Bmg[k], wm[hb][1]], writes=[Bp_])
                    residual(t, hb, p_, Bp_)
                if len(pend) >= 2:
                    pend.pop(0)()
                pend.append(layer_norm(t, SB, BSB, G, False, LNt1, BLN1))
            if Gn is not None:
                T1_half(Gn, 0)
            while pend:
                pend.pop(0)()
            for hb in range(2):
                wq, Bwq = fetch(U_MQ + hb)
                for fl in range(4):
                    qc = 4 * hb + fl
                    p_, Bp_ = bank()
                    for k in range(8):
                        P.op("pe", lambda e, k=k: e.matmul(p_[:], lhsT=wq[:, k, fl * 128:(fl + 1) * 128], rhs=SB[:, k, :],
                                                           start=(k == 0), stop=(k == 7)), reads=[Bwq, BSB], writes=[Bp_])
                    P.op("act", lambda e, qc=qc: e.activation(out=big[:, qc, :], in_=p_[:], func=AF.Copy),
                         reads=[Bp_], writes=[Bbig[qc]])
            def xa_scores(h):
                for mc in range(2):
                    p_, Bp_ = bank()
                    for dc in range(2):
                        P.op("pe", lambda e, dc=dc: e.matmul(p_[:], lhsT=KmT[:, 2 * h + dc, mc * 128:(mc + 1) * 128], rhs=big[:, 2 * h + dc, :],
                                                             start=(dc == 0), stop=(dc == 1)), reads=[BKm, Bbig[2 * h + dc]], writes=[Bp_])
                    P.op("act", lambda e: e.activation(out=big[:, 8 + 2 * h + mc, :], in_=p_[:], func=AF.Exp, scale=1.0 / 16.0),
                         reads=[Bp_], writes=[Bbig[8 + 2 * h + mc]])

            xa_scores(0)
            for h in range(4):
                if h + 1 < 4:
                    xa_scores(h + 1)
                psm, Bpsm = bank()
                for mc in range(2):
                    P.op("pe", lambda e, mc=mc: e.matmul(psm[:], lhsT=onesb[:], rhs=big[:, 8 + 2 * h + mc, :], start=(mc == 0), stop=(mc == 1)),
                         reads=[Bc, Bbig[8 + 2 * h + mc]], writes=[Bpsm])
                r = h % 2
                P.op("dve", lambda e: e.reciprocal(out=sg[:, r, :], in_=psm[:]), reads=[Bpsm], writes=[Bsg[r]])
                for dc in range(2):
                    p_, Bp_ = bank()
                    for mc in range(2):
                        P.op("pe", lambda e, mc=mc: e.matmul(p_[:], lhsT=Vm[:, mc, (2 * h + dc) * 128:(2 * h + dc + 1) * 128],
                                                             rhs=big[:, 8 + 2 * h + mc, :], start=(mc == 0), stop=(mc == 1)),
                             reads=[BVm, Bbig[8 + 2 * h + mc]], writes=[Bp_])
                    P.op("dve", lambda e: e.tensor_tensor(out=big[:, 16 + 2 * h + dc, :], in0=p_[:], in1=sg[:, r, :], op=ALU.mult),
                         reads=[Bp_, Bsg[r]], writes=[Bbig[16 + 2 * h + dc]])
            wo = [fetch(U_MO + hb) for hb in range(2)]
            pend = []
            for t in range(4):
                for hb in range(2):
                    p_, Bp_ = bank()
                    for k in range(8):
                        P.op("pe", lambda e, k=k: e.matmul(p_[:], lhsT=big[:, 16 + k, t * 128:(t + 1) * 128], rhs=wo[hb][0][:, k, :],
                                                           start=(k == 0), stop=(k == 7)), reads=[Bbig[16 + k], wo[hb][1]], writes=[Bp_])
                    residual(t, hb, p_, Bp_)
                if len(pend) >= 2:
                    pend.pop(0)()
                pend.append(layer_norm(t, SB, BSB, G, False))
            if Gn is not None:
                T1_half(Gn, 1, (0, 1))
            while pend:
                pend.pop(0)()
            load_ln("ln3_g", "ln3_b")
            if Gnn is not None:
                t0_loads(Gnn)
            for j in range(8):
                wu, Bwu = fetch(U_UP + j)
                for fl in range(4):
                    fc = 4 * j + fl
                    p_, Bp_ = bank()
                    for k in range(8):
                        P.op("pe", lambda e, k=k: e.matmul(p_[:], lhsT=wu[:, k, fl * 128:(fl + 1) * 128], rhs=SB[:, k, :],
                                                           start=(k == 0), stop=(k == 7)), reads=[Bwu, BSB], writes=[Bp_])
                    m = fl % 2
                    P.op("act", lambda e: e.activation(out=sg[:, m, :], in_=p_[:], func=AF.Relu), reads=[Bp_], writes=[Bsg[m]])
                    P.op("pool", lambda e, fc=fc: e.tensor_tensor(out=big[:, fc, :], in0=sg[:, m, :], in1=sg[:, m, :], op=ALU.mult),
                         reads=[Bsg[m]], writes=[Bbig[fc]])
            for hb in range(2):
                accs = [bank() for t in range(4)]
                for j in range(4):
                    wd, Bwd = fetch(U_DN + 4 * hb + j)
                    for fcl in range(8):
                        fc = 8 * j + fcl
                        for t in range(4):
                            p_, Bp_ = accs[t]
                            P.op("pe", lambda e, t=t: e.matmul(p_[:], lhsT=big[:, fc, t * 128:(t + 1) * 128], rhs=wd[:, fcl, :],
                                                               start=(fc == 0), stop=(fc == 31)), reads=[Bbig[fc], Bwd], writes=[Bp_])
                for t in range(4):
                    residual(t, hb, accs[t][0], accs[t][1])
            if Gn is not None:
                T1_half(Gn, 1, (2, 3))
            if Gnn is not None:
                prefetch_T(Gnn)
            def ln3_thunk(t, G=G, Gn=Gn):
                layer_norm(t, None, None, G, True)
                if Gn is not None:
                    P.dma("pool", X[:, t, :], xv[s, 4 * Gn + t], writes=[BX[t]], key="xld%d" % t)
                if t == 3:
                    load_ln("ln2_g", "ln2_b")
            if Gn is None:
                for t in range(4):
                    ln3_thunk(t)
            else:
                ln3_pend = [(lambda t=t, f=ln3_thunk: f(t)) for t in range(4)]


_CACHE = {}


def _build_program(cfg_key="full", cfg=None):
    if cfg_key in _CACHE:
        return _CACHE[cfg_key]
    cfg = cfg or {}
    nc1, c1 = make_nc(cfg)
    p1 = Prog(nc1, None)
    build(p1, c1, cfg)
    p1.finish()
    nc2, c2 = make_nc(cfg)
    p2 = Prog(nc2, p1.needed)
    build(p2, c2, cfg)
    p2.finish()
    _CACHE[cfg_key] = (nc2, p2)
    return nc2, p2


def _const_tables():
    return {
        "rope_p": _rope_table(np.arange(S)),
        "rope_f": _rope_table(np.arange(SF)),
        "e1p": _e1_table(S, False),
        "e1s": _e1_table(SF, True),
        "bd": _bd_table(),
        "cd": _cd_table(),
    }


def make_in_maps(inputs, cores=range(8)):
    f = lambda a: np.ascontiguousarray(np.asarray(a, dtype=np.float32))
    xp = f(inputs["x_prompt"])
    xs = f(inputs["x_sample"])
    mp = f(inputs["mem_prompt"])
    ms = f(inputs["mem_sample"])
    tabs = _const_tables()
    shared = {
        "w_in": f(inputs["w_in"][0]), "w_ab": f(inputs["w_attn_branch"][0]), "w_fb": f(inputs["w_fourier_branch"][0]),
        "w_mix": f(inputs["w_mix_out"][0]), "w_mq": f(inputs["w_mem_q"][0]), "w_mk": f(inputs["w_mem_k"][0]),
        "w_mv": f(inputs["w_mem_v"][0]), "w_mo": f(inputs["w_mem_o"][0]), "w_up": f(inputs["w_up"][0]),
        "w_dn": f(inputs["w_down"][0]), "qn": f(inputs["q_norm"]).reshape(1, 64), "kn": f(inputs["k_norm"]).reshape(1, 64),
    }
    for v in VNAMES:
        shared[v] = f(inputs[v]).reshape(1, D)
    shared.update(tabs)
    maps = []
    for cid in cores:
        b, cq = cid // 4, cid % 4
        m = dict(shared)
        m["xseq"] = np.ascontiguousarray(np.stack([xp[2 * cid], xp[2 * cid + 1], xs[b, cq::4]], axis=0))
        m["xfull"] = xs[b]
        m["mem"] = np.ascontiguousarray(np.stack([mp[2 * cid], mp[2 * cid + 1], ms[b]], axis=0))
        m["rope_s"] = _rope_table(cq + 4 * np.arange(S))
        m["coef"] = _coef_table(cq).reshape(128, 256)
        maps.append(m)
    return maps


def kernel(**inputs):
    nc, _ = _build_program("full", {})
    maps = make_in_maps(inputs)
    res = run_bass_kernel_spmd(nc, maps, core_ids=list(range(8)))
    yp = np.empty((16, S, D), np.float32)
    ys = np.empty((2, SF, D), np.float32)
    for cid in range(8):
        y = np.asarray(res.results[cid]["y"], dtype=np.float32)
        b, cq = cid // 4, cid % 4
        yp[2 * cid] = y[0]
        yp[2 * cid + 1] = y[1]
        ys[b, cq::4] = y[2]
    return yp, ys
```

```python
import numpy as np
import ml_dtypes
from contextlib import ExitStack
import concourse.bass as bass
import concourse.mybir as mybir
from concourse.bass_utils import run_bass_kernel_spmd

F32 = mybir.dt.float32
BF16 = mybir.dt.bfloat16
AF = mybir.ActivationFunctionType
ALU = mybir.AluOpType
AX = mybir.AxisListType

D = 1024
S = 4096
SF = 16384
NT = 32
ALPHA = 2.0 ** 0.25
RMS_EPS = 1e-6
LN_EPS = 1e-5


class Buf:
    __slots__ = ("name", "lw", "rd", "psum")

    def __init__(self, name, psum=False):
        self.name = name
        self.lw = None
        self.rd = []
        self.psum = psum


class Prog:
    ENG = ("pe", "act", "dve", "pool", "sp")

    def __init__(self, nc, needed=None):
        self.nc = nc
        self.dry = needed is None
        self.needed = set() if needed is None else needed
        self.e = {"pe": nc.tensor, "act": nc.scalar, "dve": nc.vector,
                  "pool": nc.gpsimd, "sp": nc.sync}
        self.nops = {k: 0 for k in self.ENG}
        self.sig = {k: 0 for k in self.ENG}
        self.waited = {k: {} for k in self.ENG}
        self.pending = {k: [] for k in self.ENG}
        self.sem = {}
        self.dmacnt = {}
        self.last_tok = {k: None for k in self.ENG}
        self.last_dma = {}
        self.ninst = 0

    def _sem(self, key):
        if key not in self.sem:
            cm = self.nc.semaphore("s_" + str(len(self.sem)))
            self.sem[key] = cm.__enter__()
        return self.sem[key]

    def _deps(self, eng, reads, writes):
        deps = list(self.pending[eng])
        self.pending[eng] = []
        for b in reads:
            if b.lw is not None:
                deps.append(b.lw)
            if b.psum:
                deps.extend(t for t in b.rd if t[1] != eng)
        for b in writes:
            if b.lw is not None:
                deps.append(b.lw)
            deps.extend(b.rd)
        return deps

    def _emit_waits(self, eng, deps):
        best = {}
        for t in deps:
            if t[0] == "eng":
                src, idx, val = t[1], t[2], t[3]
                if src == eng and eng == "pe":
                    continue
                key = src
            else:
                idx, val = t[2], t[3]
                key = ("dma", t[1])
            if self.waited[eng].get(key, -1) >= idx:
                continue
            if key not in best or best[key][0] < idx:
                best[key] = (idx, val, t)
        for key, (idx, val, t) in best.items():
            self.waited[eng][key] = idx
            if self.dry:
                if t[0] == "eng":
                    self.needed.add((t[1], t[2]))
            else:
                assert val is not None, ("dep not signalled", t)
                self.e[eng].wait_ge(self._sem(key), val)
                self.ninst += 1

    def _update(self, tok, reads, writes):
        for b in reads:
            b.rd.append(tok)
        for b in writes:
            b.lw = tok
            b.rd = []

    def op(self, eng, fn, reads=(), writes=()):
        deps = self._deps(eng, reads, writes)
        self._emit_waits(eng, deps)
        idx = self.nops[eng]
        self.nops[eng] += 1
        val = None
        if not self.dry:
            ins = fn(self.e[eng])
            self.ninst += 1
            if (eng, idx) in self.needed:
                self.sig[eng] += 1
                val = self.sig[eng]
                ins.then_inc(self._sem(eng), 1)
        tok = ("eng", eng, idx, val)
        self.last_tok[eng] = tok
        self._update(tok, reads, writes)
        return tok

    def dma(self, q, out, in_, reads=(), writes=(), key=None, **kw):
        deps = self._deps(q, reads, writes)
        self._emit_waits(q, deps)
        n = self.dmacnt.get(key, 0) + 1
        self.dmacnt[key] = n
        if not self.dry:
            self.e[q].dma_start(out=out, in_=in_, **kw).then_inc(self._sem(("dma", key)), 16)
            self.ninst += 1
        tok = ("dma", key, n, 16 * n)
        self.last_dma[key] = tok
        self._update(tok, reads, writes)
        return tok

    def barrier(self):
        toks = [t for t in self.last_tok.values() if t is not None]
        toks.extend(self.last_dma.values())
        for k in self.ENG:
            self.pending[k] = list(toks)

    def finish(self):
        self.barrier()
        for k in self.ENG:
            deps = self.pending[k]
            self.pending[k] = []
            self._emit_waits(k, deps)


def _bf(a):
    return np.ascontiguousarray(a.astype(np.float32)).astype(ml_dtypes.bfloat16)


def _rope_table(pos):
    pos = np.asarray(pos)
    row = (pos // 64).astype(np.float32)
    col = (pos % 64).astype(np.float32)
    freqs = (np.float32(10000.0) ** (-np.arange(0, 32, 2, dtype=np.float32) / np.float32(32))).astype(np.float32)
    ang = np.concatenate([row[:, None] * freqs, col[:, None] * freqs], axis=-1).astype(np.float32)
    ang = np.concatenate([ang, ang], axis=-1)
    c = np.cos(ang).astype(np.float32)
    s = np.sin(ang).astype(np.float32)
    s2 = np.concatenate([-s[:, :32], s[:, 32:]], axis=-1)
    return np.ascontiguousarray(np.concatenate([c, s2], axis=-1).astype(np.float32))


def _e1_table(s_true, complex_in):
    tau = np.arange(32)[:, None, None]
    p = np.arange(128)[None, :, None]
    ka = np.arange(128)[None, None, :]
    n = 32 * p + tau
    ang = 2.0 * np.pi * ((ka * n) % 4096) / 4096.0
    nrm = 1.0 / np.sqrt(float(s_true) * 128.0)
    ec = np.cos(ang) * nrm
    es = np.sin(ang) * nrm
    parts = [ec, es] + ([-ec] if complex_in else [])
    return _bf(np.concatenate(parts, axis=-1))


def _bd_table():
    l = np.arange(4)[:, None, None, None]
    tau = np.arange(32)[None, :, None, None]
    l2 = np.arange(4)[None, None, :, None]
    kb = np.arange(32)[None, None, None, :]
    ang = 2.0 * np.pi * ((tau * kb) % 32) / 32.0
    dl = (l == l2).astype(np.float64)
    bc = (dl * np.cos(ang)).reshape(128, 128)
    bs = (dl * np.sin(ang)).reshape(128, 128)
    return _bf(np.concatenate([-bs, bc, bs], axis=-1))


def _cd_table():
    j = np.arange(128)[:, None]
    j2 = np.arange(128)[None, :]
    ang = 2.0 * np.pi * ((j * j2) % 128) / 128.0
    return _bf(np.concatenate([np.cos(ang), -np.sin(ang)], axis=-1))


def _coef_table(cq):
    p = np.arange(128)[:, None, None]
    tau = np.arange(32)[None, :, None]
    n1 = np.arange(4)[None, None, :]
    n = 4096 * n1 + 32 * p + tau
    ang = 2.0 * np.pi * ((cq * n) % 16384) / 16384.0
    return np.ascontiguousarray(np.stack([np.cos(ang), -np.sin(ang)], axis=-1).astype(np.float32))


WNAMES = ["w_in", "w_ab", "w_fb", "w_mix", "w_mq", "w_mk", "w_mv", "w_mo", "w_up", "w_dn"]
WSHAPES = {"w_in": [1024, 3328], "w_ab": [512, 1024], "w_fb": [512, 1024], "w_mix": [1024, 1024],
           "w_mq": [1024, 1024], "w_mk": [1024, 1024], "w_mv": [1024, 1024], "w_mo": [1024, 1024],
           "w_up": [1024, 4096], "w_dn": [4096, 1024]}
VNAMES = ["ln1_g", "ln1_b", "ln2_g", "ln2_b", "ln3_g", "ln3_b"]

U_GA, U_GF, U_BR, U_MIX, U_MQ, U_MO, U_UP, U_DN, U_MK, U_MV = 0, 2, 4, 6, 8, 10, 12, 20, 28, 30
NUNITS = 32


def make_nc(cfg):
    nc = bass.Bass("TRN2", target_bir_lowering=False)
    c = {}

    def inp(name, shape, dt=F32):
        c[name] = nc.dram_tensor(name, shape, dt, kind="ExternalInput").ap()

    inp("xseq", [3, S, D])
    inp("xfull", [SF, D])
    inp("mem", [3, 256, D])
    for w in WNAMES:
        inp(w, WSHAPES[w])
    inp("qn", [1, 64])
    inp("kn", [1, 64])
    for v in VNAMES:
        inp(v, [1, D])
    inp("rope_p", [S, 128])
    inp("rope_s", [S, 128])
    inp("rope_f", [SF, 128])
    inp("e1p", [32, 128, 256], BF16)
    inp("e1s", [32, 128, 384], BF16)
    inp("bd", [128, 384], BF16)
    inp("cd", [128, 256], BF16)
    inp("coef", [128, 32 * 4 * 2])
    c["y"] = nc.dram_tensor("y", [3, S, D], F32, kind="ExternalOutput").ap()
    c["ws"] = nc.dram_tensor("ws", [NUNITS, 128, 4096], BF16).ap()
    c["dsc"] = nc.dram_tensor("dsc", [128, 32, 2, 512], BF16).ap()
    for name, shape, dt in cfg.get("dbg", []):
        c[name] = nc.dram_tensor(name, shape, dt, kind="ExternalOutput").ap()
    return nc, c


def build(P, c, cfg):
    nc = P.nc
    glob = ExitStack()

    uniq = {"n": 0}

    def sbuf(es, name, shape, dt):
        uniq["n"] += 1
        return es.enter_context(nc.sbuf_tensor("sb%d_%s" % (uniq["n"], name), shape, dt))

    def psum(es, name, shape, dt):
        uniq["n"] += 1
        return es.enter_context(nc.psum_tensor("ps%d_%s" % (uniq["n"], name), shape, dt))

    dbg = {n for n, _, _ in cfg.get("dbg", [])}

    ident = sbuf(glob, "ident", [128, 128], BF16)
    EB = sbuf(glob, "EB", [64, 128], BF16)
    onesf = sbuf(glob, "onesf", [128, 64], F32)
    onesb = sbuf(glob, "onesb", [128, 128], BF16)
    Bc = Buf("consts")
    tmp_es = ExitStack()
    identf = sbuf(tmp_es, "identf", [128, 128], F32)
    ebf = sbuf(tmp_es, "ebf", [64, 128], F32)

    P.op("pool", lambda e: e.memset(identf[:], 0.0), writes=[Bc])
    P.op("pool", lambda e: e.affine_select(out=identf[:], in_=identf[:], pattern=[[-1, 128]],
                                            compare_op=ALU.not_equal, fill=1.0, base=0,
                                            channel_multiplier=1), reads=[Bc], writes=[Bc])
    P.op("pool", lambda e: e.memset(ebf[:], 0.0), writes=[Bc])
    P.op("pool", lambda e: e.affine_select(out=ebf[:], in_=ebf[:], pattern=[[-1, 128]],
                                            compare_op=ALU.not_equal, fill=1.0, base=64,
                                            channel_multiplier=1), reads=[Bc], writes=[Bc])
    P.op("dve", lambda e: e.tensor_copy(out=ident[:], in_=identf[:]), reads=[Bc], writes=[Bc])
    P.op("dve", lambda e: e.tensor_copy(out=EB[:], in_=ebf[:]), reads=[Bc], writes=[Bc])
    P.op("dve", lambda e: e.memset(onesf[:], 1.0), writes=[Bc])
    P.op("dve", lambda e: e.memset(onesb[:], 1.0), writes=[Bc])
    P.barrier()
    tmp_es.close()
    P.barrier()

    Bws = [Buf("ws%d" % u) for u in range(NUNITS)]
    ws = c["ws"]

    def ws_cast(u, src):
        a, b = src.shape[1], src.shape[2]
        P.dma("pool", ws[u].rearrange("p (a b) -> p a b", a=a), src, writes=[Bws[u]], key="wsc%d" % u)

    def kview(w, c0, c1):
        return w.rearrange("(k p) n -> p k n", p=128)[:, :, c0:c1]

    ws_jobs = []
    if cfg.get("tail", True):
        w_in = c["w_in"]

        def J(fn, *a):
            ws_jobs.append(lambda: fn(*a))

        def br_job(hb):
            dst = ws[U_BR + hb].rearrange("p (a b) -> p a b", a=8)
            wab = c["w_ab"].rearrange("(g i d) n -> g d i n", g=2, i=4)
            for g in range(2):
                P.dma("pool", dst[64 * g:64 * g + 64, 0:4, :], wab[g][:, :, 512 * hb:512 * hb + 512],
                      writes=[Bws[U_BR + hb]], key="wsc%da%d" % (U_BR + hb, g))
            P.dma("pool", dst[:, 4:8, :], kview(c["w_fb"], 512 * hb, 512 * hb + 512),
                  writes=[Bws[U_BR + hb]], key="wsc%db" % (U_BR + hb))

        for hb in range(2):
            J(ws_cast, U_GA + hb, kview(w_in, 1280 + 512 * hb, 1280 + 512 * hb + 512))
            J(ws_cast, U_GF + hb, kview(w_in, 2304 + 512 * hb, 2304 + 512 * hb + 512))
            J(br_job, hb)
            J(ws_cast, U_MIX + hb, kview(c["w_mix"], 512 * hb, 512 * hb + 512))
            J(ws_cast, U_MQ + hb, kview(c["w_mq"], 512 * hb, 512 * hb + 512))
            J(ws_cast, U_MO + hb, kview(c["w_mo"], 512 * hb, 512 * hb + 512))
            J(ws_cast, U_MK + hb, kview(c["w_mk"], 512 * hb, 512 * hb + 512))
            J(ws_cast, U_MV + hb, kview(c["w_mv"], 512 * hb, 512 * hb + 512))
        for j in range(8):
            J(ws_cast, U_UP + j, kview(c["w_up"], 512 * j, 512 * j + 512))
        for hb in range(2):
            for j in range(4):
                src = c["w_dn"].rearrange("(k p) n -> p k n", p=128)[:, 8 * j:8 * j + 8, 512 * hb:512 * hb + 512]
                J(ws_cast, U_DN + 4 * hb + j, src)


    xv = c["xseq"].rearrange("s (p t) d -> s t p d", t=32)
    yv = c["y"].rearrange("s (p t) d -> s t p d", t=32)
    ropev = {0: c["rope_p"].rearrange("(p t) c -> t p c", t=32),
             1: c["rope_s"].rearrange("(p t) c -> t p c", t=32)}
    xfv = c["xfull"].rearrange("(n p t) d -> n t p d", n=4, t=32)
    ropefv = c["rope_f"].rearrange("(n p t) c -> n t p c", n=4, t=32)
    Bdsc = [Buf("dsc%d" % t) for t in range(32)]

    seqs = cfg.get("seqs", [0, 1, 2])
    for s in seqs:
      with ExitStack() as seq_es:
        sample = (s == 2)
        NKT = 128 if sample else 32
        P.barrier()
        OT = sbuf(seq_es, "OT", [128, 4, S], BF16)
        BYT = [Buf("YT%d" % i) for i in range(8)]
        BOT = [Buf("OT%d" % i) for i in range(8)]
        with ExitStack() as qes:
            QT = sbuf(qes, "QT", [128, 4, S], BF16)
            KT = sbuf(qes, "KT", [128, NKT * 128], BF16)
            VP = sbuf(qes, "VP", [128, NKT, 2, 65], BF16)
            BQT = [Buf("QT%d" % t) for t in range(32)]
            BKT = [Buf("KT%d" % t) for t in range(NKT)]
            BVP = [Buf("VP%d" % t) for t in range(NKT)]
            BVPo = Buf("VPones")
            Wf = sbuf(qes, "Wf", [128, 8, 1280], BF16)
            BWf = Buf("Wf")
            for k in range(8):
                P.dma("pool", Wf[:, k, :], c["w_in"][k * 128:(k + 1) * 128, 0:1280], writes=[BWf], key="wf")
            P.op("pool", lambda e: e.memset(VP[:, :, :, 64:65], 1.0), writes=[BVPo])

            with ExitStack() as es:
                NXB = 2 if sample else 3
                HM = 8 if sample else 20
                gqk = sbuf(es, "gqk", [128, 10, 64], F32)
                coef = sbuf(es, "coef", [128, 32, 4, 2], F32)
                for h in range(10):
                    src = c["qn"] if h < 8 else c["kn"]
                    P.dma("sp", gqk[:, h, :], src.partition_broadcast(128), writes=[Bc], key="c_gqk")
                if sample:
                    P.dma("sp", coef[:].rearrange("p a b c -> p (a b c)"), c["coef"], writes=[Bc], key="c_coef")
                xb = [sbuf(es, "xb%d" % i, [128, D], BF16) for i in range(NXB)]
                xT = [sbuf(es, "xT%d" % i, [128, 8, 128], BF16) for i in range(2)]
                sq = sbuf(es, "sq", [128, HM, 64], F32)
                ssum = sbuf(es, "ssum", [128, HM], F32)
                rstd = sbuf(es, "rstd", [128, HM], F32)
                qn_ = sbuf(es, "qn_", [128, HM, 64], F32)
                t1 = sbuf(es, "t1", [128, HM, 64], F32)
                t2 = sbuf(es, "t2", [128, HM, 64], F32)
                qr = sbuf(es, "qr", [128, max(HM * 64, 1024)], BF16)
                if sample:
                    qraw = [sbuf(es, "qraw%d" % i, [128, 1, 8, 64], F32) for i in range(2)]
                    kraw = [sbuf(es, "kraw%d" % i, [128, 4, 2, 64], F32) for i in range(2)]
                    rpq = [sbuf(es, "rpq%d" % i, [128, 1, 128], F32) for i in range(2)]
                    rpk = [sbuf(es, "rpk%d" % i, [128, 4, 128], F32) for i in range(2)]
                    uacc = [sbuf(es, "uacc%d" % i, [128, 2, 512], F32) for i in range(2)]
                else:
                    qraw = [sbuf(es, "qkraw%d" % i, [128, 2, 10, 64], F32) for i in range(2)]
                    rpq = [sbuf(es, "rpT%d" % i, [128, 2, 128], F32) for i in range(2)]
                Ub = [sbuf(es, "Ub%d" % i, [128, 2, 512], BF16) for i in range(2 if sample else 4)]
                e1 = [sbuf(es, "e1_%d" % i, [128, 384], BF16) for i in range(2)]
                tb = [sbuf(es, "tb%d" % i, [128, 2, 512], BF16) for i in range(2)]
                pT = [psum(es, "pT%d" % i, [128, 8, 128], BF16) for i in range(2)]
                pq = psum(es, "pq", [128, 512], F32)
                pkv = psum(es, "pkv", [128, 512], F32)
                pu = psum(es, "pu", [128, 512], F32)
                pqT = psum(es, "pqT", [128, 8, 128], BF16)
                ps1 = [psum(es, "ps1_%d" % i, [128, 512], F32) for i in range(2)]
                Bn = lambda n: Buf(n)
                Bxb = [Bn("xb") for i in range(NXB)]
                BxT = [Bn("xT0"), Bn("xT1")]
                Bsq, Bss, Brs, Bqn, Bt1, Bt2, Bqr = (Bn(x) for x in "sq ss rs qn t1 t2 qr".split())
                Bqraw = [Bn("qraw0"), Bn("qraw1")]
                Bkraw = [Bn("kraw0"), Bn("kraw1")]
                Brpq = [Bn("rpq0"), Bn("rpq1")]
                Brpk = [Bn("rpk0"), Bn("rpk1")]
                Bua2 = [Bn("uacc0"), Bn("uacc1")]
                BUb = [Bn("Ub%d" % i) for i in range(4)]
                Be1 = [Bn("e1_0"), Bn("e1_1")]
                Btb = [Bn("tb0"), Bn("tb1")]
                BpT = [Buf("pT0", True), Buf("pT1", True)]
                Bpq, Bpkv, Bpu, BpqT = Buf("pq", True), Buf("pkv", True), Buf("pu", True), Buf("pqT", True)
                Bps1 = [Buf("ps1_0", True), Buf("ps1_1", True)]
                e1src = c["e1s"] if sample else c["e1p"]
                e1w = 384 if sample else 256
                cnt = {"t": 0}

                def load_tile(xsrc):
                    n = cnt["t"]
                    cnt["t"] += 1
                    ib, i = n % NXB, n % 2
                    P.dma("pool", xb[ib][:], xsrc, writes=[Bxb[ib]], key="xb%d" % ib)
                    for k in range(8):
                        P.op("pe", lambda e, k=k: e.transpose(out=pT[i][:, k, :], in_=xb[ib][:, k * 128:(k + 1) * 128],
                                                              identity=ident[:]),
                             reads=[Bxb[ib], Bc], writes=[BpT[i]])
                    P.op("act", lambda e: e.activation(out=xT[i][:], in_=pT[i][:], func=AF.Copy),
                         reads=[BpT[i]], writes=[BxT[i]])
                    return i

                def project(i, ps_t, Bps, c0, c1):
                    n = c1 - c0
                    for k in range(8):
                        P.op("pe", lambda e, k=k: e.matmul(ps_t[:, 0:n], lhsT=xT[i][:, k, :], rhs=Wf[:, k, c0:c1],
                                                           start=(k == 0), stop=(k == 7)),
                             reads=[BxT[i], BWf], writes=[Bps])

                def qk_chain(raw, Braw, T, H, gain, rope, Brope):
                    N = T * H
                    v4 = lambda ap: ap[:, 0:N, :].rearrange("p (t h) d -> p t h d", t=T)
                    L = []
                    L.append(lambda: P.op("act", lambda e: e.activation(out=v4(sq), in_=raw, func=AF.Square), reads=[Braw], writes=[Bsq]))
                    L.append(lambda: P.op("dve", lambda e: e.tensor_reduce(out=ssum[:, 0:N], in_=sq[:, 0:N, :], axis=AX.X, op=ALU.add),
                                          reads=[Bsq], writes=[Bss]))
                    L.append(lambda: P.op("dve", lambda e: e.tensor_scalar(out=ssum[:, 0:N], in0=ssum[:, 0:N], scalar1=1.0 / 64.0,
                                                                           scalar2=RMS_EPS, op0=ALU.mult, op1=ALU.add),
                                          reads=[Bss], writes=[Bss]))
                    L.append(lambda: P.op("act", lambda e: e.activation(out=rstd[:, 0:N], in_=ssum[:, 0:N], func=AF.Sqrt),
                                          reads=[Bss], writes=[Brs]))
                    L.append(lambda: P.op("dve", lambda e: e.reciprocal(out=rstd[:, 0:N], in_=rstd[:, 0:N]), reads=[Brs], writes=[Brs]))
                    L.append(lambda: P.op("dve", lambda e: e.tensor_tensor(
                        out=qn_[:, 0:N, :], in0=raw.rearrange("p t h d -> p (t h) d"),
                        in1=rstd[:, 0:N].unsqueeze(2).to_broadcast([128, N, 64]), op=ALU.mult),
                        reads=[Braw, Brs], writes=[Bqn]))
                    L.append(lambda: P.op("dve", lambda e: e.tensor_tensor(out=v4(qn_), in0=v4(qn_),
                                                                           in1=gain.unsqueeze(1).to_broadcast([128, T, H, 64]), op=ALU.mult),
                                          reads=[Bqn, Bc], writes=[Bqn]))
                    L.append(lambda: P.op("dve", lambda e: e.tensor_tensor(out=v4(t1), in0=v4(qn_),
                                                                           in1=rope[:, :, 0:64].unsqueeze(2).to_broadcast([128, T, H, 64]), op=ALU.mult),
                                          reads=[Bqn, Brope], writes=[Bt1]))
                    L.append(lambda: P.op("dve", lambda e: e.tensor_tensor(out=v4(t2)[:, :, :, 0:32], in0=v4(qn_)[:, :, :, 32:64],
                                                                           in1=rope[:, :, 64:96].unsqueeze(2).to_broadcast([128, T, H, 32]), op=ALU.mult),
                                          reads=[Bqn, Brope], writes=[Bt2]))
                    L.append(lambda: P.op("dve", lambda e: e.tensor_tensor(out=v4(t2)[:, :, :, 32:64], in0=v4(qn_)[:, :, :, 0:32],
                                                                           in1=rope[:, :, 96:128].unsqueeze(2).to_broadcast([128, T, H, 32]), op=ALU.mult),
                                          reads=[Bqn, Brope], writes=[Bt2]))
                    return L, v4(t1), v4(t2)

                def stage1(ti, srcs):
                    j = ti % 2
                    P.dma("sp", e1[j][:, 0:e1w], e1src[ti], writes=[Be1[j]], key="e1_%d" % j)
                    if len(srcs) == 1:
                        (u0, B0), = srcs
                        P.op("pe", lambda e: e.matmul(ps1[0][:], lhsT=e1[j][:, 0:128], rhs=u0, start=True, stop=True),
                             reads=[Be1[j], B0], writes=[Bps1[0]])
                        P.op("pe", lambda e: e.matmul(ps1[1][:], lhsT=e1[j][:, 128:256], rhs=u0, start=True, stop=True),
                             reads=[Be1[j], B0], writes=[Bps1[1]])
                    else:
                        (u0, B0), (u1, B1) = srcs
                        P.op("pe", lambda e: e.matmul(ps1[0][:], lhsT=e1[j][:, 0:128], rhs=u0, start=True, stop=False),
                             reads=[Be1[j], B0], writes=[Bps1[0]])
                        P.op("pe", lambda e: e.matmul(ps1[0][:], lhsT=e1[j][:, 128:256], rhs=u1, start=False, stop=True),
                             reads=[Be1[j], B1], writes=[Bps1[0]])
                        P.op("pe", lambda e: e.matmul(ps1[1][:], lhsT=e1[j][:, 128:256], rhs=u0, start=True, stop=False),
                             reads=[Be1[j], B0], writes=[Bps1[1]])
                        P.op("pe", lambda e: e.matmul(ps1[1][:], lhsT=e1[j][:, 256:384], rhs=u1, start=False, stop=True),
                             reads=[Be1[j], B1], writes=[Bps1[1]])
                    P.op("act", lambda e: e.activation(out=tb[j][:, 0, :], in_=ps1[0][:], func=AF.Copy),
                         reads=[Bps1[0]], writes=[Btb[j]])
                    P.op("dve", lambda e: e.tensor_copy(out=tb[j][:, 1, :], in_=ps1[1][:]),
                         reads=[Bps1[1]], writes=[Btb[j]])
                    P.dma("sp", c["dsc"][:, ti], tb[j][:], reads=[Btb[j]], writes=[Bdsc[ti]], key="tbw%d" % j)

                def emit_q_add(t1v, t2v, tt, qro, with_k):
                    L = []
                    qv = qr[:, qro:qro + 512]
                    L.append(lambda: P.op("dve", lambda e: e.tensor_tensor(
                        out=qv.rearrange("p (i g d) -> p g i d", i=4, g=2),
                        in0=t1v[:, tt, 0:8, :].rearrange("p (g i) d -> p g i d", g=2),
                        in1=t2v[:, tt, 0:8, :].rearrange("p (g i) d -> p g i d", g=2), op=ALU.add),
                        reads=[Bt1, Bt2], writes=[Bqr]))
                    if with_k:
                        L.append(lambda: P.op("dve", lambda e: e.tensor_tensor(
                            out=qr[:, qro + 512:qro + 640].rearrange("p (h d) -> p h d", d=64),
                            in0=t1v[:, tt, 8:10, :], in1=t2v[:, tt, 8:10, :], op=ALU.add),
                            reads=[Bt1, Bt2], writes=[Bqr]))
                    return L

                def emit_q_pe(qro, tau, with_k):
                    nT = 5 if with_k else 4
                    for a in range(nT):
                        P.op("pe", lambda e, a=a: e.transpose(out=pqT[:, a, :],
                                                              in_=qr[:, qro + a * 128:qro + (a + 1) * 128],
                                                              identity=ident[:]),
                             reads=[Bqr, Bc], writes=[BpqT])
                    P.op("act", lambda e: e.activation(out=QT[:, :, tau * 128:(tau + 1) * 128], in_=pqT[:, 0:4, :],
                                                       func=AF.Copy), reads=[BpqT], writes=[BQT[tau]])
                    if with_k:
                        P.op("act", lambda e: e.activation(out=KT[:, tau * 128:(tau + 1) * 128], in_=pqT[:, 4, :],
                                                           func=AF.Copy), reads=[BpqT], writes=[BKT[tau]])

                def run_sched(iters):
                    prevB1, prevB2 = [], None
                    for (As, mkB1, B2) in iters + [([], None, None)]:
                        nA = max(len(As), 1)
                        per = (len(prevB1) + nA - 1) // nA
                        for ai in range(nA):
                            if ai < len(As):
                                As[ai]()
                            for _ in range(per):
                                if prevB1:
                                    prevB1.pop(0)()
                        while prevB1:
                            prevB1.pop(0)()
                        if prevB2 is not None:
                            prevB2()
                        prevB1 = mkB1() if mkB1 is not None else []
                        prevB2 = B2

                iters = []
                if not sample:
                    for pr in range(NT // 2):
                        par = pr % 2

                        def Atile(tt, pr=pr, par=par):
                            tau = 2 * pr + tt
                            if tt == 0:
                                P.dma("sp", rpq[par][:], ropev[0][2 * pr:2 * pr + 2].rearrange("t p c -> p t c"),
                                      writes=[Brpq[par]], key="rpq%d" % par)
                            i = load_tile(xv[s, tau])
                            project(i, pq, Bpq, 0, 512)
                            P.op("act", lambda e: e.activation(out=qraw[par][:, tt, 0:8, :],
                                                               in_=pq[:].rearrange("p (h d) -> p h d", d=64), func=AF.Copy),
                                 reads=[Bpq], writes=[Bqraw[par]])
                            project(i, pkv, Bpkv, 512, 768)
                            P.op("act", lambda e: e.activation(out=qraw[par][:, tt, 8:10, :],
                                                               in_=pkv[:, 0:128].rearrange("p (h d) -> p h d", d=64), func=AF.Copy),
                                 reads=[Bpkv], writes=[Bqraw[par]])
                            P.op("act", lambda e: e.activation(out=VP[:, tau, :, 0:64],
                                                               in_=pkv[:, 128:256].rearrange("p (g d) -> p g d", g=2),
                                                               func=AF.Copy), reads=[Bpkv], writes=[BVP[tau]])
                            project(i, pu, Bpu, 768, 1280)
                            j = tau % 4
                            P.op("act", lambda e: e.activation(out=Ub[j][:, 0, :], in_=pu[:], func=AF.Copy),
                                 reads=[Bpu], writes=[BUb[j]])

                        def mkB1(pr=pr, par=par):
                            L, t1v, t2v = qk_chain(qraw[par][:], Bqraw[par], 2, 10, gqk[:, 0:10, :], rpq[par][:], Brpq[par])
                            for tt in range(2):
                                L += emit_q_add(t1v, t2v, tt, tt * 640, True)
                            return L

                        def B2(pr=pr):
                            for tt in range(2):
                                tau = 2 * pr + tt
                                stage1(tau, [(Ub[tau % 4][:, 0, :], BUb[tau % 4])])
                            for tt in range(2):
                                emit_q_pe(tt * 640, 2 * pr + tt, True)
                        iters.append(([lambda a=Atile: a(0), lambda a=Atile: a(1)], mkB1, B2))
                else:
                    for tau in range(NT):
                        par = tau % 2

                        def Aown(tau=tau, par=par):
                            P.dma("sp", rpq[par][:, 0, :], ropev[1][tau], writes=[Brpq[par]], key="rpq%d" % par)
                            P.dma("sp", rpk[par][:], ropefv[:, tau].rearrange("n p c -> p n c"),
                                  writes=[Brpk[par]], key="rpk%d" % par)
                            i = load_tile(xv[s, tau])
                            project(i, pq, Bpq, 0, 512)
                            P.op("act", lambda e: e.activation(out=qraw[par][:, 0, :, :],
                                                               in_=pq[:].rearrange("p (h d) -> p h d", d=64), func=AF.Copy),
                                 reads=[Bpq], writes=[Bqraw[par]])

                        def Afull(n1, tau=tau, par=par):
                            ua, Bu = uacc[par], Bua2[par]
                            kt = tau * 4 + n1
                            i2 = load_tile(xfv[n1, tau])
                            project(i2, pkv, Bpkv, 512, 768)
                            P.op("act", lambda e: e.activation(out=kraw[par][:, n1, :, :],
                                                               in_=pkv[:, 0:128].rearrange("p (h d) -> p h d", d=64), func=AF.Copy),
                                 reads=[Bpkv], writes=[Bkraw[par]])
                            P.op("act", lambda e: e.activation(out=VP[:, kt, :, 0:64],
                                                               in_=pkv[:, 128:256].rearrange("p (g d) -> p g d", g=2),
                                                               func=AF.Copy), reads=[Bpkv], writes=[BVP[kt]])
                            project(i2, pu, Bpu, 768, 1280)
                            for ri in range(2):
                                sc = coef[:, tau, n1, ri:ri + 1]
                                if n1 == 0:
                                    P.op("dve", lambda e: e.tensor_scalar_mul(out=ua[:, ri, :], in0=pu[:], scalar1=sc),
                                         reads=[Bpu, Bc], writes=[Bu])
                                else:
                                    P.op("dve", lambda e: e.scalar_tensor_tensor(
                                        out=ua[:, ri, :], in0=pu[:], scalar=sc, in1=ua[:, ri, :],
                                        op0=ALU.mult, op1=ALU.add), reads=[Bpu, Bu, Bc], writes=[Bu])

                        def mkB1(tau=tau, par=par):
                            L, t1v, t2v = qk_chain(qraw[par][:], Bqraw[par], 1, 8, gqk[:, 0:8, :], rpq[par][:], Brpq[par])
                            L += emit_q_add(t1v, t2v, 0, 0, False)
                            L2, k1v, k2v = qk_chain(kraw[par][:], Bkraw[par], 4, 2, gqk[:, 8:10, :], rpk[par][:], Brpk[par])
                            L2.append(lambda: P.op("dve", lambda e: e.tensor_tensor(
                                out=qr[:, 512:1024].rearrange("p (t h d) -> p t h d", t=4, h=2), in0=k1v, in1=k2v, op=ALU.add),
                                reads=[Bt1, Bt2], writes=[Bqr]))
                            return L + L2

                        def B2(tau=tau, par=par):
                            P.op("act", lambda e: e.activation(out=Ub[par][:], in_=uacc[par][:], func=AF.Copy),
                                 reads=[Bua2[par]], writes=[BUb[par]])
                            stage1(tau, [(Ub[par][:, 0, :], BUb[par]), (Ub[par][:, 1, :], BUb[par])])
                            emit_q_pe(0, tau, False)
                            for a in range(4):
                                P.op("pe", lambda e, a=a: e.transpose(out=pqT[:, 4 + a, :], in_=qr[:, 512 + a * 128:512 + (a + 1) * 128],
                                                                      identity=ident[:]),
                                     reads=[Bqr, Bc], writes=[BpqT])
                            P.op("act", lambda e: e.activation(
                                out=KT[:, tau * 512:(tau + 1) * 512].rearrange("p (a q) -> p a q", a=4),
                                in_=pqT[:, 4:8, :], func=AF.Copy), reads=[BpqT], writes=[BKT[4 * tau + a] for a in range(4)])
                        iters.append(([Aown] + [(lambda n1=n1, f=Afull: f(n1)) for n1 in range(4)], mkB1, B2))
                run_sched(iters)

            P.barrier()
            if "d_QT" in dbg:
                P.dma("sp", c["d_QT"], QT[:], reads=BQT, key="d_QT")
                P.dma("sp", c["d_KT"], KT[:, 0:4096], reads=BKT, key="d_KT")
                P.dma("sp", c["d_VP"], VP[:, 0:32], reads=BVP + [BVPo], key="d_VP")

            if cfg.get("attn", True):
                with ExitStack() as es:
                    NPS = 2
                    pS = [psum(es, "pS%d" % i, [128, 2, 512], F32) for i in range(NPS)]
                    pacc = [psum(es, "pacc%d" % g, [128, 512], F32) for g in range(2)]
                    pbc = psum(es, "pbc", [128, 512], F32)
                    ppk = psum(es, "ppk", [128, 512], F32)
                    NPT = 5
                    PTs = [sbuf(es, "PT%d" % i, [128, 2, 512], BF16) for i in range(NPT)]
                    accs = [sbuf(es, "accs%d" % g, [128, 512], F32) for g in range(2)]
                    rs = sbuf(es, "rs", [128, 2, 512], F32)
                    bcs = sbuf(es, "bcs", [64, 512], F32)
                    stg = [sbuf(es, "stg%d" % g, [64, 512], BF16) for g in range(2)]
                    BpS = [Buf("pS", True) for i in range(NPS)]
                    Bpacc = [Buf("pacc", True) for g in range(2)]
                    Bpbc, Bppk = Buf("pbc", True), Buf("ppk", True)
                    Brs_, Bbcs = Buf("rs"), Buf("bcs")
                    BPTs = [Buf("PT") for i in range(NPT)]
                    Baccs = [Buf("accs") for g in range(2)]
                    Bstg = [Buf("stg") for g in range(2)]
                    epi = []
                    for tau in range(NT):
                        qcols = slice(tau * 128, (tau + 1) * 128)
                        if ws_jobs:
                            ws_jobs.pop(0)()

                        def score(kt):
                            i = kt % NPS
                            for g in range(2):
                                P.op("pe", lambda e, g=g: e.matmul(
                                    pS[i][:, g, :].rearrange("p (i q) -> p i q", i=4), lhsT=KT[64 * g:64 * g + 64, kt * 128:(kt + 1) * 128],
                                    rhs=QT[64 * g:64 * g + 64, :, qcols], start=True, stop=True),
                                    reads=[BKT[kt], BQT[tau]], writes=[BpS[i]])

                        def expo(kt):
                            i = kt % NPS
                            j = kt % NPT
                            P.op("act", lambda e: e.activation(out=PTs[j][:], in_=pS[i][:], func=AF.Exp, scale=0.125),
                                 reads=[BpS[i]], writes=[BPTs[j]])

                        def pv(kt):
                            j = kt % NPT
                            for g in range(2):
                                P.op("pe", lambda e, g=g: e.matmul(
                                    pacc[g][0:65, :], lhsT=VP[:, kt, g, :], rhs=PTs[j][:, g, :],
                                    start=(kt == 0), stop=(kt == NKT - 1)),
                                    reads=[BPTs[j], BVP[kt], BVPo], writes=[Bpacc[g]])

                        score(0)
                        score(1)
                        for kt in range(NKT):
                            expo(kt)
                            if kt + 2 < NKT:
                                score(kt + 2)
                            pv(kt)
                            if epi and kt % 2 == 1:
                                epi.pop(0)()
                        while epi:
                            epi.pop(0)()
                        for g in range(2):
                            P.op("dve", lambda e, g=g: e.tensor_copy(out=accs[g][0:65, :], in_=pacc[g][0:65, :]),
                                 reads=[Bpacc[g]], writes=[Baccs[g]])

                        def mk_epi(tau=tau, qcols=qcols):
                            L = []
                            for g in range(2):
                                L.append(lambda g=g: P.op("dve", lambda e: e.reciprocal(out=rs[64:65, g, :], in_=accs[g][64:65, :]),
                                                          reads=[Baccs[g]], writes=[Brs_]))
                                L.append(lambda g=g: P.op("pe", lambda e: e.matmul(pbc[0:64, :], lhsT=onesf[64:65, 0:64], rhs=rs[64:65, g, :],
                                                                                  start=True, stop=True),
                                                          reads=[Brs_, Bc], writes=[Bpbc]))
                                L.append(lambda g=g: P.op("dve", lambda e: e.tensor_copy(out=bcs[:], in_=pbc[0:64, :]),
                                                          reads=[Bpbc], writes=[Bbcs]))
                                L.append(lambda g=g: P.op("dve", lambda e: e.tensor_tensor(out=stg[g][:], in0=accs[g][0:64, :], in1=bcs[:], op=ALU.mult),
                                                          reads=[Baccs[g], Bbcs], writes=[Bstg[g]]))

                            def pack():
                                P.op("pe", lambda e: e.matmul(ppk[:], lhsT=ident[0:64, :], rhs=stg[0][:], start=True, stop=False),
                                     reads=[Bstg[0], Bc], writes=[Bppk])
                                P.op("pe", lambda e: e.matmul(ppk[:], lhsT=EB[:], rhs=stg[1][:], start=False, stop=True),
                                     reads=[Bstg[1], Bc], writes=[Bppk])
                            L.append(pack)
                            L.append(lambda: P.op("dve", lambda e: e.tensor_copy(out=OT[:, :, qcols], in_=ppk[:].rearrange("p (i q) -> p i q", i=4)),
                                                  reads=[Bppk], writes=[BOT[tau // 4]]))
                            return L
                        epi = mk_epi()
                    while epi:
                        epi.pop(0)()
        while ws_jobs:
            ws_jobs.pop(0)()
        P.barrier()

        YT = sbuf(seq_es, "YT", [128, 4, S], BF16)
        if cfg.get("fft2", True):
            with ExitStack() as es:
                cdt = sbuf(es, "cdt", [128, 256], BF16)
                bdt = sbuf(es, "bdt", [128, 384], BF16)
                P.dma("sp", cdt[:], c["cd"], writes=[Bc], key="c_cd")
                P.dma("sp", bdt[:], c["bd"], writes=[Bc], key="c_bd")
                NT2 = 8
                T2 = [sbuf(es, "T2_%d" % i, [128, 2, 512], BF16) for i in range(NT2)]
                Zb = [sbuf(es, "Zb%d" % i, [128, 4, 2, 512], BF16) for i in range(2)]
                pz = [psum(es, "pz%d" % i, [128, 2, 256], F32) for i in range(4)]
                pyc = [psum(es, "pyc%d" % i, [128, 512], F32) for i in range(2)]
                BT2 = [Buf("T2") for i in range(NT2)]
                BZb = [Buf("Zb") for i in range(2)]
                Bpz = [Buf("pz", True) for i in range(4)]
                Bpyc = [Buf("pyc", True) for i in range(2)]
                dscv = c["dsc"].rearrange("(m l) t c h -> m (l t) c h", l=4)
                pzc = 0
                for M in range(8):
                    zb = M % 2
                    for mm in range(4):
                        m = 4 * M + mm
                        i3 = m % NT2
                        P.dma("sp", T2[i3][:], dscv[m], reads=Bdsc, writes=[BT2[i3]], key="t2_%d" % i3)
                        for gp in range(2):
                            pzi = pzc % 4
                            pzc += 1
                            for gg in range(2):
                                g = 2 * gp + gg
                                P.op("pe", lambda e, g=g, gg=gg, pzi=pzi: e.matmul(
                                    pz[pzi][:, gg, :], lhsT=T2[i3][:, 0, g * 128:(g + 1) * 128], rhs=bdt[:, 128:384],
                                    start=True, stop=False), reads=[BT2[i3], Bc], writes=[Bpz[pzi]])
                                P.op("pe", lambda e, g=g, gg=gg, pzi=pzi: e.matmul(
                                    pz[pzi][:, gg, :], lhsT=T2[i3][:, 1, g * 128:(g + 1) * 128], rhs=bdt[:, 0:256],
                                    start=False, stop=True), reads=[BT2[i3], Bc], writes=[Bpz[pzi]])
                            eng = "act" if gp == 0 else "dve"
                            outap = Zb[zb][:, 2 * gp:2 * gp + 2, :, mm * 128:(mm + 1) * 128]
                            inap = pz[pzi][:].rearrange("p g (c t) -> p g c t", c=2)
                            if eng == "act":
                                P.op("act", lambda e, outap=outap, inap=inap: e.activation(out=outap, in_=inap, func=AF.Copy),
                                     reads=[Bpz[pzi]], writes=[BZb[zb]])
                            else:
                                P.op("dve", lambda e, outap=outap, inap=inap: e.tensor_copy(out=outap, in_=inap),
                                     reads=[Bpz[pzi]], writes=[BZb[zb]])
                    for g in range(4):
                        pi = g % 2
                        P.op("pe", lambda e, g=g, pi=pi: e.matmul(pyc[pi][:], lhsT=cdt[:, 0:128], rhs=Zb[zb][:, g, 0, :],
                                                                 start=True, stop=False), reads=[BZb[zb], Bc], writes=[Bpyc[pi]])
                        P.op("pe", lambda e, g=g, pi=pi: e.matmul(pyc[pi][:], lhsT=cdt[:, 128:256], rhs=Zb[zb][:, g, 1, :],
                                                                 start=False, stop=True), reads=[BZb[zb], Bc], writes=[Bpyc[pi]])
                        ytv = YT[:, g, :].rearrange("q (t k r) -> q t k r", t=32, k=32, r=4)
                        t0 = 16 * (M % 2)
                        outap = ytv[:, t0:t0 + 16, :, M // 2]
                        inap = pyc[pi][:].rearrange("p (a k) -> p a k", a=16)
                        if g % 2 == 0:
                            P.op("act", lambda e, outap=outap, inap=inap: e.activation(out=outap, in_=inap, func=AF.Copy),
                                 reads=[Bpyc[pi]], writes=BYT)
                        else:
                            P.op("dve", lambda e, outap=outap, inap=inap: e.tensor_copy(out=outap, in_=inap),
                                 reads=[Bpyc[pi]], writes=BYT)
            P.barrier()

        if "d_YT" in dbg:
            P.dma("sp", c["d_YT"], YT[:], reads=BYT, key="d_YT")
        if "d_OT" in dbg:
            P.dma("sp", c["d_OT"], OT[:], reads=BOT, key="d_OT")

        if cfg.get("tail", True):
            tail_phase(P, c, cfg, s, glob, sbuf, psum, YT, OT, BYT, BOT, Bws, ident, onesb, Bc, xv, yv)
            P.barrier()


def tail_phase(P, c, cfg, s, glob, sbuf, psum, YT, OT, BYT, BOT, Bws, ident, onesb, Bc, xv, yv):
    nc = P.nc
    ws = c["ws"]
    NW = 4
    with ExitStack() as es:
        X = sbuf(es, "X", [128, 4, D], F32)
        xpre = sbuf(es, "xpre", [128, 4, D], BF16)
        xbt = [xpre[:, i, :] for i in range(4)]
        aT = [sbuf(es, "aT%d" % i, [128, 8, 512], BF16) for i in range(2)]
        big = sbuf(es, "big", [128, 32, 512], BF16)
        sg = sbuf(es, "sg", [128, 2, 512], F32)
        LNt = sbuf(es, "LNt", [128, 2, D], F32)
        LNt1 = sbuf(es, "LNt1", [128, 2, D], F32)
        wr = [sbuf(es, "wr%d" % i, [128, 4096], BF16) for i in range(NW)]
        KmT = sbuf(es, "KmT", [128, 8, 256], BF16)
        Vm = sbuf(es, "Vm", [128, 2, D], BF16)
        st = sbuf(es, "st", [128, 4, 2, 6], F32)
        mv = sbuf(es, "mv", [128, 4, 2], F32)
        rstd = sbuf(es, "rstd_t", [128, 4], F32)
        nmr = sbuf(es, "nmr", [128, 4], F32)
        ptr = [psum(es, "ptr%d" % i, [128, 8, 128], BF16) for i in range(2)]
        NB = 6
        pb = [psum(es, "pb%d" % i, [128, 512], F32) for i in range(NB)]
        BX = [Buf("X%d" % t) for t in range(4)]
        Bxpre = [Buf("xpre") for i in range(4)]
        Bxbt = Bxpre
        BaT = [Buf("aT") for i in range(2)]
        Bbig = [Buf("big%d" % i) for i in range(32)]
        Bsg = [Buf("sg") for i in range(2)]
        BLN = Buf("LNt")
        BLN1 = Buf("LNt1")
        Bwr = [Buf("wr") for i in range(NW)]
        BKm, BVm = Buf("KmT"), Buf("Vm")
        Bst = [Buf("st") for t in range(4)]
        Bmv = [Buf("mv") for t in range(4)]
        Brstd = [Buf("rstd") for t in range(4)]
        Bnmr = [Buf("nmr") for t in range(4)]
        Bptr = [Buf("ptr", True) for i in range(2)]
        Bpb = [Buf("pb", True) for i in range(NB)]
        state = {"w": 0, "b": 0, "x": 0, "p": 0}

        def fetch(u):
            sl = state["w"] % NW
            state["w"] += 1
            P.dma("sp", wr[sl][:], ws[u], reads=[Bws[u]], writes=[Bwr[sl]], key="wr%d" % sl)
            return wr[sl][:].rearrange("p (a b) -> p a b", a=8), Bwr[sl]

        def bank():
            i = state["b"] % NB
            state["b"] += 1
            return pb[i], Bpb[i]

        def transp(src, Bsrc, dst, Bdst, c0, n=128):
            pi = state["p"] % 2
            state["p"] += 1
            for k in range(8):
                P.op("pe", lambda e, k=k: e.transpose(out=ptr[pi][:, k, :], in_=src[:, k * 128:(k + 1) * 128], identity=ident[:]),
                     reads=[Bsrc, Bc], writes=[Bptr[pi]])
            P.op("act", lambda e: e.activation(out=dst[:, :, c0:c0 + n], in_=ptr[pi][:, :, 0:n], func=AF.Copy),
                 reads=[Bptr[pi]], writes=[Bdst])

        def load_ln(gname, bname):
            P.dma("pool", LNt[:, 0, :], c[gname].partition_broadcast(128), writes=[BLN], key="lng")
            P.dma("pool", LNt[:, 1, :], c[bname].partition_broadcast(128), writes=[BLN], key="lnb")

        for mc in range(2):
            i = state["x"] % 2
            state["x"] += 1
            P.dma("pool", xbt[i][:], c["mem"][s, mc * 128:(mc + 1) * 128, :], writes=[Bxbt[i]], key="memb%d" % i)
            transp(xbt[i], Bxbt[i], aT[0], BaT[0], mc * 128)
        for hb in range(2):
            wk, Bwk = fetch(U_MK + hb)
            for fl in range(4):
                p_, Bp_ = bank()
                for k in range(8):
                    P.op("pe", lambda e, k=k: e.matmul(p_[:, 0:256], lhsT=wk[:, k, fl * 128:(fl + 1) * 128], rhs=aT[0][:, k, 0:256],
                                                       start=(k == 0), stop=(k == 7)), reads=[Bwk, BaT[0]], writes=[Bp_])
                P.op("act", lambda e: e.activation(out=KmT[:, 4 * hb + fl, :], in_=p_[:, 0:256], func=AF.Copy),
                     reads=[Bp_], writes=[BKm])
        for hb in range(2):
            wv, Bwv = fetch(U_MV + hb)
            for mc in range(2):
                p_, Bp_ = bank()
                for k in range(8):
                    P.op("pe", lambda e, k=k: e.matmul(p_[:], lhsT=aT[0][:, k, mc * 128:(mc + 1) * 128], rhs=wv[:, k, :],
                                                       start=(k == 0), stop=(k == 7)), reads=[Bwv, BaT[0]], writes=[Bp_])
                P.op("dve", lambda e: e.tensor_copy(out=Vm[:, mc, hb * 512:(hb + 1) * 512], in_=p_[:]),
                     reads=[Bp_], writes=[BVm])

        def layer_norm(t, dstT, BdstT, G, last, tab=None, Btab=None):
            tab = LNt if tab is None else tab
            Btab = BLN if Btab is None else Btab
            xt = X[:, t, :]
            for a in range(2):
                P.op("dve", lambda e, a=a: e.bn_stats(out=st[:, t, a, :], in_=X[:, t, a * 512:(a + 1) * 512]),
                     reads=[BX[t]], writes=[Bst[t]])
            P.op("dve", lambda e: e.bn_aggr(out=mv[:, t, :], in_=st[:, t, :, :].rearrange("p a b -> p (a b)")),
                 reads=[Bst[t]], writes=[Bmv[t]])
            P.op("dve", lambda e: e.tensor_scalar_add(out=rstd[:, t:t + 1], in0=mv[:, t, 1:2], scalar1=LN_EPS),
                 reads=[Bmv[t]], writes=[Brstd[t]])
            P.op("act", lambda e: e.activation(out=rstd[:, t:t + 1], in_=rstd[:, t:t + 1], func=AF.Sqrt),
                 reads=[Brstd[t]], writes=[Brstd[t]])
            P.op("dve", lambda e: e.scalar_tensor_tensor(out=xt, in0=xt, scalar=mv[:, t, 0:1], in1=tab[:, 0, :],
                                                         op0=ALU.subtract, op1=ALU.mult),
                 reads=[BX[t], Bmv[t], Btab], writes=[BX[t]])
            P.op("dve", lambda e: e.reciprocal(out=rstd[:, t:t + 1], in_=rstd[:, t:t + 1]), reads=[Brstd[t]], writes=[Brstd[t]])
            P.op("dve", lambda e: e.scalar_tensor_tensor(out=xt, in0=xt, scalar=rstd[:, t:t + 1], in1=tab[:, 1, :],
                                                         op0=ALU.mult, op1=ALU.add),
                 reads=[BX[t], Brstd[t], Btab], writes=[BX[t]])
            if not last:
                i = state["x"] % 4
                state["x"] += 1
                P.op("act", lambda e: e.activation(out=xbt[i][:], in_=xt, func=AF.Copy), reads=[BX[t]], writes=[Bxbt[i]])
                return lambda: transp(xbt[i], Bxbt[i], dstT, BdstT, t * 128)
            else:
                P.dma("pool", yv[s, 4 * G + t], xt, reads=[BX[t]], key="yst%d" % t)
                return None

        def residual(t, hb, p_, Bp_):
            xs = X[:, t, hb * 512:(hb + 1) * 512]
            P.op("dve", lambda e: e.scalar_tensor_tensor(out=xs, in0=xs, scalar=ALPHA, in1=p_[:], op0=ALU.mult, op1=ALU.add),
                 reads=[BX[t], Bp_], writes=[BX[t]])

        def t0_loads(G):
            for t in range(4):
                P.dma("pool", xpre[:, t, :], xv[s, 4 * G + t], writes=[Bxpre[t]], key="xpre%d" % t)

        def t0_transposes(G):
            a = G % 2
            for t in range(4):
                transp(xpre[:, t, :], Bxpre[t], aT[a], BaT[a], t * 128)

        mg = sbuf(es, "mg", [128, 8, 512], BF16)
        Bmg = [Buf("mg%d" % i) for i in range(8)]
        SA, BSA = aT[0], BaT[0]
        SB, BSB = aT[1], BaT[1]

        def prefetch_T(G):
            for t in range(4):
                transp(xpre[:, t, :], Bxpre[t], SA, BSA, t * 128)

        def T1_half(G, hb, fls=(0, 1, 2, 3)):
            gc = slice(512 * G, 512 * G + 512)
            wga, Bga = fetch(U_GA + hb)
            wgf, Bgf = fetch(U_GF + hb)
            wbr, Bbr = fetch(U_BR + hb)
            for fl in fls:
                fc = 4 * hb + fl
                fs = slice(fl * 128, (fl + 1) * 128)
                pga, Bpga = bank()
                for k in range(8):
                    P.op("pe", lambda e, k=k: e.matmul(pga[:], lhsT=wga[:, k, fs], rhs=SA[:, k, :], start=(k == 0), stop=(k == 7)),
                         reads=[Bga, BSA], writes=[Bpga])
                pgf, Bpgf = bank()
                for k in range(8):
                    P.op("pe", lambda e, k=k: e.matmul(pgf[:], lhsT=wgf[:, k, fs], rhs=SA[:, k, :], start=(k == 0), stop=(k == 7)),
                         reads=[Bgf, BSA], writes=[Bpgf])
                pya, Bpya = bank()
                for a in range(4):
                    P.op("pe", lambda e, a=a: e.matmul(pya[:], lhsT=wbr[:, a, fs], rhs=OT[:, a, gc], start=(a == 0), stop=(a == 3)),
                         reads=[Bbr, BOT[G]], writes=[Bpya])
                pyf, Bpyf = bank()
                for a in range(4):
                    P.op("pe", lambda e, a=a: e.matmul(pyf[:], lhsT=wbr[:, 4 + a, fs], rhs=YT[:, a, gc], start=(a == 0), stop=(a == 3)),
                         reads=[Bbr, BYT[G]], writes=[Bpyf])
                P.op("act", lambda e: e.activation(out=sg[:, 0, :], in_=pga[:], func=AF.Sigmoid), reads=[Bpga], writes=[Bsg[0]])
                P.op("act", lambda e: e.activation(out=sg[:, 1, :], in_=pgf[:], func=AF.Sigmoid), reads=[Bpgf], writes=[Bsg[1]])
                P.op("dve", lambda e: e.tensor_tensor(out=sg[:, 0, :], in0=pya[:], in1=sg[:, 0, :], op=ALU.mult),
                     reads=[Bpya, Bsg[0]], writes=[Bsg[0]])
                P.op("dve", lambda e: e.tensor_tensor(out=sg[:, 1, :], in0=pyf[:], in1=sg[:, 1, :], op=ALU.mult),
                     reads=[Bpyf, Bsg[1]], writes=[Bsg[1]])
                P.op("pool", lambda e: e.tensor_tensor(out=mg[:, fc, :], in0=sg[:, 0, :], in1=sg[:, 1, :], op=ALU.add),
                     reads=[Bsg[0], Bsg[1]], writes=[Bmg[fc]])

        groups = cfg.get("groups", list(range(8)))
        P.dma("pool", LNt1[:, 0, :], c["ln1_g"].partition_broadcast(128), writes=[BLN1], key="ln1g")
        P.dma("pool", LNt1[:, 1, :], c["ln1_b"].partition_broadcast(128), writes=[BLN1], key="ln1b")
        load_ln("ln2_g", "ln2_b")
        t0_loads(groups[0])
        prefetch_T(groups[0])
        T1_half(groups[0], 0)
        T1_half(groups[0], 1)
        if len(groups) > 1:
            t0_loads(groups[1])
            prefetch_T(groups[1])
        ln3_pend = []
        for gi, G in enumerate(groups):
            Gn = groups[gi + 1] if gi + 1 < len(groups) else None
            Gnn = groups[gi + 2] if gi + 2 < len(groups) else None
            if gi == 0:
                for t in range(4):
                    P.dma("pool", X[:, t, :], xv[s, 4 * G + t], writes=[BX[t]], key="xld%d" % t)
            wm = [fetch(U_MIX + hb) for hb in range(2)]
            pend = []
            for t in range(4):
                for _ in range(3 if t == 0 else 1):
                    if ln3_pend:
                        ln3_pend.pop(0)()
                for hb in range(2):
                    p_, Bp_ = bank()
                    for k in range(8):
                        P.op("pe", lambda e, k=k: e.matmul(p_[:], lhsT=mg[:, k, t * 128:(t + 1) * 128], rhs=wm[hb][0][:, k, :],
                                                           start=(k == 0), stop=(k == 7)), reads=[Bmg[k], wm[hb][1]], writes=[Bp_])
                    residual(t, hb, p_, Bp_)
                if len(pend) >= 2:
                    pend.pop(0)()
                pend.append(layer_norm(t, SB, BSB, G, False, LNt1, BLN1))
            if Gn is not None:
                T1_half(Gn, 0)
            while pend:
                pend.pop(0)()
            for hb in range(2):
                wq, Bwq = fetch(U_MQ + hb)
                for fl in range(4):
                    qc = 4 * hb + fl
                    p_, Bp_ = bank()
                    for k in range(8):
                        P.op("pe", lambda e, k=k: e.matmul(p_[:], lhsT=wq[:, k, fl * 128:(fl + 1) * 128], rhs=SB[:, k, :],
                                                           start=(k == 0), stop=(k == 7)), reads=[Bwq, BSB], writes=[Bp_])
                    P.op("act", lambda e, qc=qc: e.activation(out=big[:, qc, :], in_=p_[:], func=AF.Copy),
                         reads=[Bp_], writes=[Bbig[qc]])
            def xa_scores(h):
                for mc in range(2):
                    p_, Bp_ = bank()
                    for dc in range(2):
                        P.op("pe", lambda e, dc=dc: e.matmul(p_[:], lhsT=KmT[:, 2 * h + dc, mc * 128:(mc + 1) * 128], rhs=big[:, 2 * h + dc, :],
                                                             start=(dc == 0), stop=(dc == 1)), reads=[BKm, Bbig[2 * h + dc]], writes=[Bp_])
                    P.op("act", lambda e: e.activation(out=big[:, 8 + 2 * h + mc, :], in_=p_[:], func=AF.Exp, scale=1.0 / 16.0),
                         reads=[Bp_], writes=[Bbig[8 + 2 * h + mc]])

            xa_scores(0)
            for h in range(4):
                if h + 1 < 4:
                    xa_scores(h + 1)
                psm, Bpsm = bank()
                for mc in range(2):
                    P.op("pe", lambda e, mc=mc: e.matmul(psm[:], lhsT=onesb[:], rhs=big[:, 8 + 2 * h + mc, :], start=(mc == 0), stop=(mc == 1)),
                         reads=[Bc, Bbig[8 + 2 * h + mc]], writes=[Bpsm])
                r = h % 2
                P.op("dve", lambda e: e.reciprocal(out=sg[:, r, :], in_=psm[:]), reads=[Bpsm], writes=[Bsg[r]])
                for dc in range(2):
                    p_, Bp_ = bank()
                    for mc in range(2):
                        P.op("pe", lambda e, mc=mc: e.matmul(p_[:], lhsT=Vm[:, mc, (2 * h + dc) * 128:(2 * h + dc + 1) * 128],
                                                             rhs=big[:, 8 + 2 * h + mc, :], start=(mc == 0), stop=(mc == 1)),
                             reads=[BVm, Bbig[8 + 2 * h + mc]], writes=[Bp_])
                    P.op("dve", lambda e: e.tensor_tensor(out=big[:, 16 + 2 * h + dc, :], in0=p_[:], in1=sg[:, r, :], op=ALU.mult),
                         reads=[Bp_, Bsg[r]], writes=[Bbig[16 + 2 * h + dc]])
            wo = [fetch(U_MO + hb) for hb in range(2)]
            pend = []
            for t in range(4):
                for hb in range(2):
                    p_, Bp_ = bank()
                    for k in range(8):
                        P.op("pe", lambda e, k=k: e.matmul(p_[:], lhsT=big[:, 16 + k, t * 128:(t + 1) * 128], rhs=wo[hb][0][:, k, :],
                                                           start=(k == 0), stop=(k == 7)), reads=[Bbig[16 + k], wo[hb][1]], writes=[Bp_])
                    residual(t, hb, p_, Bp_)
                if len(pend) >= 2:
                    pend.pop(0)()
                pend.append(layer_norm(t, SB, BSB, G, False))
            if Gn is not None:
                T1_half(Gn, 1, (0, 1))
            while pend:
                pend.pop(0)()
            load_ln("ln3_g", "ln3_b")
            if Gnn is not None:
                t0_loads(Gnn)
            for j in range(8):
                wu, Bwu = fetch(U_UP + j)
                for fl in range(4):
                    fc = 4 * j + fl
                    p_, Bp_ = bank()
                    for k in range(8):
                        P.op("pe", lambda e, k=k: e.matmul(p_[:], lhsT=wu[:, k, fl * 128:(fl + 1) * 128], rhs=SB[:, k, :],
                                                           start=(k == 0), stop=(k == 7)), reads=[Bwu, BSB], writes=[Bp_])
                    m = fl % 2
                    P.op("act", lambda e: e.activation(out=sg[:, m, :], in_=p_[:], func=AF.Relu), reads=[Bp_], writes=[Bsg[m]])
                    P.op("pool", lambda e, fc=fc: e.tensor_tensor(out=big[:, fc, :], in0=sg[:, m, :], in1=sg[:, m, :], op=ALU.mult),
                         reads=[Bsg[m]], writes=[Bbig[fc]])
            for hb in range(2):
                accs = [bank() for t in range(4)]
                for j in range(4):
                    wd, Bwd = fetch(U_DN + 4 * hb + j)
                    for fcl in range(8):
                        fc = 8 * j + fcl
                        for t in range(4):
                            p_, Bp_ = accs[t]
                            P.op("pe", lambda e, t=t: e.matmul(p_[:], lhsT=big[:, fc, t * 128:(t + 1) * 128], rhs=wd[:, fcl, :],
                                                               start=(fc == 0), stop=(fc == 31)), reads=[Bbig[fc], Bwd], writes=[Bp_])
                for t in range(4):
                    residual(t, hb, accs[t][0], accs[t][1])
            if Gn is not None:
                T1_half(Gn, 1, (2, 3))
            if Gnn is not None:
                prefetch_T(Gnn)
            def ln3_thunk(t, G=G, Gn=Gn):
                layer_norm(t, None, None, G, True)
                if Gn is not None:
                    P.dma("pool", X[:, t, :], xv[s, 4 * Gn + t], writes=[BX[t]], key="xld%d" % t)
                if t == 3:
                    load_ln("ln2_g", "ln2_b")
            if Gn is None:
                for t in range(4):
                    ln3_thunk(t)
            else:
                ln3_pend = [(lambda t=t, f=ln3_thunk: f(t)) for t in range(4)]


_CACHE = {}


def _build_program(cfg_key="full", cfg=None):
    if cfg_key in _CACHE:
        return _CACHE[cfg_key]
    cfg = cfg or {}
    nc1, c1 = make_nc(cfg)
    p1 = Prog(nc1, None)
    build(p1, c1, cfg)
    p1.finish()
    nc2, c2 = make_nc(cfg)
    p2 = Prog(nc2, p1.needed)
    build(p2, c2, cfg)
    p2.finish()
    _CACHE[cfg_key] = (nc2, p2)
    return nc2, p2


def _const_tables():
    return {
        "rope_p": _rope_table(np.arange(S)),
        "rope_f": _rope_table(np.arange(SF)),
        "e1p": _e1_table(S, False),
        "e1s": _e1_table(SF, True),
        "bd": _bd_table(),
        "cd": _cd_table(),
    }


def make_in_maps(inputs, cores=range(8)):
    f = lambda a: np.ascontiguousarray(np.asarray(a, dtype=np.float32))
    xp = f(inputs["x_prompt"])
    xs = f(inputs["x_sample"])
    mp = f(inputs["mem_prompt"])
    ms = f(inputs["mem_sample"])
    tabs = _const_tables()
    shared = {
        "w_in": f(inputs["w_in"][0]), "w_ab": f(inputs["w_attn_branch"][0]), "w_fb": f(inputs["w_fourier_branch"][0]),
        "w_mix": f(inputs["w_mix_out"][0]), "w_mq": f(inputs["w_mem_q"][0]), "w_mk": f(inputs["w_mem_k"][0]),
        "w_mv": f(inputs["w_mem_v"][0]), "w_mo": f(inputs["w_mem_o"][0]), "w_up": f(inputs["w_up"][0]),
        "w_dn": f(inputs["w_down"][0]), "qn": f(inputs["q_norm"]).reshape(1, 64), "kn": f(inputs["k_norm"]).reshape(1, 64),
    }
    for v in VNAMES:
        shared[v] = f(inputs[v]).reshape(1, D)
    shared.update(tabs)
    maps = []
    for cid in cores:
        b, cq = cid // 4, cid % 4
        m = dict(shared)
        m["xseq"] = np.ascontiguousarray(np.stack([xp[2 * cid], xp[2 * cid + 1], xs[b, cq::4]], axis=0))
        m["xfull"] = xs[b]
        m["mem"] = np.ascontiguousarray(np.stack([mp[2 * cid], mp[2 * cid + 1], ms[b]], axis=0))
        m["rope_s"] = _rope_table(cq + 4 * np.arange(S))
        m["coef"] = _coef_table(cq).reshape(128, 256)
        maps.append(m)
    return maps


def kernel(**inputs):
    nc, _ = _build_program("full", {})
    maps = make_in_maps(inputs)
    res = run_bass_kernel_spmd(nc, maps, core_ids=list(range(8)))
    yp = np.empty((16, S, D), np.float32)
    ys = np.empty((2, SF, D), np.float32)
    for cid in range(8):
        y = np.asarray(res.results[cid]["y"], dtype=np.float32)
        b, cq = cid // 4, cid % 4
        yp[2 * cid] = y[0]
        yp[2 * cid + 1] = y[1]
        ys[b, cq::4] = y[2]
    return yp, ys
```
